# Optimizing a Trainium2 kernel written in Bass

```python
import math
import jax, jax.numpy as jnp
from jax import lax
import numpy as np

D_MODEL = 1024
BATCH = 16
SEQ = 256
DEPTH = 1
DEC_BATCH = 2
DEC_SEQ = 2048
PAST_LEN = 256

GRID_W = 64
F_WIDTH = D_MODEL
F_GROUPS = 4
F_GROUP_W = F_WIDTH // F_GROUPS
D_INNER = 2 * D_MODEL
SSD_HEAD_DIM = 64
SSD_HEADS = D_INNER // SSD_HEAD_DIM
SSD_GROUPS = 8
D_STATE = 128
CONV_K = 3
CONV_DIM = D_INNER + 2 * SSD_GROUPS * D_STATE
CHUNK = 128
EPS = 1e-6
IN_WIDTH = 2 * F_WIDTH + D_INNER + CONV_DIM + 2 * SSD_HEADS + 2 * D_MODEL

kernel_name = "hybrid_fnet_ssd_prefix_diffusion_step"


def rmsnorm(x, w):
    xf = x.astype(jnp.float32)
    y = xf * lax.rsqrt(jnp.mean(xf * xf, axis=-1, keepdims=True) + EPS)
    return y.astype(x.dtype) * w


def grouped_rmsnorm(x, w, groups):
    shp = x.shape
    xf = x.astype(jnp.float32).reshape(shp[:-1] + (groups, shp[-1] // groups))
    y = xf * lax.rsqrt(jnp.mean(xf * xf, axis=-1, keepdims=True) + EPS)
    return y.reshape(shp).astype(x.dtype) * w


def centred_dwconv(u, w, b):
    out = lax.conv_general_dilated(
        u, w[:, None, :].astype(u.dtype), window_strides=(1,),
        padding=[(CONV_K // 2, CONV_K // 2)],
        dimension_numbers=("NWC", "WIO", "NWC"),
        feature_group_count=u.shape[-1])
    return out + b


def fourier_mix(u, on_grid):
    bt, L, _ = u.shape
    uf = u.astype(jnp.float32)
    if on_grid:
        rows = L // GRID_W
        uf = uf.reshape(bt, rows, GRID_W, F_GROUPS, F_GROUP_W)
        axes = (1, 2, 4)
    else:
        uf = uf.reshape(bt, L, F_GROUPS, F_GROUP_W)
        axes = (1, 3)
    out = jnp.fft.fftn(uf, axes=axes, norm="ortho").real
    return out.reshape(bt, L, F_WIDTH).astype(u.dtype)


def ssd_scan(x, dt, a, b_mat, c_mat, init_state):
    bt, L, H, P = x.shape
    G, N = b_mat.shape[-2], b_mat.shape[-1]
    R = H // G
    nc = L // CHUNK
    xf = x.astype(jnp.float32)
    dtf = dt.astype(jnp.float32)
    xdt = (xf * dtf[..., None]).reshape(bt, nc, CHUNK, G, R, P)
    bc = b_mat.astype(jnp.float32).reshape(bt, nc, CHUNK, G, N)
    cc = c_mat.astype(jnp.float32).reshape(bt, nc, CHUNK, G, N)
    da = (dtf * a.astype(jnp.float32)).reshape(bt, nc, CHUNK, G, R).transpose(0, 3, 4, 1, 2)
    acs = jnp.cumsum(da, axis=-1)
    seg = acs[..., :, None] - acs[..., None, :]
    tri = jnp.tril(jnp.ones((CHUNK, CHUNK), dtype=bool))
    lmat = jnp.exp(jnp.where(tri, seg, -jnp.inf))
    y_diag = jnp.einsum("bclgn,bcsgn,bgrcls,bcsgrp->bclgrp", cc, bc, lmat, xdt)
    decay_states = jnp.exp(acs[..., -1:] - acs)
    chunk_states = jnp.einsum("bcsgn,bgrcs,bcsgrp->bcgrpn", bc, decay_states, xdt)
    chunk_decay = jnp.exp(acs[..., -1])

    def step(s, inp):
        cs, cd = inp
        return s * cd[..., None, None] + cs, s

    s0 = init_state.astype(jnp.float32).reshape(bt, G, R, P, N)
    final, states_in = lax.scan(step, s0, (jnp.moveaxis(chunk_states, 1, 0), jnp.moveaxis(chunk_decay, 3, 0)))
    states_in = jnp.moveaxis(states_in, 0, 1)
    y_off = jnp.einsum("bclgn,bcgrpn,bgrcl->bclgrp", cc, states_in, jnp.exp(acs))
    y = (y_diag + y_off).reshape(bt, L, H, P).astype(x.dtype)
    return y, final.reshape(bt, H, P, N)


def bidir_ssd(xs, dt_f, dt_b, a_f, a_b, b_mat, c_mat, init_f, init_b):
    flip = lambda t: jnp.flip(t, axis=1)
    y_f, s_f = ssd_scan(xs, dt_f, a_f, b_mat, c_mat, init_f)
    y_b, s_b = ssd_scan(flip(xs), flip(dt_b), a_b, flip(b_mat), flip(c_mat), init_b)
    return y_f + flip(y_b), s_f, s_b


def trunk_layer(x, cond, init_f, init_b, on_grid, ada_w, ada_b, pre_w, post_w, w_in, conv_w, conv_b,
                dtb_f, dtb_b, alog_f, alog_b, d_skip, ssd_norm_w, fnet_w, w_branch_f, w_branch_s, w_out):
    bt, L, _ = x.shape
    ada = jax.nn.silu(cond) @ ada_w + ada_b
    shift, scale, gate = jnp.split(ada[:, None, :], 3, axis=-1)
    h = rmsnorm(x, pre_w) * (1.0 + scale) + shift
    p = h @ w_in
    cuts = np.cumsum([F_WIDTH, F_WIDTH, D_INNER, CONV_DIM, SSD_HEADS, SSD_HEADS, D_MODEL])
    u_f, g_f, z, xbc, dt_f, dt_b, mg_f, mg_s = jnp.split(p, cuts, axis=-1)

    f_out = ((fourier_mix(u_f, on_grid) @ fnet_w) * jax.nn.silu(g_f)) @ w_branch_f

    xbc = jax.nn.silu(centred_dwconv(xbc, conv_w, conv_b))
    xs, b_mat, c_mat = jnp.split(xbc, [D_INNER, D_INNER + SSD_GROUPS * D_STATE], axis=-1)
    xs = xs.reshape(bt, L, SSD_HEADS, SSD_HEAD_DIM)
    b_mat = b_mat.reshape(bt, L, SSD_GROUPS, D_STATE)
    c_mat = c_mat.reshape(bt, L, SSD_GROUPS, D_STATE)
    dtf = jax.nn.softplus(dt_f.astype(jnp.float32) + dtb_f)
    dtb = jax.nn.softplus(dt_b.astype(jnp.float32) + dtb_b)
    a_f = -jnp.exp(alog_f.astype(jnp.float32))
    a_b = -jnp.exp(alog_b.astype(jnp.float32))
    y, s_f, s_b = bidir_ssd(xs, dtf, dtb, a_f, a_b, b_mat, c_mat, init_f, init_b)
    y = (y + xs * d_skip[:, None]).reshape(bt, L, D_INNER)
    y = grouped_rmsnorm(y * jax.nn.silu(z), ssd_norm_w, SSD_GROUPS)
    s_out = y @ w_branch_s

    merged = jax.nn.sigmoid(mg_f) * f_out + jax.nn.sigmoid(mg_s) * s_out
    o = merged @ w_out
    return x + gate * rmsnorm(o, post_w), s_f, s_b


def setup_inputs(seed: int = 0) -> dict:
    key = jax.random.key(seed)
    ks = jax.random.split(key, 26)
    nrm = lambda k, shape, s: jax.random.normal(k, shape, jnp.float32) * s

    def dt_bias(k):
        u = jax.random.uniform(k, (DEPTH, SSD_HEADS), jnp.float32)
        dt0 = jnp.exp(u * (math.log(0.1) - math.log(0.001)) + math.log(0.001))
        return dt0 + jnp.log(-jnp.expm1(-dt0))

    st_shape = (DEC_BATCH, DEPTH, SSD_HEADS, SSD_HEAD_DIM, D_STATE)
    return {
        "x_prompt": nrm(ks[0], (BATCH, SEQ, D_MODEL), 1.0),
        "x_sample": nrm(ks[1], (DEC_BATCH, DEC_SEQ, D_MODEL), 1.0),
        "c": nrm(ks[2], (DEC_BATCH, D_MODEL), 1.0),
        "state_ssd_fwd": nrm(ks[3], st_shape, 0.05),
        "state_ssd_bwd": nrm(ks[4], st_shape, 0.05),
        "c_ctx": nrm(ks[5], (D_MODEL,), 1.0),
        "ada_w": nrm(ks[6], (DEPTH, D_MODEL, 3 * D_MODEL), 0.5 * D_MODEL ** -0.5),
        "ada_b": nrm(ks[7], (DEPTH, 3 * D_MODEL), 0.02),
        "pre_norm_w": 1.0 + nrm(ks[8], (DEPTH, D_MODEL), 0.1),
        "post_norm_w": 1.0 + nrm(ks[9], (DEPTH, D_MODEL), 0.1),
        "w_in": nrm(ks[10], (DEPTH, D_MODEL, IN_WIDTH), D_MODEL ** -0.5),
        "conv_w": nrm(ks[11], (DEPTH, CONV_K, CONV_DIM), CONV_K ** -0.5),
        "conv_b": nrm(ks[12], (DEPTH, CONV_DIM), 0.02),
        "dt_bias_fwd": dt_bias(ks[13]),
        "dt_bias_bwd": dt_bias(ks[14]),
        "a_log_fwd": jnp.log(jax.random.uniform(ks[15], (DEPTH, SSD_HEADS), jnp.float32, 1.0, 16.0)),
        "a_log_bwd": jnp.log(jax.random.uniform(ks[16], (DEPTH, SSD_HEADS), jnp.float32, 1.0, 16.0)),
        "d_skip": 1.0 + nrm(ks[17], (DEPTH, SSD_HEADS), 0.1),
        "ssd_norm_w": 1.0 + nrm(ks[18], (DEPTH, D_INNER), 0.1),
        "fnet_w": nrm(ks[19], (DEPTH, F_WIDTH, F_WIDTH), F_WIDTH ** -0.5),
        "w_branch_f": nrm(ks[20], (DEPTH, F_WIDTH, D_MODEL), F_WIDTH ** -0.5),
        "w_branch_s": nrm(ks[21], (DEPTH, D_INNER, D_MODEL), D_INNER ** -0.5),
        "w_out": nrm(ks[22], (DEPTH, D_MODEL, D_MODEL), D_MODEL ** -0.5),
    }


def reference(x_prompt, x_sample, c, state_ssd_fwd, state_ssd_bwd, c_ctx, ada_w, ada_b, pre_norm_w,
              post_norm_w, w_in, conv_w, conv_b, dt_bias_fwd, dt_bias_bwd, a_log_fwd, a_log_bwd, d_skip,
              ssd_norm_w, fnet_w, w_branch_f, w_branch_s, w_out):
    ctx_cond = c_ctx[None, :]
    zero_state = jnp.zeros((x_prompt.shape[0], SSD_HEADS, SSD_HEAD_DIM, D_STATE), jnp.float32)
    xp = x_prompt
    xl = x_sample
    new_f = []
    new_b = []
    for i in range(DEPTH):
        lw = (ada_w[i], ada_b[i], pre_norm_w[i], post_norm_w[i], w_in[i], conv_w[i], conv_b[i],
              dt_bias_fwd[i], dt_bias_bwd[i], a_log_fwd[i], a_log_bwd[i], d_skip[i], ssd_norm_w[i],
              fnet_w[i], w_branch_f[i], w_branch_s[i], w_out[i])
        xp, sf, sb = trunk_layer(xp, ctx_cond, zero_state, zero_state, False, *lw)
        new_f.append(sf)
        new_b.append(sb)
        xl, _, _ = trunk_layer(xl, c, state_ssd_fwd[:, i], state_ssd_bwd[:, i], True, *lw)
    new_state_ssd_fwd = jnp.stack(new_f, axis=1)
    new_state_ssd_bwd = jnp.stack(new_b, axis=1)
    return (xp, xl, new_state_ssd_fwd, new_state_ssd_bwd)
```

```python
import math
import numpy as np
import ml_dtypes
import concourse.bass as bass
import concourse.mybir as mybir
from concourse.bass_utils import run_bass_kernel_spmd

F32 = mybir.dt.float32
BF16 = mybir.dt.bfloat16
AF = mybir.ActivationFunctionType
ALU = mybir.AluOpType

SAME_ENGINE_SYNC = True
EPS = 1e-6
D = 1024
NX, NP_, NM = 2048, 512, 512
COL_X, COL_P, COL_ME = 0, 2048, 2560
NCOL = 2048 + 512 + 514
W_UF, W_GF, W_Z, W_XS, W_B, W_C, W_DTF, W_DTB, W_MGF, W_MGS = 0, 1024, 2048, 4096, 6144, 7168, 8192, 8224, 8256, 9280


class StopBuild(Exception):
    pass


class Tile:
    __slots__ = ("name", "t", "last_write", "reads", "dsem")

    def __init__(self, name, t):
        self.name = name
        self.t = t
        self.last_write = None
        self.reads = {}
        self.dsem = None

    def __getitem__(self, k):
        return self.t[k]


class Sched:
    ENG = ("pe", "act", "dve", "pool", "sp")

    def __init__(self, nc, arena_words):
        self.nc = nc
        self.streams = {e: [] for e in self.ENG}
        self.count = {}
        self.seen = {e: {} for e in self.ENG}
        self.sems = {}
        self.ndsem = 0
        for e in self.ENG:
            self._mksem(e)
        self.arena = nc.alloc_sbuf_tensor("arena", [128, arena_words], F32)
        self.arena_words = arena_words
        self.top = 0
        self.psum = [Tile("ps%d" % i, nc.alloc_psum_tensor("ps%d" % i, [128, 512], F32)) for i in range(8)]
        self.psi = 0
        self.reserved = []
        self.nops = 0

    def _mksem(self, key):
        self.sems[key] = self.nc.alloc_semaphore(name="s_" + key)
        self.count[key] = 0

    def alloc(self, name, free_shape, dt):
        n = int(np.prod(free_shape))
        words = (n * (2 if dt == BF16 else 4) + 3) // 4
        words = (words + 7) // 8 * 8
        assert self.top + words <= self.arena_words, (name, self.top, words)
        ap = self.arena[:, self.top:self.top + words]
        self.top += words
        if dt == BF16:
            ap = ap.bitcast(BF16)[:, 0:n]
        else:
            ap = ap[:, 0:n]
        if len(free_shape) == 2:
            ap = ap.rearrange("p (a b) -> p a b", a=free_shape[0])
        elif len(free_shape) == 3:
            ap = ap.rearrange("p (a b c) -> p a b c", a=free_shape[0], b=free_shape[1])
        return Tile(name, ap)

    def bank(self):
        while True:
            t = self.psum[self.psi]
            self.psi = (self.psi + 1) % 8
            if t not in self.reserved:
                return t

    def reserve_bank(self):
        t = self.bank()
        self.reserved.append(t)
        return t

    def unreserve(self, t):
        self.reserved.remove(t)

    def _deps(self, reads, writes):
        deps = []
        for t in reads:
            if t.last_write is not None:
                deps.append(t.last_write)
        for t in writes:
            if t.last_write is not None:
                deps.append(t.last_write)
            deps.extend(t.reads.items())
        return deps

    def _emit_waits(self, eng, deps, skip_own=False):
        need = {}
        for key, val in deps:
            if key == eng and (skip_own or not SAME_ENGINE_SYNC):
                continue
            if self.seen[eng].get(key, 0) >= val:
                continue
            if need.get(key, 0) < val:
                need[key] = val
        for key, val in need.items():
            self.seen[eng][key] = val
            sem = self.sems[key]
            self.streams[eng].append(lambda e, sem=sem, val=val: e.wait_ge(sem, val))

    def _post(self, tk, reads, writes):
        for t in reads:
            if t.reads.get(tk[0], 0) < tk[1]:
                t.reads[tk[0]] = tk[1]
        for t in writes:
            t.last_write = tk
            t.reads = {}

    def op(self, eng, fn, reads=(), writes=()):
        self._emit_waits(eng, self._deps(reads, writes), skip_own=(eng == "pe"))
        self.count[eng] += 1
        tk = (eng, self.count[eng])
        sem = self.sems[eng]
        self.streams[eng].append(lambda e, fn=fn, sem=sem: fn(e).then_inc(sem, 1))
        self._post(tk, reads, writes)
        self.nops += 1
        return tk

    def dma(self, q, out_ap, in_ap, sem_tile, reads=(), writes=()):
        self._emit_waits(q, self._deps(reads, writes))
        st = sem_tile
        if st.dsem is None:
            st.dsem = "d%d" % self.ndsem
            self.ndsem += 1
            self._mksem(st.dsem)
        key = st.dsem
        self.count[key] += 16
        tk = (key, self.count[key])
        sem = self.sems[key]
        self.streams[q].append(lambda e, o=out_ap, i=in_ap, sem=sem: e.dma_start(out=o, in_=i).then_inc(sem, 16))
        self._post(tk, reads, writes)
        return tk

    def barrier(self):
        tks = [(k, v) for k, v in self.count.items() if v > 0]
        for e in self.ENG:
            self._emit_waits(e, tks)

    def release(self, mark):
        self.barrier()
        self.top = mark

    def mm(self, out, lhsT, rhs, start, stop, reads, writes):
        return self.op("pe", lambda e: e.matmul(out, lhsT=lhsT, rhs=rhs, start=start, stop=stop), reads, writes)

    def tr(self, out, in_, ident, reads, writes):
        return self.op("pe", lambda e: e.transpose(out=out, in_=in_, identity=ident), reads, writes)

    def act(self, out, in_, func, reads, writes, bias=None, scale=None, accum=None):
        kw = {}
        if bias is not None:
            kw["bias"] = bias
        if scale is not None:
            kw["scale"] = scale
        if accum is not None:
            kw["accum_out"] = accum
        return self.op("act", lambda e: e.activation(out=out, in_=in_, func=func, **kw), reads, writes)

    def tt(self, eng, out, in0, in1, op, reads, writes):
        return self.op(eng, lambda e: e.tensor_tensor(out=out, in0=in0, in1=in1, op=op), reads, writes)

    def ts(self, eng, out, in0, s1, op0, reads, writes, s2=None, op1=None):
        if op1 is None:
            return self.op(eng, lambda e: e.tensor_scalar(out=out, in0=in0, scalar1=s1, scalar2=None, op0=op0),
                           reads, writes)
        return self.op(eng, lambda e: e.tensor_scalar(out=out, in0=in0, scalar1=s1, scalar2=s2, op0=op0, op1=op1),
                       reads, writes)

    def stt(self, eng, out, in0, scalar, in1, op0, op1, reads, writes):
        return self.op(eng, lambda e: e.scalar_tensor_tensor(out=out, in0=in0, scalar=scalar, in1=in1,
                                                              op0=op0, op1=op1), reads, writes)

    def cp(self, eng, out, in_, reads, writes):
        if eng == "act":
            return self.op("act", lambda e: e.copy(out=out, in_=in_), reads, writes)
        return self.op(eng, lambda e: e.tensor_copy(out=out, in_=in_), reads, writes)

    def memset(self, eng, ap, val, writes):
        return self.op(eng, lambda e: e.memset(ap, val), (), writes)

    def emit(self):
        nc = self.nc
        with nc.Block() as block:
            @block.tensor
            def _(e):
                for f in self.streams["pe"]:
                    f(e)

            @block.scalar
            def _(e):
                for f in self.streams["act"]:
                    f(e)

            @block.vector
            def _(e):
                for f in self.streams["dve"]:
                    f(e)

            @block.gpsimd
            def _(e):
                for f in self.streams["pool"]:
                    f(e)

            @block.sync
            def _(e):
                for f in self.streams["sp"]:
                    f(e)


ARENA_WORDS = 53000


def run_pipeline(units):
    if not units:
        return
    nst = max(len(u) for u in units)
    for step in range(len(units) + nst - 1):
        for si in range(nst):
            k = step - si
            if 0 <= k < len(units) and si < len(units[k]):
                units[k][si]()

NROW_SM = 160
PROMPT_TOK = [(COL_P + i * 128) for i in range(4)]


def build(dbg=False, stop=None):
    import os
    stop = stop or os.environ.get('KSTOP')
    nc = bass.Bass("TRN2", target_bir_lowering=False)
    S = Sched(nc, ARENA_WORDS)

    def din(name, shape, dt=F32):
        return nc.dram_tensor(name, list(shape), dt, kind="ExternalInput").ap()

    def dout(name, shape):
        return nc.dram_tensor(name, list(shape), F32, kind="ExternalOutput").ap()

    xX = din("xX", [2048, D]); xP = din("xP", [512, D]); xM = din("xM", [512, D]); xH = din("xH", [2, D])
    fl = din("fl", [1, 8]); cT = din("cT", [128, 16]); stT = din("stT", [2, 128, 2048])
    tcs = din("tcs", [2, 2048, 512], BF16); t256 = din("t256", [3, 256, 256], BF16)
    cwd = din("cw", [128, 96]); cbd = din("cb", [128, 32]); rowsm = din("rowsm", [1, NROW_SM])
    nwTd = din("nwT", [128, 16])
    rowbig = din("rowbig", [1, 4096])
    ada_w = din("ada_w", [D, 3072]); ada_b = din("ada_b", [1, 3072]); w_in = din("w_in", [D, 10304])
    fnet_w = din("fnet_w", [D, D]); wbf = din("wbf", [D, D]); wbs = din("wbs", [2048, D]); w_out = din("w_out", [D, D])
    yP = dout("yP", [512, D]); yM = dout("yM", [512, D])
    sF = dout("sF", [2, 32, 64, 128]); sB = dout("sB", [2, 32, 64, 128])
    def dump(name, tile, shape, dt=F32):
        if not dbg:
            return
        d = nc.dram_tensor("dbg_" + name, [128] + list(shape), dt, kind="ExternalOutput").ap()
        S.dma("sp", d, tile[:], tile, reads=[tile])

    def wview(w, c0, n):
        return w[:, c0:c0 + n].rearrange("(kc p) n -> p kc n", p=128)

    identb = S.alloc("identb", [128], BF16); identf = S.alloc("identf", [128], F32)
    LEf = S.alloc("LEf", [128], F32); onesf = S.alloc("onesf", [128], F32)
    LEb = S.alloc("LEb", [128], BF16); GEb = S.alloc("GEb", [128], BF16)
    GTb = S.alloc("GTb", [128], BF16); LTb = S.alloc("LTb", [128], BF16)
    epsb = S.alloc("epsb", [1], F32); oneb = S.alloc("oneb", [1], F32)
    sel2 = S.alloc("sel2", [256], F32)
    cw = S.alloc("cw", [96], F32); cb = S.alloc("cb", [32], F32)
    rows = S.alloc("rows", [NROW_SM], F32); flg = S.alloc("flg", [8], F32)
    Arow = S.alloc("Arow", [64], F32)
    g2 = S.alloc("g2", [2, D], F32)
    hTPM = S.alloc("hTPM", [8, 1026], BF16)
    fmT = S.alloc("fmT", [8, 1024], BF16)
    dtO = S.alloc("dtO", [8, 64], F32); scO = S.alloc("scO", [8, 64], F32)
    dtdec = S.alloc("dtdec", [8, 64], F32); cdA = S.alloc("cdA", [8, 64], F32)
    wini = S.alloc("wini", [64], F32)
    dabf = S.alloc("dabf", [8, 64], BF16); ebias = S.alloc("ebias", [8, 64], F32)
    NEGf = S.alloc("NEGf", [128], BF16); NEGb = S.alloc("NEGb", [128], BF16); NLTb = S.alloc("NLTb", [128], BF16)
    SmF = S.alloc("SmF", [8, 256], BF16); SmB = S.alloc("SmB", [8, 256], BF16)
    mark_pre_hTX = S.top
    hTX = S.alloc("hTX", [8, 2048], BF16)
    dtdecX = S.alloc("dtdecX", [16, 64], F32)
    mark_base = S.top

    tmpf = S.alloc("tmpf", [128], F32)

    def tri(dst_bf, pattern_step, cm, cmp, also_f32=None):
        S.memset("pool", tmpf[:], 1.0, [tmpf])
        S.op("pool", lambda e: e.affine_select(out=tmpf[:], in_=tmpf[:], pattern=[[pattern_step, 128]],
                                               compare_op=cmp, fill=0.0, base=0, channel_multiplier=cm),
             [tmpf], [tmpf])
        if dst_bf is not None:
            S.cp("pool", dst_bf[:], tmpf[:], [tmpf], [dst_bf])
        if also_f32 is not None:
            S.cp("pool", also_f32[:], tmpf[:], [tmpf], [also_f32])

    tri(identb, -1, 1, ALU.is_equal, identf)
    tri(LEb, 1, -1, ALU.is_ge, LEf)
    tri(GEb, -1, 1, ALU.is_ge)
    tri(GTb, -1, 1, ALU.is_gt)
    tri(LTb, 1, -1, ALU.is_gt)
    S.ts("pool", NEGf[:], GTb[:], -30000.0, ALU.mult, [GTb], [NEGf])
    S.ts("pool", NEGb[:], LTb[:], -30000.0, ALU.mult, [LTb], [NEGb])
    S.ts("pool", NLTb[:], LTb[:], -1.0, ALU.mult, [LTb], [NLTb])
    S.memset("pool", onesf[:], 1.0, [onesf])
    S.memset("pool", epsb[:], EPS, [epsb])
    S.memset("pool", oneb[:], 1.0, [oneb])
    S.memset("pool", sel2[:], 1.0, [sel2])
    S.op("pool", lambda e: e.affine_select(out=sel2[:], in_=sel2[:], pattern=[[1, 256]], compare_op=ALU.is_ge,
                                           fill=0.0, base=0, channel_multiplier=-128), [sel2], [sel2])
    S.op("pool", lambda e: e.affine_select(out=sel2[:], in_=sel2[:], pattern=[[-1, 256]], compare_op=ALU.is_ge,
                                           fill=0.0, base=127, channel_multiplier=128), [sel2], [sel2])
    S.dma("sp", cw[:], cwd, cw, writes=[cw])
    S.dma("sp", cb[:], cbd, cb, writes=[cb])
    S.dma("sp", rows[:], rowsm.partition_broadcast(128), rows, writes=[rows])
    S.dma("sp", flg[:], fl.partition_broadcast(128), flg, writes=[flg])
    S.act(Arow[:], rows[:, 64:128], AF.Exp, [rows], [Arow])
    S.ts("dve", Arow[:], Arow[:], -1.0, ALU.mult, [Arow], [Arow])

    if stop == 'const':
        dump('rows', rows, [NROW_SM]); S.barrier(); S.emit(); return nc
    g1 = S.alloc("g1", [2, D], F32); shf = S.alloc("shf", [2, D], F32)
    mark1 = S.top
    scT = S.alloc("scT", [8, 2], F32)
    ada2 = S.alloc("ada2", [3072], F32)
    prew = S.alloc("prew", [D], F32); postw = S.alloc("postw", [D], F32)
    wst = [S.alloc("wst%d" % i, [8, 512], F32) for i in range(3)]
    S.dma("sp", scT[:].rearrange("p a b -> p (a b)"), cT, scT, writes=[scT])
    S.dma("sp", ada2[0:2, :], ada_b.partition_broadcast(2), ada2, writes=[ada2])
    S.dma("sp", prew[:], rowbig[:, 2048:3072].partition_broadcast(128), prew, writes=[prew])
    S.dma("sp", postw[:], rowbig[:, 3072:4096].partition_broadcast(128), postw, writes=[postw])
    S.act(scT[:], scT[:], AF.Silu, [scT], [scT])
    for cbk in range(6):
        wt = wst[cbk % 3]
        S.dma(("sp", "act")[cbk % 2], wt[:], wview(ada_w, cbk * 512, 512), wt, writes=[wt])
        pb = S.bank()
        for kc in range(8):
            S.mm(pb[0:2, :], scT[:, kc, :], wt[:, kc, :], kc == 0, kc == 7, [scT, wt], [pb])
        S.tt("dve", ada2[0:2, cbk * 512:(cbk + 1) * 512], pb[0:2, :], ada2[0:2, cbk * 512:(cbk + 1) * 512],
             ALU.add, [pb, ada2], [ada2])
    if stop == 'p0a':
        dump('ada2', ada2, [3072]); S.barrier(); S.emit(); return nc
    for c in range(2):
        for cbk in range(6):
            pb = S.bank()
            S.mm(pb[:, :], sel2[0:2, c * 128:(c + 1) * 128], ada2[0:2, cbk * 512:(cbk + 1) * 512], True, True,
                 [sel2, ada2], [pb])
            sl = slice((cbk % 2) * 512, (cbk % 2) * 512 + 512)
            if cbk < 2:
                S.cp("act", shf[:, c, sl], pb[:, :], [pb], [shf])
            elif cbk < 4:
                S.stt("dve", g1[:, c, sl], pb[:, :], 1.0, prew[:, sl], ALU.add, ALU.mult, [pb, prew], [g1])
            else:
                S.tt("dve", g2[:, c, sl], pb[:, :], postw[:, sl], ALU.mult, [pb, postw], [g2])
    S.release(mark1)

    if stop == 'p0':
        dump('g2', g2, [2, D]); S.barrier(); S.emit(); return nc
    xbuf = [S.alloc("xbuf%d" % i, [D], F32) for i in range(5)]
    xhalo = S.alloc("xhalo", [D], F32)
    tmpb = [S.alloc("tmpb%d" % i, [D], F32) for i in range(2)]
    hbb = [S.alloc("hbb%d" % i, [D], BF16) for i in range(2)]
    stat = [S.alloc("stat%d" % i, [4], F32) for i in range(4)]
    junk = S.alloc("junk", [D], F32)
    S.memset("pool", xhalo[:], 0.0, [xhalo])
    tiles = []
    for i in range(16):
        tiles.append((xX[i * 128:(i + 1) * 128, :], 128, 1, ("X", i)))
    for i in range(4):
        tiles.append((xP[i * 128:(i + 1) * 128, :], 128, 0, ("P", i)))
    for i in range(4):
        tiles.append((xM[i * 128:(i + 1) * 128, :], 128, 1, ("M", i)))
    tiles.append((xH, 2, 1, ("H", 0)))
    def p1_tile(ti, src, nrow, cond, kind, idx):
        xt = xhalo if kind == "H" else xbuf[ti % 5]
        st = stat[ti % 4]; tb = tmpb[ti % 2]; hb = hbb[ti % 2]
        st_ = {}

        def s0():
            S.dma("sp", xt[0:nrow, :], src, xt, writes=[xt])

        def s0b():
            S.act(junk[:], xt[:], AF.Square, [xt], [junk, st], accum=st[:, 0:1])

        def s0c():
            S.act(st[:, 1:2], st[:, 0:1], AF.Ln, [st, epsb], [st], bias=epsb[:, 0:1], scale=1.0 / D)

        def s0d():
            S.act(st[:, 2:3], st[:, 1:2], AF.Exp, [st], [st], scale=-0.5)

        def s1():
            S.stt("dve", tb[:], xt[:], st[:, 2:3], g1[:, cond, :], ALU.mult, ALU.mult, [xt, st, g1], [tb])

        def s1b():
            S.tt("dve", hb[:], tb[:], shf[:, cond, :], ALU.add, [tb, shf], [hb])

        def s2():
            pb = S.bank()
            st_["pb"] = pb
            pbb = pb[:, :].bitcast(BF16)
            for kc in range(8):
                S.tr(pbb[:, kc * 128:(kc + 1) * 128], hb[:, kc * 128:(kc + 1) * 128], identb[:], [hb, identb], [pb])

        def s3():
            pb = st_["pb"]
            pv = pb[:, :].bitcast(BF16).rearrange("p (a b) -> p a b", a=8)
            if kind == "X":
                S.cp("act", hTX[:, :, idx * 128:(idx + 1) * 128], pv, [pb], [hTX])
            elif kind == "P":
                S.cp("act", hTPM[:, :, idx * 128:(idx + 1) * 128], pv, [pb], [hTPM])
            elif kind == "M":
                S.cp("act", hTPM[:, :, 513 + idx * 128:513 + (idx + 1) * 128], pv, [pb], [hTPM])
            else:
                S.ts("dve", hTPM[:, :, 512:513], pv[:, :, 0:1], flg[:, 0:1], ALU.mult, [pb, flg], [hTPM])
                S.ts("dve", hTPM[:, :, 1025:1026], pv[:, :, 1:2], flg[:, 1:2], ALU.mult, [pb, flg], [hTPM])

        return [s0, s0b, s0c, s0d, s1, s1b, s2, s3]

    run_pipeline([p1_tile(ti, src, nrow, cond, kind, idx) for ti, (src, nrow, cond, (kind, idx)) in enumerate(tiles)])
    S.release(mark_base)
    dump("hTX", hTX, [8, 2048], BF16)
    dump("hTPM", hTPM, [8, 1026], BF16)
    dump("g2", g2, [2, D])
    if stop == 'p1':
        S.barrier(); S.emit(); return nc
    mark_dt = S.top
    wdt = S.alloc("wdt", [8, 64], BF16)
    S.dma("pool", wdt[:], wview(w_in, W_DTF, 64), wdt, writes=[wdt])
    t_dtr = S.alloc("t_dtr", [8, 64], F32); t_e = S.alloc("t_e", [8, 64], F32); t_dt = S.alloc("t_dt", [8, 64], F32)
    t_da = S.alloc("t_da", [8, 64], F32); t_tot = S.alloc("t_tot", [8, 64], F32); t_x = S.alloc("t_x", [8, 64], F32)
    t_dec = S.alloc("t_dec", [8, 64], F32); t_y = S.alloc("t_y", [8, 64], F32)
    totX = S.alloc("totX", [16, 64], F32); Ppre = S.alloc("Ppre", [17, 64], F32)
    omg = S.alloc("omg", [16, 64], F32); t_o = S.alloc("t_o", [16, 32], F32); t_o2 = S.alloc("t_o2", [16, 32], F32)

    def chunk_cols(ci):
        if ci < 4:
            return hTPM, ci * 128
        if ci < 8:
            return hTPM, 513 + (ci - 4) * 128
        return hTX, (ci - 8) * 128

    def v8(ap):
        return ap.rearrange("p (a b) -> p a b", a=8)

    def dtA(bi):
        pb = S.bank()
        for j in range(8):
            ht, c0 = chunk_cols(bi * 8 + j)
            for kc in range(8):
                S.mm(pb[:, j * 64:(j + 1) * 64], ht[:, kc, c0:c0 + 128], wdt[:, kc, :], kc == 0, kc == 7, [ht, wdt], [pb])
        S.tt("dve", t_dtr[:], v8(pb[:, :]), rows[:, 0:64].unsqueeze(1).to_broadcast([128, 8, 64]), ALU.add,
             [pb, rows], [t_dtr])
        S.act(t_e[:], t_dtr[:], AF.Exp, [t_dtr], [t_e])
        S.act(t_dt[:], t_e[:], AF.Ln, [t_e, oneb], [t_dt], bias=oneb[:, 0:1])
        S.tt("pool", t_da[:], t_dt[:], Arow[:].unsqueeze(1).to_broadcast([128, 8, 64]), ALU.mult, [t_dt, Arow], [t_da])

    def dtB(bi):
        pi = S.bank(); po = S.bank()
        da_flat = t_da[:].rearrange("p a b -> p (a b)")
        S.mm(pi[:, :], LEf[:], da_flat, True, True, [LEf, t_da], [pi])
        S.mm(po[:, :], onesf[:], da_flat, True, True, [onesf, t_da], [po])
        piv = v8(pi[:, :]); pov = v8(po[:, :])
        S.cp("act", t_tot[:], pov, [po], [t_tot])
        if bi == 0:
            S.act(cdA[:], t_tot[:], AF.Exp, [t_tot], [cdA])
        S.tt("dve", t_x[:, :, 0:32], t_tot[:, :, 0:32], piv[:, :, 0:32], ALU.subtract, [t_tot, pi], [t_x])
        S.tt("dve", t_x[:, :, 32:64], piv[:, :, 32:64], t_da[:, :, 32:64], ALU.subtract, [pi, t_da], [t_x])
        S.act(t_dec[:], t_x[:], AF.Exp, [t_x], [t_dec])
        if bi == 0:
            S.tt("pool", dtdec[:], t_dt[:], t_dec[:], ALU.mult, [t_dt, t_dec], [dtdec])
        else:
            S.tt("pool", dtdecX[:, (bi - 1) * 8:bi * 8, :], t_dt[:], t_dec[:], ALU.mult, [t_dt, t_dec], [dtdecX])
        if bi > 0:
            S.cp("pool", totX[:, (bi - 1) * 8:bi * 8, :], t_tot[:], [t_tot], [totX])
        if bi == 0:
            S.cp("pool", dtO[:], t_dt[:], [t_dt], [dtO])
            S.act(scO[:, :, 0:32], piv[:, :, 0:32], AF.Exp, [pi], [scO])
            dav = dabf[:].rearrange("p c (g d h) -> p c g d h", g=8, d=2, h=4)
            ebv = ebias[:].rearrange("p c (g d h) -> p c g d h", g=8, d=2, h=4)
            for d_ in range(2):
                S.cp("pool", dav[:, :, :, d_, :], t_da[:, :, d_ * 32:(d_ + 1) * 32].rearrange("p c (g h) -> p c g h", g=8),
                     [t_da], [dabf])
            for d_ in range(2):
                S.cp("pool", t_y[:, :, d_ * 32:(d_ + 1) * 32].rearrange("p c (g h) -> p c g h", g=8), dav[:, :, :, d_, :],
                     [dabf], [t_y])
            pr = S.bank()
            S.mm(pr[:, :], LEf[:], t_y[:].rearrange("p a b -> p (a b)"), True, True, [LEf, t_y], [pr])
            prv = v8(pr[:, :])
            S.ts("dve", ebv[:, :, :, 0, :], prv[:, :, 0:32].rearrange("p c (g h) -> p c g h", g=8), -1.0, ALU.mult, [pr], [ebias])
            S.tt("dve", ebv[:, :, :, 1, :], prv[:, :, 32:64].rearrange("p c (g h) -> p c g h", g=8),
                 t_y[:, :, 32:64].rearrange("p c (g h) -> p c g h", g=8), ALU.subtract, [pr, t_y], [ebias])
            S.tt("dve", t_y[:, :, 32:64], t_tot[:, :, 32:64], t_x[:, :, 32:64], ALU.subtract, [t_tot, t_x], [t_y])
            S.act(scO[:, :, 32:64], t_y[:, :, 32:64], AF.Exp, [t_y], [scO])

    def dtOmega():
        S.memset("pool", Ppre[:, 0:1, :], 0.0, [Ppre])
        for k in range(16):
            S.tt("dve", Ppre[:, k + 1, :], Ppre[:, k, :], totX[:, k, :], ALU.add, [Ppre, totX], [Ppre])
        S.memset("pool", omg[:], 0.0, [omg])
        S.memset("pool", wini[:], 0.0, [wini])
        for j in range(1, 4):
            n = 4 * j
            S.tt("dve", t_o[:, 0:n, :], Ppre[:, n:n + 1, 0:32].to_broadcast([128, n, 32]), Ppre[:, 1:n + 1, 0:32],
                 ALU.subtract, [Ppre], [t_o])
            S.act(t_o2[:, 0:n, :], t_o[:, 0:n, :], AF.Exp, [t_o], [t_o2])
            S.stt("dve", omg[:, 0:n, 0:32], t_o2[:, 0:n, :], flg[:, 2 + j:3 + j], omg[:, 0:n, 0:32], ALU.mult, ALU.add,
                  [t_o2, flg, omg], [omg])
        for j in range(0, 3):
            b0 = 4 * (j + 1); n = 16 - b0
            S.tt("dve", t_o[:, 0:n, :], Ppre[:, b0:16, 32:64], Ppre[:, b0:b0 + 1, 32:64].to_broadcast([128, n, 32]),
                 ALU.subtract, [Ppre], [t_o])
            S.act(t_o2[:, 0:n, :], t_o[:, 0:n, :], AF.Exp, [t_o], [t_o2])
            S.stt("dve", omg[:, b0:16, 32:64], t_o2[:, 0:n, :], flg[:, 2 + j:3 + j], omg[:, b0:16, 32:64], ALU.mult, ALU.add,
                  [t_o2, flg, omg], [omg])
        for j in range(4):
            S.act(t_o[:, 0, :], Ppre[:, 4 * j, 0:32], AF.Exp, [Ppre], [t_o])
            S.stt("dve", wini[:, 0:32], t_o[:, 0, :], flg[:, 2 + j:3 + j], wini[:, 0:32], ALU.mult, ALU.add, [t_o, flg, wini], [wini])
            S.tt("dve", t_o[:, 1, :], Ppre[:, 16, 32:64], Ppre[:, 4 * (j + 1), 32:64], ALU.subtract, [Ppre], [t_o])
            S.act(t_o[:, 2, :], t_o[:, 1, :], AF.Exp, [t_o], [t_o])
            S.stt("dve", wini[:, 32:64], t_o[:, 2, :], flg[:, 2 + j:3 + j], wini[:, 32:64], ALU.mult, ALU.add, [t_o, flg, wini], [wini])
        S.tt("dve", dtdecX[:], dtdecX[:], omg[:], ALU.mult, [dtdecX, omg], [dtdecX])

    if stop == 'dt':
        dtA(0); dtB(0); dtA(1); dtB(1); dtA(2); dtB(2); dtOmega()
        S.barrier(); S.emit(); return nc

    tc_sb = S.alloc("tc_sb", [16, 512], BF16); ts_sb = S.alloc("ts_sb", [16, 512], BF16)
    S.dma("sp", tc_sb[:], tcs[0].rearrange("(lt p) n -> p lt n", p=128), tc_sb, writes=[tc_sb])
    S.dma("sp", ts_sb[:], tcs[1].rearrange("(lt p) n -> p lt n", p=128), ts_sb, writes=[ts_sb])
    t2 = S.alloc("t2", [3, 2, 256], BF16)
    S.dma("sp", t2[:], t256.rearrange("w (c p) n -> p w c n", p=128), t2, writes=[t2])
    wuf = [S.alloc("wuf%d" % i, [8, 256], BF16) for i in range(2)]
    Ug = [S.alloc("Ug%d" % i, [20, 256], BF16) for i in range(1)]
    Vt = S.alloc("Vt", [2, 2, 1024], BF16)
    ev = [0]

    def evac(out, in_, reads, writes):
        ev[0] += 1
        S.cp("act" if ev[0] % 2 == 0 else "dve", out, in_, reads, writes)

    dtA(0)
    for g in range(4):
        w = wuf[g % 2]; U = Ug[0]
        S.dma("pool", w[:], wview(w_in, W_UF + g * 256, 256), w, writes=[w])
        for tp in range(10):
            pb = S.bank()
            for j in range(2):
                ti = tp * 2 + j
                ht, c0 = (hTX, ti * 128) if ti < 16 else (hTPM, (ti - 16) * 128)
                for kc in range(8):
                    S.mm(pb[:, j * 256:(j + 1) * 256], ht[:, kc, c0:c0 + 128], w[:, kc, :], kc == 0, kc == 7,
                         [ht, w], [pb])
            evac(U[:, tp * 2:tp * 2 + 2, :], pb[:, :].rearrange("p (a b) -> p a b", a=2), [pb], [U])
        if g == 0:
            dtB(0); dtA(1)
        if g == 1:
            dtB(2)
        for cs, tab in ((0, tc_sb), (1, ts_sb)):
            for wc in range(2):
                pb = S.bank()
                for lt in range(16):
                    S.mm(pb[:, :], U[:, lt, wc * 128:(wc + 1) * 128], tab[:, lt, :], lt == 0, lt == 15, [U, tab], [pb])
                evac(Vt[:, cs, wc, 512:1024], pb[:, :], [pb], [Vt])
        if g == 0:
            dtB(1); dtA(2)
        if g == 1:
            dtOmega()
        for cs in range(2):
            for wc in range(2):
                pb = S.bank()
                for sq in range(2):
                    for lt in range(2):
                        S.mm(pb[:, sq * 256:(sq + 1) * 256], U[:, 16 + sq * 2 + lt, wc * 128:(wc + 1) * 128],
                             t2[:, 0 if cs == 0 else 2, lt, :], lt == 0, lt == 1, [U, t2], [pb])
                evac(Vt[:, cs, wc, 0:512], pb[:, :], [pb], [Vt])
        for tb in range(2):
            for wpc in range(2):
                pb = S.bank()
                k = 0
                for cs in range(2):
                    for wc in range(2):
                        S.mm(pb[:, :], t2[:, 0 if cs == 0 else 1, wc, wpc * 128:(wpc + 1) * 128],
                             Vt[:, cs, wc, tb * 512:(tb + 1) * 512], k == 0, k == 3, [t2, Vt], [pb])
                        k += 1
                evac(fmT[:, g * 2 + wpc, tb * 512:(tb + 1) * 512], pb[:, :], [pb], [fmT])
    dump("fmT", fmT, [8, 1024], BF16)
    dump("dtO", dtO, [8, 64]); dump("scO", scO, [8, 64])
    dump("dtdec", dtdec, [8, 64]); dump("cdA", cdA, [8, 64])
    S.release(mark_dt)
    if stop == 'fourier':
        S.barrier(); S.emit(); return nc
    accs = [S.alloc("acc%d" % i, [512], F32) for i in range(4)]
    cvk = [0]

    def conv_unit(wt, wcol0, cidx, ht, hc0, n_in, o_rel0, n_out, left_from, right_to, dest_ap, dest_tile):
        st_ = {}

        def s0():
            pb = S.bank()
            st_["pb"] = pb
            for kc in range(8):
                S.mm(pb[:, 0:n_in], wt[:, kc, wcol0:wcol0 + 128], ht[:, kc, hc0:hc0 + n_in], kc == 0, kc == 7, [wt, ht], [pb])

        def s1():
            pb = st_["pb"]
            cvk[0] += 1
            acc = accs[cvk[0] % len(accs)]
            st_["acc"] = acc
            S.act(acc[:, 0:n_out], pb[:, o_rel0:o_rel0 + n_out], AF.Identity, [pb, cw, cb], [acc],
                  bias=cb[:, cidx:cidx + 1], scale=cw[:, cidx * 3 + 1:cidx * 3 + 2])
        def s1b():
            pb = st_["pb"]; acc = st_["acc"]
            a, b = max(o_rel0, left_from), o_rel0 + n_out
            S.stt("dve", acc[:, a - o_rel0:b - o_rel0], pb[:, a - 1:b - 1], cw[:, cidx * 3:cidx * 3 + 1],
                  acc[:, a - o_rel0:b - o_rel0], ALU.mult, ALU.add, [pb, cw, acc], [acc])
            a, b = o_rel0, min(o_rel0 + n_out, right_to)
            S.stt("dve", acc[:, a - o_rel0:b - o_rel0], pb[:, a + 1:b + 1], cw[:, cidx * 3 + 2:cidx * 3 + 3],
                  acc[:, a - o_rel0:b - o_rel0], ALU.mult, ALU.add, [pb, cw, acc], [acc])

        def s2():
            acc = st_["acc"]
            S.act(dest_ap, acc[:, 0:n_out], AF.Silu, [acc], [dest_tile])

        return [s0, s1, s1b, s2]

    def h4(ap):
        return ap.rearrange("p (h q) -> p h q", h=4)

    def bc4(ap):
        return ap.unsqueeze(2).to_broadcast([128, 4, 64])

    wgc = [S.alloc("wgc%d" % i, [8, 384], BF16) for i in range(2)]
    xsTX = S.alloc("xsTX", [2, 2048], BF16); BTX = S.alloc("BTX", [2048], BF16)
    xbs = [S.alloc("xbs%d" % i, [384], BF16) for i in range(3)]
    xdd = [S.alloc("xdd%d" % i, [256], BF16) for i in range(6)]
    Sst = [S.alloc("Sst%d" % i, [256], F32) for i in range(2)]
    Stmp = [S.alloc("Stmp%d" % i, [256], F32) for i in range(2)]
    segs = [(0, 410), (410, 820), (820, 1230), (1230, 1640), (1640, 2048)]
    def load_wgc(g):
        wt = wgc[g % 2]
        S.dma("pool", wt[:, :, 0:256], wview(w_in, W_XS + g * 256, 256), wt, writes=[wt])
        S.dma("pool", wt[:, :, 256:384], wview(w_in, W_B + g * 128, 128), wt, writes=[wt])

    load_wgc(0)
    for g in range(8):
        wt = wgc[g % 2]
        units = []
        for (o0, o1) in segs:
            i0, i1 = max(o0 - 1, 0), min(o1 + 1, 2048)
            for cc in range(3):
                cidx = (2 * g + cc) if cc < 2 else 16 + g
                dest = xsTX[:, cc, o0:o1] if cc < 2 else BTX[:, o0:o1]
                units.append(conv_unit(wt, cc * 128, cidx, hTX, i0, i1 - i0, o0 - i0, o1 - o0, 1 if i0 == 0 else 0,
                                       2047 - i0, dest, xsTX if cc < 2 else BTX))
        run_pipeline(units)
        if g + 1 < 8:
            load_wgc(g + 1)
        psF = S.reserve_bank(); psB = S.reserve_bank()
        def ctx_chunk(c):
            st_ = {}

            def s0():
                pb = S.bank(); pbb = pb[:, :].bitcast(BF16)
                st_["pb"] = pb
                for cc in range(2):
                    S.tr(pbb[:, cc * 128:(cc + 1) * 128], xsTX[:, cc, c * 128:(c + 1) * 128], identb[:], [xsTX, identb], [pb])
                S.tr(pbb[:, 256:384], BTX[:, c * 128:(c + 1) * 128], identb[:], [BTX, identb], [pb])

            def s1():
                pb = st_["pb"]; pbb = pb[:, :].bitcast(BF16)
                xb = xbs[c % len(xbs)]
                S.cp("act", xb[:], pbb[:, 0:384], [pb], [xb])
                for d in range(2):
                    if (d == 0 and c >= 12) or (d == 1 and c < 4):
                        continue
                    col0 = d * 32 + g * 4
                    xd = xdd[(2 * c + d) % len(xdd)]
                    S.tt("dve", h4(xd[:]), h4(xb[:, 0:256]), bc4(dtdecX[:, c, col0:col0 + 4]),
                         ALU.mult, [xb, dtdecX], [xd])

            def s2():
                xb = xbs[c % len(xbs)]
                for d in range(2):
                    if (d == 0 and c >= 12) or (d == 1 and c < 4):
                        continue
                    xd = xdd[(2 * c + d) % len(xdd)]
                    pacc = psF if d == 0 else psB
                    first = (c == 0) if d == 0 else (c == 4)
                    last = (c == 11) if d == 0 else (c == 15)
                    S.mm(pacc[:, 0:256], xb[:, 256:384], xd[:], first, last, [xb, xd], [pacc])

            return [s0, s1, s2]

        run_pipeline([ctx_chunk(c) for c in range(16)])
        for d in range(2):
            Sm = SmF if d == 0 else SmB
            pacc = psF if d == 0 else psB
            S.dma("sp", Sst[d][:], stT[d][:, g * 256:(g + 1) * 256], Sst[d], writes=[Sst[d]])
            S.tt("dve", h4(Stmp[d][:]), h4(Sst[d][:]), bc4(wini[:, d * 32 + g * 4:d * 32 + g * 4 + 4]), ALU.mult,
                 [Sst[d], wini], [Stmp[d]])
            S.tt("dve", Sm[:, g, :], Stmp[d][:], pacc[:, 0:256], ALU.add, [Stmp[d], pacc], [Sm])
        S.unreserve(psF); S.unreserve(psB)
    dump("SmF", SmF, [8, 256], BF16); dump("SmB", SmB, [8, 256], BF16)
    S.release(mark_pre_hTX)
    if stop == 'ctx':
        S.barrier(); S.emit(); return nc
    ynT = S.alloc("ynT", [16, 1024], BF16)
    mark_m = S.top
    accs = [S.alloc("accm%d" % i, [264], F32) for i in range(4)]
    wg2 = [S.alloc("wg2_%d" % i, [8, 768], BF16) for i in range(2)]
    xsT = S.alloc("xsT", [2, 1024], BF16); BT = S.alloc("BT", [1024], BF16); CT = S.alloc("CT", [1024], BF16)
    zs = S.alloc("zs", [8, 256], BF16)
    xb8 = S.alloc("xb8", [8, 384], BF16)
    xdt = S.alloc("xdt", [8, 2, 256], BF16)
    SinB = S.alloc("SinB", [8, 2, 256], BF16)
    xbc_ = [Tile("xb8_%d" % c, xb8[:, c, :]) for c in range(8)]
    xdc_ = [Tile("xdt_%d" % c, xdt[:, c, :, :]) for c in range(8)]
    zsc_ = [Tile("zs_%d" % c, zs[:, c, :]) for c in range(8)]
    sic_ = [[Tile("sin_%d_%d" % (c, d), SinB[:, c, d, :]) for d in range(2)] for c in range(8)]
    xdd2 = [S.alloc("xdd2_%d" % i, [256], BF16) for i in range(6)]
    Sa = [S.alloc("Sa%d" % i, [256], F32) for i in range(6)]
    Stm = [S.alloc("Stm%d" % i, [256], F32) for i in range(4)]
    stg = [S.alloc("stg%d" % i, [512], F32) for i in range(1)]
    DAt = [S.alloc("DAt%d" % i, [8, 128], BF16) for i in range(2)]
    Ee = [S.alloc("E%d" % i, [8, 128], BF16) for i in range(2)]
    Xb = [S.alloc("Xb%d" % i, [8, 128], F32) for i in range(2)]
    Gs = [S.alloc("Gs%d" % i, [128], BF16) for i in range(4)]
    Ww = [S.alloc("W%d" % i, [8, 128], BF16) for i in range(2)]
    yA = [S.alloc("yA%d" % i, [256], F32) for i in range(4)]
    yH = [[Tile("yA%d_%d" % (i, h), yA[i][:, h * 64:(h + 1) * 64]) for h in range(4)] for i in range(4)]
    Dm2 = [S.alloc("Dm%d" % i, [4, 128], BF16) for i in range(2)]
    ynk = [S.alloc("ynk%d" % i, [256], BF16) for i in range(2)]
    jk = S.alloc("jk", [256], F32)
    st2 = [S.alloc("st2_%d" % i, [4], F32) for i in range(2)]
    rr = {"sa": 0, "xd": 0, "tm": 0, "stg": 0, "ew": 0}

    def nxt(key, lst):
        rr[key] += 1
        return lst[rr[key] % len(lst)]

    def ew():
        return "dve"

    def load_wg2(g):
        wt = wg2[g % 2]
        S.dma("pool", wt[:, :, 0:256], wview(w_in, W_XS + g * 256, 256), wt, writes=[wt])
        S.dma("pool", wt[:, :, 256:384], wview(w_in, W_B + g * 128, 128), wt, writes=[wt])
        S.dma("pool", wt[:, :, 384:512], wview(w_in, W_C + g * 128, 128), wt, writes=[wt])
        S.dma("pool", wt[:, :, 512:768], wview(w_in, W_Z + g * 256, 256), wt, writes=[wt])

    def front_units(g):
        wt = wg2[g % 2]
        units = []
        for cc in range(4):
            cidx = (2 * g + cc) if cc < 2 else (16 + g if cc == 2 else 24 + g)
            dtile = xsT if cc < 2 else (BT if cc == 2 else CT)
            for u in range(4):
                dest = dtile[:, cc, u * 256:(u + 1) * 256] if cc < 2 else dtile[:, u * 256:(u + 1) * 256]
                if u < 2:
                    units.append(conv_unit(wt, cc * 128, cidx, hTPM, u * 256, 256, 0, 256, 1, 255, dest, dtile))
                else:
                    units.append(conv_unit(wt, cc * 128, cidx, hTPM, 512 + (u - 2) * 256, 258, 1, 256, 0, 10 ** 6,
                                           dest, dtile))

        def z_unit(cp_):
            st_ = {}

            def s0():
                pb = S.bank()
                st_["pb"] = pb
                for j in range(2):
                    ht, c0 = chunk_cols(cp_ * 2 + j)
                    for kc in range(8):
                        S.mm(pb[:, j * 256:(j + 1) * 256], ht[:, kc, c0:c0 + 128], wt[:, kc, 512:768], kc == 0, kc == 7,
                             [ht, wt], [pb])

            def s1():
                pb = st_["pb"]
                S.act(zs[:, cp_ * 2:cp_ * 2 + 2, :], pb[:, :].rearrange("p (a b) -> p a b", a=2), AF.Silu, [pb],
                      [zsc_[cp_ * 2], zsc_[cp_ * 2 + 1]])

            return [s0, s1]

        for cp_ in range(4):
            units.append(z_unit(cp_))
        return units

    load_wg2(0)
    run_pipeline(front_units(0))
    for g in range(8):
        if g + 1 < 8:
            load_wg2(g + 1)
        Dm = Dm2[g % 2]
        for h in range(4):
            S.ts("pool", Dm[:, h, :], identb[:], rows[:, 128 + g * 4 + h:129 + g * 4 + h], ALU.mult, [identb, rows], [Dm])

        def tr_chunk(c):
            st_ = {}

            def s0():
                pb = S.bank(); pbb = pb[:, :].bitcast(BF16)
                st_["pb"] = pb
                for cc in range(2):
                    S.tr(pbb[:, cc * 128:(cc + 1) * 128], xsT[:, cc, c * 128:(c + 1) * 128], identb[:], [xsT, identb], [pb])
                S.tr(pbb[:, 256:384], BT[:, c * 128:(c + 1) * 128], identb[:], [BT, identb], [pb])

            def s1():
                pb = st_["pb"]; pbb = pb[:, :].bitcast(BF16)
                S.cp("act", xbc_[c][:], pbb[:, 0:384], [pb], [xbc_[c]])
                for d in range(2):
                    S.tt("pool", h4(xdc_[c][:, d, :]), h4(xbc_[c][:, 0:256]), bc4(dtO[:, c, d * 32 + g * 4:d * 32 + g * 4 + 4]),
                         ALU.mult, [xbc_[c], dtO], [xdc_[c]])

            return [s0, s1]

        run_pipeline([tr_chunk(c) for c in range(8)])

        def mk_xd(c, d):
            xd = nxt("xd", xdd2)
            col0 = d * 32 + g * 4
            S.tt(ew(), h4(xd[:]), h4(xbc_[c][:, 0:256]), bc4(dtdec[:, c, col0:col0 + 4]), ALU.mult, [xbc_[c], dtdec], [xd])
            return xd

        def sc_mm(c, d):
            xd = mk_xd(c, d)
            pc = S.bank()
            S.mm(pc[:, 0:256], xbc_[c][:, 256:384], xd[:], True, True, [xbc_[c], xd], [pc])
            return pc

        def upd(Sold_ap, Sold_tile, c, d, pc):
            col0 = d * 32 + g * 4
            tm = nxt("tm", Stm); Snew = nxt("sa", Sa)
            S.tt(ew(), h4(tm[:]), h4(Sold_ap), bc4(cdA[:, c, col0:col0 + 4]), ALU.mult, [Sold_tile, cdA], [tm])
            S.tt("dve", Snew[:], tm[:], pc[:, 0:256], ALU.add, [tm, pc], [Snew])
            return Snew

        for sq in range(2):
            c0, c1 = 2 * sq, 2 * sq + 1
            pb = S.bank()
            for d, (ca, cb_) in enumerate(((c0, c1), (c1, c0))):
                col0 = d * 32 + g * 4
                xa = mk_xd(ca, d); xb_ = mk_xd(cb_, d); xa2 = nxt("xd", xdd2)
                S.tt(ew(), h4(xa2[:]), h4(xa[:]), bc4(cdA[:, cb_, col0:col0 + 4]), ALU.mult, [xa, cdA], [xa2])
                pc = S.bank()
                S.mm(pc[:, 0:256], xbc_[ca][:, 256:384], xa[:], True, True, [xbc_[ca], xa], [pc])
                S.cp("act", sic_[cb_][d][:], pc[:, 0:256], [pc], [sic_[cb_][d]])
                for hh in range(2):
                    osl = slice((d * 2 + hh) * 128, (d * 2 + hh + 1) * 128)
                    S.mm(pb[:, osl], xa2[:, hh * 128:(hh + 1) * 128], xbc_[ca][:, 256:384], True, False, [xa2, xbc_[ca]], [pb])
                    S.mm(pb[:, osl], xb_[:, hh * 128:(hh + 1) * 128], xbc_[cb_][:, 256:384], False, True, [xb_, xbc_[cb_]], [pb])
            sg = nxt("stg", stg)
            S.cp("act", sg[:], pb[:, :], [pb], [sg])
            for d, dst in enumerate((sF, sB)):
                for hh in range(2):
                    h0 = g * 4 + hh * 2
                    S.dma("sp", dst[sq, h0:h0 + 2].rearrange("h p n -> (h p) n"),
                          sg[:, (d * 2 + hh) * 128:(d * 2 + hh + 1) * 128], sg, reads=[sg])
        curs = [(SmF[:, g, :], SmF), (SmB[:, g, :], SmB)]
        orders = [[4, 5, 6, 7], [7, 6, 5, 4]]
        pcs = {}
        for k in range(3):
            pbk = S.bank()
            for d in range(2):
                c = orders[d][k]
                xd = mk_xd(c, d)
                S.mm(pbk[:, d * 256:(d + 1) * 256], xbc_[c][:, 256:384], xd[:], True, True, [xbc_[c], xd], [pbk])
                pcs[(k, d)] = pbk
        for k in range(4):
            for d in range(2):
                c = orders[d][k]
                cur_ap, cur_t = curs[d]
                S.cp("act", sic_[c][d][:], cur_ap, [cur_t], [sic_[c][d]])
                if k < 3:
                    col0 = d * 32 + g * 4
                    tm = nxt("tm", Stm); Snew = nxt("sa", Sa)
                    S.tt("dve", h4(tm[:]), h4(cur_ap), bc4(cdA[:, c, col0:col0 + 4]), ALU.mult, [cur_t, cdA], [tm])
                    S.tt("dve", Snew[:], tm[:], pcs[(k, d)][:, d * 256:(d + 1) * 256], ALU.add, [tm, pcs[(k, d)]], [Snew])
                    curs[d] = (Snew[:], Snew)

        def out_chunk(c):
            tc0 = c * 128
            has_f = (c % 2 == 1) if c < 4 else True
            has_b = (c % 2 == 0) if c < 4 else True
            i2 = c % 2
            st_ = {}

            bkA = [S.psum[0 + i2], S.psum[2 + i2]]; bkC = S.psum[4 + i2]; bkD = S.psum[6 + i2]

            def sA():
                pg = bkC
                st_["pg"] = pg
                S.mm(pg[:, 0:128], BT[:, tc0:tc0 + 128], CT[:, tc0:tc0 + 128], True, True, [BT, CT], [pg])
                S.cp("act", Gs[c % 4][:], pg[:, 0:128], [pg], [Gs[c % 4]])
                da_ = DAt[i2]
                S.cp("act", da_[:], dabf[:, c, g * 8:(g + 1) * 8].unsqueeze(2).to_broadcast([128, 8, 128]), [dabf], [da_])
                st_["ps"] = []
                for d in range(2):
                    U2 = LEb if d == 0 else NLTb
                    NG = NEGf if d == 0 else NEGb
                    ps = bkA[d]
                    st_["ps"].append(ps)
                    for h in range(4):
                        S.mm(ps[:, h * 128:(h + 1) * 128], da_[:, d * 4 + h, :], U2[:], True, False, [da_, U2], [ps])
                        S.mm(ps[:, h * 128:(h + 1) * 128], identb[:], NG[:], False, True, [identb, NG], [ps])

            def sA2():
                x_ = Xb[i2]
                for d in range(2):
                    ps = st_["ps"][d]
                    col = g * 8 + d * 4
                    S.tt("dve", x_[:, d * 4:(d + 1) * 4, :], ps[:, :].rearrange("p (h q) -> p h q", h=4),
                         ebias[:, c, col:col + 4].unsqueeze(2).to_broadcast([128, 4, 128]), ALU.add, [ps, ebias], [x_])

            def sB():
                S.act(Ee[i2][:], Xb[i2][:], AF.Exp, [Xb[i2]], [Ee[i2]])

            def sB2():
                S.tt("dve", Ww[i2][:, 0:4, :], Ee[i2][:, 0:4, :], Gs[c % 4][:].unsqueeze(1).to_broadcast([128, 4, 128]), ALU.mult,
                     [Ee[i2], Gs[c % 4]], [Ww[i2]])
                S.tt("pool", Ww[i2][:, 4:8, :], Ee[i2][:, 4:8, :], Gs[c % 4][:].unsqueeze(1).to_broadcast([128, 4, 128]), ALU.mult,
                     [Ee[i2], Gs[c % 4]], [Ww[i2]])

            def sC():
                w_ = Ww[i2]
                py = bkC
                pyo = 128
                for h in range(4):
                    S.mm(py[:, pyo + h * 64:pyo + (h + 1) * 64], w_[:, h, :], xdc_[c][:, 0, h * 64:(h + 1) * 64], True, False,
                         [w_, xdc_[c]], [py])
                    S.mm(py[:, pyo + h * 64:pyo + (h + 1) * 64], w_[:, 4 + h, :], xdc_[c][:, 1, h * 64:(h + 1) * 64], False, False,
                         [w_, xdc_[c]], [py])
                    S.mm(py[:, pyo + h * 64:pyo + (h + 1) * 64], Dm[:, h, :], xbc_[c][:, h * 64:(h + 1) * 64], False, True,
                         [Dm, xbc_[c]], [py])
                po = bkD
                if has_f:
                    S.mm(po[:, 0:256], CT[:, tc0:tc0 + 128], sic_[c][0][:], True, True, [CT, sic_[c][0]], [po])
                if has_b:
                    S.mm(po[:, 256:512], CT[:, tc0:tc0 + 128], sic_[c][1][:], True, True, [CT, sic_[c][1]], [po])
                y = yA[c % 4]; yh = yH[c % 4]
                S.cp("act", y[:], py[:, pyo:pyo + 256], [py], [y] + yh)

            def sC2():
                po = bkD
                y = yA[c % 4]; yh = yH[c % 4]
                for (has, off, sc0) in ((has_f, 0, g * 4), (has_b, 256, 32 + g * 4)):
                    if not has:
                        continue
                    for h in range(4):
                        hs = slice(h * 64, (h + 1) * 64)
                        S.stt("dve", y[:, hs], po[:, off + h * 64:off + (h + 1) * 64], scO[:, c, sc0 + h:sc0 + h + 1],
                              y[:, hs], ALU.mult, ALU.add, [po, scO, yh[h]], [yh[h]])

            def sD():
                y = yA[c % 4]
                S.tt("pool", y[:], y[:], zsc_[c][:], ALU.mult, yH[c % 4] + [zsc_[c]], [y] + yH[c % 4])

            def sD2():
                y = yA[c % 4]
                st = st2[i2]
                S.act(jk[:], y[:], AF.Square, [y] + yH[c % 4], [jk, st], accum=st[:, 0:1])
                S.act(st[:, 1:2], st[:, 0:1], AF.Ln, [st, epsb], [st], bias=epsb[:, 0:1], scale=1.0 / 256)
                S.act(st[:, 2:3], st[:, 1:2], AF.Exp, [st], [st], scale=-0.5)
                S.act(ynk[i2][:], y[:], AF.Copy, [y, st] + yH[c % 4], [ynk[i2]], scale=st[:, 2:3])

            def sE():
                pt = bkC; ptb = pt[:, :].bitcast(BF16)
                for j in range(2):
                    S.tr(ptb[:, 768 + j * 128:768 + (j + 1) * 128], ynk[i2][:, j * 128:(j + 1) * 128], identb[:],
                         [ynk[i2], identb], [pt])
                S.cp("act", ynT[:, g * 2:g * 2 + 2, tc0:tc0 + 128], ptb[:, 768:1024].rearrange("p (a b) -> p a b", a=2),
                     [pt], [ynT])

            return [sA, sA2, sB, sB2, sC, sC2, sD, sD2, sE]

        merged = [out_chunk(c) for c in range(8)]
        if g + 1 < 8:
            merged += [[] for _ in range(1)]
            merged += front_units(g + 1)
        S.reserved.extend(S.psum[4:8])
        run_pipeline(merged)
        for bk in S.psum[4:8]:
            S.reserved.remove(bk)
    dump("ynT", ynT, [16, 1024], BF16)
    S.release(mark_m)
    if stop == 'ssd':
        S.barrier(); S.emit(); return nc
    f1T = S.alloc("f1T", [8, 1024], BF16); mT = S.alloc("mT", [8, 1024], BF16)
    mark_t = S.top
    wA = [S.alloc("wA%d" % i, [8, 512], BF16) for i in range(2)]
    wB = [S.alloc("wB%d" % i, [8, 512], BF16) for i in range(2)]
    sgt = [S.alloc("sgt%d" % i, [512], F32) for i in range(2)]
    for cbk in range(2):
        wf = wA[cbk % 2]; wg_ = wB[cbk % 2]
        S.dma("pool", wf[:], wview(fnet_w, cbk * 512, 512), wf, writes=[wf])
        S.dma("pool", wg_[:], wview(w_in, W_GF + cbk * 512, 512), wg_, writes=[wg_])
        for tb in range(2):
            hc0 = 0 if tb == 0 else 513
            for c4 in range(4):
                cp_ = cbk * 4 + c4
                p1 = S.bank(); p2 = S.bank()
                for kc in range(8):
                    S.mm(p1[:, :], wf[:, kc, c4 * 128:(c4 + 1) * 128], fmT[:, kc, tb * 512:(tb + 1) * 512], kc == 0, kc == 7,
                         [wf, fmT], [p1])
                for kc in range(8):
                    S.mm(p2[:, :], wg_[:, kc, c4 * 128:(c4 + 1) * 128], hTPM[:, kc, hc0:hc0 + 512], kc == 0, kc == 7,
                         [wg_, hTPM], [p2])
                sg = sgt[c4 % 2]
                S.act(sg[:], p2[:, :], AF.Silu, [p2], [sg])
                S.tt("dve", f1T[:, cp_, tb * 512:(tb + 1) * 512], p1[:, :], sg[:], ALU.mult, [p1, sg], [f1T])
    S.release(mark_t)
    wA2 = [S.alloc("wA2_%d" % i, [8, 256], BF16) for i in range(2)]; wS2 = [S.alloc("wS_%d" % i, [16, 256], BF16) for i in range(2)]
    wC2 = [S.alloc("wC_%d" % i, [8, 256], BF16) for i in range(2)]; wD2 = [S.alloc("wD_%d" % i, [8, 256], BF16) for i in range(2)]
    nwT = S.alloc("nwT", [16], F32)
    S.dma("sp", nwT[:], nwTd, nwT, writes=[nwT])
    sga = [S.alloc("sga%d" % i, [512], F32) for i in range(2)]
    sgb = [S.alloc("sgb%d" % i, [512], F32) for i in range(2)]

    def load_t2(cbk):
        i = cbk % 2
        S.dma("pool", wA2[i][:], wview(wbf, cbk * 256, 256), wA2[i], writes=[wA2[i]])
        S.dma("pool", wS2[i][:], wview(wbs, cbk * 256, 256), wS2[i], writes=[wS2[i]])
        S.dma("pool", wC2[i][:], wview(w_in, W_MGF + cbk * 256, 256), wC2[i], writes=[wC2[i]])
        S.dma("pool", wD2[i][:], wview(w_in, W_MGS + cbk * 256, 256), wD2[i], writes=[wD2[i]])
        S.tt("dve", wS2[i][:], wS2[i][:], nwT[:].unsqueeze(2).to_broadcast([128, 16, 256]), ALU.mult, [wS2[i], nwT], [wS2[i]])

    load_t2(0)
    for cbk in range(4):
        if cbk + 1 < 4:
            load_t2(cbk + 1)
        wA = wA2[cbk % 2]; wS = wS2[cbk % 2]; wC = wC2[cbk % 2]; wD = wD2[cbk % 2]
        for tb in range(2):
            hc0 = 0 if tb == 0 else 513
            tsl = slice(tb * 512, (tb + 1) * 512)
            for c4 in range(2):
                cp_ = cbk * 2 + c4
                csl = slice(c4 * 128, (c4 + 1) * 128)
                pA = S.bank(); pB = S.bank(); pC = S.bank(); pD = S.bank()
                for kc in range(8):
                    S.mm(pC[:, :], wC[:, kc, csl], hTPM[:, kc, hc0:hc0 + 512], kc == 0, kc == 7, [wC, hTPM], [pC])
                for kc in range(8):
                    S.mm(pD[:, :], wD[:, kc, csl], hTPM[:, kc, hc0:hc0 + 512], kc == 0, kc == 7, [wD, hTPM], [pD])
                for kc in range(8):
                    S.mm(pA[:, :], wA[:, kc, csl], f1T[:, kc, tsl], kc == 0, kc == 7, [wA, f1T], [pA])
                for kc in range(16):
                    S.mm(pB[:, :], wS[:, kc, csl], ynT[:, kc, tsl], kc == 0, kc == 15, [wS, ynT], [pB])
                a_ = sga[c4 % 2]; b_ = sgb[c4 % 2]
                S.act(a_[:], pC[:, :], AF.Sigmoid, [pC], [a_])
                S.act(b_[:], pD[:, :], AF.Sigmoid, [pD], [b_])
                S.tt("dve", a_[:], pA[:, :], a_[:], ALU.mult, [pA, a_], [a_])
                S.tt("dve", b_[:], pB[:, :], b_[:], ALU.mult, [pB, b_], [b_])
                S.tt("dve", mT[:, cp_, tsl], a_[:], b_[:], ALU.add, [a_, b_], [mT])
    S.release(mark_t)
    wo = [S.alloc("wo%d" % i, [8, 512], BF16) for i in range(2)]
    xt3 = [S.alloc("xt3_%d" % i, [D], F32) for i in range(3)]
    tmp3 = [S.alloc("tmp3_%d" % i, [D], F32) for i in range(2)]
    ot3 = [S.alloc("ot3_%d" % i, [D], F32) for i in range(2)]
    st3 = [S.alloc("st3_%d" % i, [8], F32) for i in range(2)]
    jk3 = S.alloc("jk3", [512], F32)
    for cbk in range(2):
        S.dma("pool", wo[cbk][:], wview(w_out, cbk * 512, 512), wo[cbk], writes=[wo[cbk]])
    def t3_tile(ti):
        src = xP[ti * 128:(ti + 1) * 128, :] if ti < 4 else xM[(ti - 4) * 128:(ti - 3) * 128, :]
        dst = yP[ti * 128:(ti + 1) * 128, :] if ti < 4 else yM[(ti - 4) * 128:(ti - 3) * 128, :]
        cond = 0 if ti < 4 else 1
        xt = xt3[ti % 3]; ot = ot3[ti % 2]; st = st3[ti % 2]; tmp = tmp3[ti % 2]
        st_ = {}

        def s0():
            S.dma("sp", xt[:], src, xt, writes=[xt])
            pO = [S.bank(), S.bank()]
            st_["pO"] = pO
            for cbk in range(2):
                for kc in range(8):
                    S.mm(pO[cbk][:, :], mT[:, kc, ti * 128:(ti + 1) * 128], wo[cbk][:, kc, :], kc == 0, kc == 7,
                         [mT, wo[cbk]], [pO[cbk]])

        def s1():
            pO = st_["pO"]
            for cbk in range(2):
                S.act(jk3[:], pO[cbk][:, :], AF.Square, [pO[cbk]], [jk3, st], accum=st[:, cbk:cbk + 1])
            S.tt("dve", st[:, 2:3], st[:, 0:1], st[:, 1:2], ALU.add, [st], [st])
            S.act(st[:, 3:4], st[:, 2:3], AF.Ln, [st, epsb], [st], bias=epsb[:, 0:1], scale=1.0 / D)
            S.act(st[:, 4:5], st[:, 3:4], AF.Exp, [st], [st], scale=-0.5)

        def s2():
            pO = st_["pO"]
            for cbk in range(2):
                sl = slice(cbk * 512, (cbk + 1) * 512)
                S.stt("dve", tmp[:, sl], pO[cbk][:, :], st[:, 4:5], g2[:, cond, sl], ALU.mult, ALU.mult,
                      [pO[cbk], st, g2], [tmp])
                S.tt("dve", ot[:, sl], tmp[:, sl], xt[:, sl], ALU.add, [tmp, xt], [ot])

        def s3():
            S.dma("sp", dst, ot[:], ot, reads=[ot])

        return [s0, s1, s2, s3]

    run_pipeline([t3_tile(ti) for ti in range(8)])
    S.barrier()
    S.emit()
    return nc


def _dft_tables():
    l = np.arange(2048)
    r, c = l // 64, l % 64
    ph = (np.outer(r, r) / 32.0 + np.outer(c, c) / 64.0)
    ang = 2.0 * np.pi * (ph % 1.0)
    sc = 1.0 / math.sqrt(2048.0)
    cpos = (np.cos(ang) * sc).astype(np.float32)
    spos = (np.sin(ang) * sc).astype(np.float32)
    k = np.arange(256)
    a2 = 2.0 * np.pi * ((np.outer(k, k) % 256) / 256.0)
    c256 = (np.cos(a2) / 16.0).astype(np.float32)
    s256 = (np.sin(a2) / 16.0).astype(np.float32)
    return cpos, spos, c256, s256


def prep_inputs(x_prompt, x_sample, c, state_ssd_fwd, state_ssd_bwd, c_ctx, ada_w, ada_b, pre_norm_w,
                post_norm_w, w_in, conv_w, conv_b, dt_bias_fwd, dt_bias_bwd, a_log_fwd, a_log_bwd, d_skip,
                ssd_norm_w, fnet_w, w_branch_f, w_branch_s, w_out):
    f = lambda a: np.ascontiguousarray(np.asarray(a, dtype=np.float32))
    bf = lambda a: np.ascontiguousarray(a.astype(ml_dtypes.bfloat16))
    x_prompt, x_sample, c, c_ctx = f(x_prompt), f(x_sample), f(c), f(c_ctx)
    sf, sb = f(state_ssd_fwd), f(state_ssd_bwd)
    cpos, spos, c256, s256 = _dft_tables()
    t256 = bf(np.stack([c256, -s256, s256]))
    cw = f(conv_w)[0].reshape(3, 32, 128).transpose(2, 1, 0).reshape(128, 96)
    cb = f(conv_b)[0].reshape(32, 128).T
    rowsm = np.concatenate([f(dt_bias_fwd)[0], f(dt_bias_bwd)[0], f(a_log_fwd)[0], f(a_log_bwd)[0],
                            f(d_skip)[0]])[None, :]
    rowbig = np.concatenate([f(ssd_norm_w)[0], f(pre_norm_w)[0], f(post_norm_w)[0]])[None, :]
    shared = {
        "t256": t256, "cw": f(cw), "cb": f(cb), "rowsm": f(rowsm), "rowbig": f(rowbig), "nwT": f(f(ssd_norm_w)[0].reshape(16, 128).T),
        "ada_w": f(ada_w)[0], "ada_b": f(ada_b), "w_in": f(w_in)[0], "fnet_w": f(fnet_w)[0],
        "wbf": f(w_branch_f)[0], "wbs": f(w_branch_s)[0], "w_out": f(w_out)[0],
    }
    maps = []
    zero_row = np.zeros((1, D), np.float32)
    for core in range(8):
        b, q = core // 4, core % 4
        xs = x_sample[b]
        hl = xs[q * 512 - 1:q * 512] if q > 0 else zero_row
        hr = xs[(q + 1) * 512:(q + 1) * 512 + 1] if q < 3 else zero_row
        flags = np.zeros((1, 8), np.float32)
        flags[0, 0] = 1.0 if q > 0 else 0.0
        flags[0, 1] = 1.0 if q < 3 else 0.0
        flags[0, 2 + q] = 1.0
        cc = np.stack([c_ctx, c[b]])
        cT = cc.reshape(2, 8, 128).transpose(2, 1, 0).reshape(128, 16)
        st = np.stack([sf[b, 0], sb[b, 0]])
        stT = st.reshape(2, 2048, 128).transpose(0, 2, 1)
        m = dict(shared)
        m.update({
            "xX": f(xs), "xP": f(x_prompt[2 * core:2 * core + 2].reshape(512, D)),
            "xM": f(xs[q * 512:(q + 1) * 512]), "xH": f(np.concatenate([hl, hr], 0)),
            "fl": flags, "cT": f(cT), "stT": f(stT),
            "tcs": bf(np.stack([cpos[:, q * 512:(q + 1) * 512], spos[:, q * 512:(q + 1) * 512]])),
        })
        maps.append(m)
    return maps


_NC_CACHE = {}


def kernel(**inputs):
    maps = prep_inputs(**inputs)
    if "nc" not in _NC_CACHE:
        _NC_CACHE["nc"] = build(False)
    res = run_bass_kernel_spmd(_NC_CACHE["nc"], maps, core_ids=list(range(8)))
    r = res.results
    y_prompt = np.concatenate([r[i]["yP"].reshape(2, 256, D) for i in range(8)], 0).astype(np.float32)
    y_sample = np.stack([np.concatenate([r[4 * b + q]["yM"] for q in range(4)], 0) for b in range(2)]).astype(np.float32)
    nf = np.concatenate([r[i]["sF"] for i in range(8)], 0)[:, None].astype(np.float32)
    nb = np.concatenate([r[i]["sB"] for i in range(8)], 0)[:, None].astype(np.float32)
    return (y_prompt, y_sample, nf, nb)
```

```python
import math
import numpy as np
import ml_dtypes
import concourse.bass as bass
import concourse.mybir as mybir
from concourse.bass_utils import run_bass_kernel_spmd

F32 = mybir.dt.float32
BF16 = mybir.dt.bfloat16
AF = mybir.ActivationFunctionType
ALU = mybir.AluOpType

SAME_ENGINE_SYNC = True
EPS = 1e-6
D = 1024
NX, NP_, NM = 2048, 512, 512
COL_X, COL_P, COL_ME = 0, 2048, 2560
NCOL = 2048 + 512 + 514
W_UF, W_GF, W_Z, W_XS, W_B, W_C, W_DTF, W_DTB, W_MGF, W_MGS = 0, 1024, 2048, 4096, 6144, 7168, 8192, 8224, 8256, 9280


class StopBuild(Exception):
    pass


class Tile:
    __slots__ = ("name", "t", "last_write", "reads", "dsem")

    def __init__(self, name, t):
        self.name = name
        self.t = t
        self.last_write = None
        self.reads = {}
        self.dsem = None

    def __getitem__(self, k):
        return self.t[k]


class Sched:
    ENG = ("pe", "act", "dve", "pool", "sp")

    def __init__(self, nc, arena_words):
        self.nc = nc
        self.streams = {e: [] for e in self.ENG}
        self.count = {}
        self.seen = {e: {} for e in self.ENG}
        self.sems = {}
        self.ndsem = 0
        for e in self.ENG:
            self._mksem(e)
        self.arena = nc.alloc_sbuf_tensor("arena", [128, arena_words], F32)
        self.arena_words = arena_words
        self.top = 0
        self.psum = [Tile("ps%d" % i, nc.alloc_psum_tensor("ps%d" % i, [128, 512], F32)) for i in range(8)]
        self.psi = 0
        self.reserved = []
        self.nops = 0

    def _mksem(self, key):
        self.sems[key] = self.nc.alloc_semaphore(name="s_" + key)
        self.count[key] = 0

    def alloc(self, name, free_shape, dt):
        n = int(np.prod(free_shape))
        words = (n * (2 if dt == BF16 else 4) + 3) // 4
        words = (words + 7) // 8 * 8
        assert self.top + words <= self.arena_words, (name, self.top, words)
        ap = self.arena[:, self.top:self.top + words]
        self.top += words
        if dt == BF16:
            ap = ap.bitcast(BF16)[:, 0:n]
        else:
            ap = ap[:, 0:n]
        if len(free_shape) == 2:
            ap = ap.rearrange("p (a b) -> p a b", a=free_shape[0])
        elif len(free_shape) == 3:
            ap = ap.rearrange("p (a b c) -> p a b c", a=free_shape[0], b=free_shape[1])
        return Tile(name, ap)

    def bank(self):
        while True:
            t = self.psum[self.psi]
            self.psi = (self.psi + 1) % 8
            if t not in self.reserved:
                return t

    def reserve_bank(self):
        t = self.bank()
        self.reserved.append(t)
        return t

    def unreserve(self, t):
        self.reserved.remove(t)

    def _deps(self, reads, writes):
        deps = []
        for t in reads:
            if t.last_write is not None:
                deps.append(t.last_write)
        for t in writes:
            if t.last_write is not None:
                deps.append(t.last_write)
            deps.extend(t.reads.items())
        return deps

    def _emit_waits(self, eng, deps, skip_own=False):
        need = {}
        for key, val in deps:
            if key == eng and (skip_own or not SAME_ENGINE_SYNC):
                continue
            if self.seen[eng].get(key, 0) >= val:
                continue
            if need.get(key, 0) < val:
                need[key] = val
        for key, val in need.items():
            self.seen[eng][key] = val
            sem = self.sems[key]
            self.streams[eng].append(lambda e, sem=sem, val=val: e.wait_ge(sem, val))

    def _post(self, tk, reads, writes):
        for t in reads:
            if t.reads.get(tk[0], 0) < tk[1]:
                t.reads[tk[0]] = tk[1]
        for t in writes:
            t.last_write = tk
            t.reads = {}

    def op(self, eng, fn, reads=(), writes=()):
        self._emit_waits(eng, self._deps(reads, writes), skip_own=(eng == "pe"))
        self.count[eng] += 1
        tk = (eng, self.count[eng])
        sem = self.sems[eng]
        self.streams[eng].append(lambda e, fn=fn, sem=sem: fn(e).then_inc(sem, 1))
        self._post(tk, reads, writes)
        self.nops += 1
        return tk

    def dma(self, q, out_ap, in_ap, sem_tile, reads=(), writes=()):
        self._emit_waits(q, self._deps(reads, writes))
        st = sem_tile
        if st.dsem is None:
            st.dsem = "d%d" % self.ndsem
            self.ndsem += 1
            self._mksem(st.dsem)
        key = st.dsem
        self.count[key] += 16
        tk = (key, self.count[key])
        sem = self.sems[key]
        self.streams[q].append(lambda e, o=out_ap, i=in_ap, sem=sem: e.dma_start(out=o, in_=i).then_inc(sem, 16))
        self._post(tk, reads, writes)
        return tk

    def barrier(self):
        tks = [(k, v) for k, v in self.count.items() if v > 0]
        for e in self.ENG:
            self._emit_waits(e, tks)

    def release(self, mark):
        self.barrier()
        self.top = mark

    def mm(self, out, lhsT, rhs, start, stop, reads, writes):
        return self.op("pe", lambda e: e.matmul(out, lhsT=lhsT, rhs=rhs, start=start, stop=stop), reads, writes)

    def tr(self, out, in_, ident, reads, writes):
        return self.op("pe", lambda e: e.transpose(out=out, in_=in_, identity=ident), reads, writes)

    def act(self, out, in_, func, reads, writes, bias=None, scale=None, accum=None):
        kw = {}
        if bias is not None:
            kw["bias"] = bias
        if scale is not None:
            kw["scale"] = scale
        if accum is not None:
            kw["accum_out"] = accum
        return self.op("act", lambda e: e.activation(out=out, in_=in_, func=func, **kw), reads, writes)

    def tt(self, eng, out, in0, in1, op, reads, writes):
        return self.op(eng, lambda e: e.tensor_tensor(out=out, in0=in0, in1=in1, op=op), reads, writes)

    def ts(self, eng, out, in0, s1, op0, reads, writes, s2=None, op1=None):
        if op1 is None:
            return self.op(eng, lambda e: e.tensor_scalar(out=out, in0=in0, scalar1=s1, scalar2=None, op0=op0),
                           reads, writes)
        return self.op(eng, lambda e: e.tensor_scalar(out=out, in0=in0, scalar1=s1, scalar2=s2, op0=op0, op1=op1),
                       reads, writes)

    def stt(self, eng, out, in0, scalar, in1, op0, op1, reads, writes):
        return self.op(eng, lambda e: e.scalar_tensor_tensor(out=out, in0=in0, scalar=scalar, in1=in1,
                                                              op0=op0, op1=op1), reads, writes)

    def cp(self, eng, out, in_, reads, writes):
        if eng == "act":
            return self.op("act", lambda e: e.copy(out=out, in_=in_), reads, writes)
        return self.op(eng, lambda e: e.tensor_copy(out=out, in_=in_), reads, writes)

    def memset(self, eng, ap, val, writes):
        return self.op(eng, lambda e: e.memset(ap, val), (), writes)

    def emit(self):
        nc = self.nc
        with nc.Block() as block:
            @block.tensor
            def _(e):
                for f in self.streams["pe"]:
                    f(e)

            @block.scalar
            def _(e):
                for f in self.streams["act"]:
                    f(e)

            @block.vector
            def _(e):
                for f in self.streams["dve"]:
                    f(e)

            @block.gpsimd
            def _(e):
                for f in self.streams["pool"]:
                    f(e)

            @block.sync
            def _(e):
                for f in self.streams["sp"]:
                    f(e)


ARENA_WORDS = 53000


def run_pipeline(units):
    if not units:
        return
    nst = max(len(u) for u in units)
    for step in range(len(units) + nst - 1):
        for si in range(nst):
            k = step - si
            if 0 <= k < len(units) and si < len(units[k]):
                units[k][si]()

NROW_SM = 160
PROMPT_TOK = [(COL_P + i * 128) for i in range(4)]


def build(dbg=False, stop=None):
    import os
    stop = stop or os.environ.get('KSTOP')
    nc = bass.Bass("TRN2", target_bir_lowering=False)
    S = Sched(nc, ARENA_WORDS)

    def din(name, shape, dt=F32):
        return nc.dram_tensor(name, list(shape), dt, kind="ExternalInput").ap()

    def dout(name, shape):
        return nc.dram_tensor(name, list(shape), F32, kind="ExternalOutput").ap()

    xX = din("xX", [2048, D]); xP = din("xP", [512, D]); xM = din("xM", [512, D]); xH = din("xH", [2, D])
    fl = din("fl", [1, 8]); cT = din("cT", [128, 16]); stT = din("stT", [2, 128, 2048])
    tcs = din("tcs", [2, 2048, 512], BF16); t256 = din("t256", [3, 256, 256], BF16)
    cwd = din("cw", [128, 96]); cbd = din("cb", [128, 32]); rowsm = din("rowsm", [1, NROW_SM])
    nwTd = din("nwT", [128, 16])
    rowbig = din("rowbig", [1, 4096])
    ada_w = din("ada_w", [D, 3072]); ada_b = din("ada_b", [1, 3072]); w_in = din("w_in", [D, 10304])
    fnet_w = din("fnet_w", [D, D]); wbf = din("wbf", [D, D]); wbs = din("wbs", [2048, D]); w_out = din("w_out", [D, D])
    yP = dout("yP", [512, D]); yM = dout("yM", [512, D])
    sF = dout("sF", [2, 32, 64, 128]); sB = dout("sB", [2, 32, 64, 128])
    def dump(name, tile, shape, dt=F32):
        if not dbg:
            return
        d = nc.dram_tensor("dbg_" + name, [128] + list(shape), dt, kind="ExternalOutput").ap()
        S.dma("sp", d, tile[:], tile, reads=[tile])

    def wview(w, c0, n):
        return w[:, c0:c0 + n].rearrange("(kc p) n -> p kc n", p=128)

    identb = S.alloc("identb", [128], BF16); identf = S.alloc("identf", [128], F32)
    LEf = S.alloc("LEf", [128], F32); onesf = S.alloc("onesf", [128], F32)
    LEb = S.alloc("LEb", [128], BF16); GEb = S.alloc("GEb", [128], BF16)
    GTb = S.alloc("GTb", [128], BF16); LTb = S.alloc("LTb", [128], BF16)
    epsb = S.alloc("epsb", [1], F32); oneb = S.alloc("oneb", [1], F32)
    sel2 = S.alloc("sel2", [256], F32)
    cw = S.alloc("cw", [96], F32); cb = S.alloc("cb", [32], F32)
    rows = S.alloc("rows", [NROW_SM], F32); flg = S.alloc("flg", [8], F32)
    Arow = S.alloc("Arow", [64], F32)
    g2 = S.alloc("g2", [2, D], F32)
    hTPM = S.alloc("hTPM", [8, 1026], BF16)
    fmT = S.alloc("fmT", [8, 1024], BF16)
    dtO = S.alloc("dtO", [8, 64], F32); scO = S.alloc("scO", [8, 64], F32)
    dtdec = S.alloc("dtdec", [8, 64], F32); cdA = S.alloc("cdA", [8, 64], F32)
    wini = S.alloc("wini", [64], F32)
    dabf = S.alloc("dabf", [8, 64], BF16); ebias = S.alloc("ebias", [8, 64], F32)
    NEGf = S.alloc("NEGf", [128], BF16); NEGb = S.alloc("NEGb", [128], BF16); NLTb = S.alloc("NLTb", [128], BF16)
    SmF = S.alloc("SmF", [8, 256], BF16); SmB = S.alloc("SmB", [8, 256], BF16)
    mark_pre_hTX = S.top
    hTX = S.alloc("hTX", [8, 2048], BF16)
    dtdecX = S.alloc("dtdecX", [16, 64], F32)
    mark_base = S.top

    wdt = S.alloc("wdt", [8, 64], BF16)
    wuf0 = S.alloc("wuf0", [8, 256], BF16)
    t2 = S.alloc("t2", [3, 2, 256], BF16)
    mark_base2 = S.top
    tmpf = S.alloc("tmpf", [128], F32)

    def tri(dst_bf, pattern_step, cm, cmp, also_f32=None):
        S.memset("pool", tmpf[:], 1.0, [tmpf])
        S.op("pool", lambda e: e.affine_select(out=tmpf[:], in_=tmpf[:], pattern=[[pattern_step, 128]],
                                               compare_op=cmp, fill=0.0, base=0, channel_multiplier=cm),
             [tmpf], [tmpf])
        if dst_bf is not None:
            S.cp("pool", dst_bf[:], tmpf[:], [tmpf], [dst_bf])
        if also_f32 is not None:
            S.cp("pool", also_f32[:], tmpf[:], [tmpf], [also_f32])

    tri(identb, -1, 1, ALU.is_equal, identf)
    tri(LEb, 1, -1, ALU.is_ge, LEf)
    tri(GEb, -1, 1, ALU.is_ge)
    tri(GTb, -1, 1, ALU.is_gt)
    tri(LTb, 1, -1, ALU.is_gt)
    S.ts("pool", NEGf[:], GTb[:], -30000.0, ALU.mult, [GTb], [NEGf])
    S.ts("pool", NEGb[:], LTb[:], -30000.0, ALU.mult, [LTb], [NEGb])
    S.ts("pool", NLTb[:], LTb[:], -1.0, ALU.mult, [LTb], [NLTb])
    S.memset("pool", onesf[:], 1.0, [onesf])
    S.memset("pool", epsb[:], EPS, [epsb])
    S.memset("pool", oneb[:], 1.0, [oneb])
    S.memset("pool", sel2[:], 1.0, [sel2])
    S.op("pool", lambda e: e.affine_select(out=sel2[:], in_=sel2[:], pattern=[[1, 256]], compare_op=ALU.is_ge,
                                           fill=0.0, base=0, channel_multiplier=-128), [sel2], [sel2])
    S.op("pool", lambda e: e.affine_select(out=sel2[:], in_=sel2[:], pattern=[[-1, 256]], compare_op=ALU.is_ge,
                                           fill=0.0, base=127, channel_multiplier=128), [sel2], [sel2])
    S.dma("sp", cw[:], cwd, cw, writes=[cw])
    S.dma("sp", cb[:], cbd, cb, writes=[cb])
    S.dma("sp", rows[:], rowsm.partition_broadcast(128), rows, writes=[rows])
    S.dma("sp", flg[:], fl.partition_broadcast(128), flg, writes=[flg])
    S.act(Arow[:], rows[:, 64:128], AF.Exp, [rows], [Arow])
    S.ts("dve", Arow[:], Arow[:], -1.0, ALU.mult, [Arow], [Arow])

    if stop == 'const':
        dump('rows', rows, [NROW_SM]); S.barrier(); S.emit(); return nc
    g1 = S.alloc("g1", [2, D], F32); shf = S.alloc("shf", [2, D], F32)
    mark1 = S.top
    scT = S.alloc("scT", [8, 2], F32)
    ada2 = S.alloc("ada2", [3072], F32)
    prew = S.alloc("prew", [D], F32); postw = S.alloc("postw", [D], F32)
    wst = [S.alloc("wst%d" % i, [8, 512], F32) for i in range(3)]
    S.dma("sp", scT[:].rearrange("p a b -> p (a b)"), cT, scT, writes=[scT])
    S.dma("sp", ada2[0:2, :], ada_b.partition_broadcast(2), ada2, writes=[ada2])
    S.dma("sp", prew[:], rowbig[:, 2048:3072].partition_broadcast(128), prew, writes=[prew])
    S.dma("sp", postw[:], rowbig[:, 3072:4096].partition_broadcast(128), postw, writes=[postw])
    S.act(scT[:], scT[:], AF.Silu, [scT], [scT])
    for cbk in range(6):
        wt = wst[cbk % 3]
        S.dma(("sp", "act")[cbk % 2], wt[:], wview(ada_w, cbk * 512, 512), wt, writes=[wt])
        pb = S.bank()
        for kc in range(8):
            S.mm(pb[0:2, :], scT[:, kc, :], wt[:, kc, :], kc == 0, kc == 7, [scT, wt], [pb])
        S.tt("dve", ada2[0:2, cbk * 512:(cbk + 1) * 512], pb[0:2, :], ada2[0:2, cbk * 512:(cbk + 1) * 512],
             ALU.add, [pb, ada2], [ada2])
    if stop == 'p0a':
        dump('ada2', ada2, [3072]); S.barrier(); S.emit(); return nc
    S.dma("pool", wdt[:], wview(w_in, W_DTF, 64), wdt, writes=[wdt])
    S.dma("pool", wuf0[:], wview(w_in, W_UF, 256), wuf0, writes=[wuf0])
    S.dma("sp", t2[:], t256.rearrange("w (c p) n -> p w c n", p=128), t2, writes=[t2])
    for c in range(2):
        for cbk in range(6):
            pb = S.bank()
            S.mm(pb[:, :], sel2[0:2, c * 128:(c + 1) * 128], ada2[0:2, cbk * 512:(cbk + 1) * 512], True, True,
                 [sel2, ada2], [pb])
            sl = slice((cbk % 2) * 512, (cbk % 2) * 512 + 512)
            if cbk < 2:
                S.cp("act", shf[:, c, sl], pb[:, :], [pb], [shf])
            elif cbk < 4:
                S.stt("dve", g1[:, c, sl], pb[:, :], 1.0, prew[:, sl], ALU.add, ALU.mult, [pb, prew], [g1])
            else:
                S.tt("dve", g2[:, c, sl], pb[:, :], postw[:, sl], ALU.mult, [pb, postw], [g2])
    S.release(mark1)

    if stop == 'p0':
        dump('g2', g2, [2, D]); S.barrier(); S.emit(); return nc
    xbuf = [S.alloc("xbuf%d" % i, [D], F32) for i in range(5)]
    xhalo = S.alloc("xhalo", [D], F32)
    tmpb = [S.alloc("tmpb%d" % i, [D], F32) for i in range(2)]
    hbb = [S.alloc("hbb%d" % i, [D], BF16) for i in range(2)]
    stat = [S.alloc("stat%d" % i, [4], F32) for i in range(4)]
    junk = S.alloc("junk", [D], F32)
    S.memset("pool", xhalo[:], 0.0, [xhalo])
    tiles = []
    for i in range(16):
        tiles.append((xX[i * 128:(i + 1) * 128, :], 128, 1, ("X", i)))
    for i in range(4):
        tiles.append((xP[i * 128:(i + 1) * 128, :], 128, 0, ("P", i)))
    for i in range(4):
        tiles.append((xM[i * 128:(i + 1) * 128, :], 128, 1, ("M", i)))
    tiles.append((xH, 2, 1, ("H", 0)))
    def p1_tile(ti, src, nrow, cond, kind, idx):
        xt = xhalo if kind == "H" else xbuf[ti % 5]
        st = stat[ti % 4]; tb = tmpb[ti % 2]; hb = hbb[ti % 2]
        st_ = {}

        def s0():
            S.dma("sp", xt[0:nrow, :], src, xt, writes=[xt])

        def s0b():
            S.act(junk[:], xt[:], AF.Square, [xt], [junk, st], accum=st[:, 0:1])

        def s0c():
            S.act(st[:, 1:2], st[:, 0:1], AF.Ln, [st, epsb], [st], bias=epsb[:, 0:1], scale=1.0 / D)

        def s0d():
            S.act(st[:, 2:3], st[:, 1:2], AF.Exp, [st], [st], scale=-0.5)

        def s1():
            S.stt("dve", tb[:], xt[:], st[:, 2:3], g1[:, cond, :], ALU.mult, ALU.mult, [xt, st, g1], [tb])

        def s1b():
            S.tt("dve", hb[:], tb[:], shf[:, cond, :], ALU.add, [tb, shf], [hb])

        def s2():
            pb = S.bank()
            st_["pb"] = pb
            pbb = pb[:, :].bitcast(BF16)
            for kc in range(8):
                S.tr(pbb[:, kc * 128:(kc + 1) * 128], hb[:, kc * 128:(kc + 1) * 128], identb[:], [hb, identb], [pb])

        def s3():
            pb = st_["pb"]
            pv = pb[:, :].bitcast(BF16).rearrange("p (a b) -> p a b", a=8)
            if kind == "X":
                S.cp("act", hTX[:, :, idx * 128:(idx + 1) * 128], pv, [pb], [hTX])
            elif kind == "P":
                S.cp("act", hTPM[:, :, idx * 128:(idx + 1) * 128], pv, [pb], [hTPM])
            elif kind == "M":
                S.cp("act", hTPM[:, :, 513 + idx * 128:513 + (idx + 1) * 128], pv, [pb], [hTPM])
            else:
                S.ts("dve", hTPM[:, :, 512:513], pv[:, :, 0:1], flg[:, 0:1], ALU.mult, [pb, flg], [hTPM])
                S.ts("dve", hTPM[:, :, 1025:1026], pv[:, :, 1:2], flg[:, 1:2], ALU.mult, [pb, flg], [hTPM])

        return [s0, s0b, s0c, s0d, s1, s1b, s2, s3]

    run_pipeline([p1_tile(ti, src, nrow, cond, kind, idx) for ti, (src, nrow, cond, (kind, idx)) in enumerate(tiles)])
    S.release(mark_base2)
    dump("hTX", hTX, [8, 2048], BF16)
    dump("hTPM", hTPM, [8, 1026], BF16)
    dump("g2", g2, [2, D])
    if stop == 'p1':
        S.barrier(); S.emit(); return nc
    mark_dt = mark_base
    t_dtr = S.alloc("t_dtr", [8, 64], F32); t_e = S.alloc("t_e", [8, 64], F32); t_dt = S.alloc("t_dt", [8, 64], F32)
    t_da = S.alloc("t_da", [8, 64], F32); t_tot = S.alloc("t_tot", [8, 64], F32); t_x = S.alloc("t_x", [8, 64], F32)
    t_dec = S.alloc("t_dec", [8, 64], F32); t_y = S.alloc("t_y", [8, 64], F32)
    totX = S.alloc("totX", [16, 64], F32); Ppre = S.alloc("Ppre", [17, 64], F32)
    omg = S.alloc("omg", [16, 64], F32); t_o = S.alloc("t_o", [16, 32], F32); t_o2 = S.alloc("t_o2", [16, 32], F32)

    def chunk_cols(ci):
        if ci < 4:
            return hTPM, ci * 128
        if ci < 8:
            return hTPM, 513 + (ci - 4) * 128
        return hTX, (ci - 8) * 128

    def v8(ap):
        return ap.rearrange("p (a b) -> p a b", a=8)

    def dtA(bi):
        pb = S.bank()
        for j in range(8):
            ht, c0 = chunk_cols(bi * 8 + j)
            for kc in range(8):
                S.mm(pb[:, j * 64:(j + 1) * 64], ht[:, kc, c0:c0 + 128], wdt[:, kc, :], kc == 0, kc == 7, [ht, wdt], [pb])
        S.tt("dve", t_dtr[:], v8(pb[:, :]), rows[:, 0:64].unsqueeze(1).to_broadcast([128, 8, 64]), ALU.add,
             [pb, rows], [t_dtr])
        S.act(t_e[:], t_dtr[:], AF.Exp, [t_dtr], [t_e])
        S.act(t_dt[:], t_e[:], AF.Ln, [t_e, oneb], [t_dt], bias=oneb[:, 0:1])
        S.tt("pool", t_da[:], t_dt[:], Arow[:].unsqueeze(1).to_broadcast([128, 8, 64]), ALU.mult, [t_dt, Arow], [t_da])

    def dtB(bi):
        pi = S.bank(); po = S.bank()
        da_flat = t_da[:].rearrange("p a b -> p (a b)")
        S.mm(pi[:, :], LEf[:], da_flat, True, True, [LEf, t_da], [pi])
        S.mm(po[:, :], onesf[:], da_flat, True, True, [onesf, t_da], [po])
        piv = v8(pi[:, :]); pov = v8(po[:, :])
        S.cp("act", t_tot[:], pov, [po], [t_tot])
        if bi == 0:
            S.act(cdA[:], t_tot[:], AF.Exp, [t_tot], [cdA])
        S.tt("dve", t_x[:, :, 0:32], t_tot[:, :, 0:32], piv[:, :, 0:32], ALU.subtract, [t_tot, pi], [t_x])
        S.tt("dve", t_x[:, :, 32:64], piv[:, :, 32:64], t_da[:, :, 32:64], ALU.subtract, [pi, t_da], [t_x])
        S.act(t_dec[:], t_x[:], AF.Exp, [t_x], [t_dec])
        if bi == 0:
            S.tt("pool", dtdec[:], t_dt[:], t_dec[:], ALU.mult, [t_dt, t_dec], [dtdec])
        else:
            S.tt("pool", dtdecX[:, (bi - 1) * 8:bi * 8, :], t_dt[:], t_dec[:], ALU.mult, [t_dt, t_dec], [dtdecX])
        if bi > 0:
            S.cp("pool", totX[:, (bi - 1) * 8:bi * 8, :], t_tot[:], [t_tot], [totX])
        if bi == 0:
            S.cp("pool", dtO[:], t_dt[:], [t_dt], [dtO])
            S.act(scO[:, :, 0:32], piv[:, :, 0:32], AF.Exp, [pi], [scO])
            dav = dabf[:].rearrange("p c (g d h) -> p c g d h", g=8, d=2, h=4)
            ebv = ebias[:].rearrange("p c (g d h) -> p c g d h", g=8, d=2, h=4)
            for d_ in range(2):
                S.cp("pool", dav[:, :, :, d_, :], t_da[:, :, d_ * 32:(d_ + 1) * 32].rearrange("p c (g h) -> p c g h", g=8),
                     [t_da], [dabf])
            for d_ in range(2):
                S.cp("pool", t_y[:, :, d_ * 32:(d_ + 1) * 32].rearrange("p c (g h) -> p c g h", g=8), dav[:, :, :, d_, :],
                     [dabf], [t_y])
            pr = S.bank()
            S.mm(pr[:, :], LEf[:], t_y[:].rearrange("p a b -> p (a b)"), True, True, [LEf, t_y], [pr])
            prv = v8(pr[:, :])
            S.ts("dve", ebv[:, :, :, 0, :], prv[:, :, 0:32].rearrange("p c (g h) -> p c g h", g=8), -1.0, ALU.mult, [pr], [ebias])
            S.tt("dve", ebv[:, :, :, 1, :], prv[:, :, 32:64].rearrange("p c (g h) -> p c g h", g=8),
                 t_y[:, :, 32:64].rearrange("p c (g h) -> p c g h", g=8), ALU.subtract, [pr, t_y], [ebias])
            S.tt("dve", t_y[:, :, 32:64], t_tot[:, :, 32:64], t_x[:, :, 32:64], ALU.subtract, [t_tot, t_x], [t_y])
            S.act(scO[:, :, 32:64], t_y[:, :, 32:64], AF.Exp, [t_y], [scO])

    def dtOmega():
        S.memset("pool", Ppre[:, 0:1, :], 0.0, [Ppre])
        for k in range(16):
            S.tt("dve", Ppre[:, k + 1, :], Ppre[:, k, :], totX[:, k, :], ALU.add, [Ppre, totX], [Ppre])
        S.memset("pool", omg[:], 0.0, [omg])
        S.memset("pool", wini[:], 0.0, [wini])
        for j in range(1, 4):
            n = 4 * j
            S.tt("dve", t_o[:, 0:n, :], Ppre[:, n:n + 1, 0:32].to_broadcast([128, n, 32]), Ppre[:, 1:n + 1, 0:32],
                 ALU.subtract, [Ppre], [t_o])
            S.act(t_o2[:, 0:n, :], t_o[:, 0:n, :], AF.Exp, [t_o], [t_o2])
            S.stt("dve", omg[:, 0:n, 0:32], t_o2[:, 0:n, :], flg[:, 2 + j:3 + j], omg[:, 0:n, 0:32], ALU.mult, ALU.add,
                  [t_o2, flg, omg], [omg])
        for j in range(0, 3):
            b0 = 4 * (j + 1); n = 16 - b0
            S.tt("dve", t_o[:, 0:n, :], Ppre[:, b0:16, 32:64], Ppre[:, b0:b0 + 1, 32:64].to_broadcast([128, n, 32]),
                 ALU.subtract, [Ppre], [t_o])
            S.act(t_o2[:, 0:n, :], t_o[:, 0:n, :], AF.Exp, [t_o], [t_o2])
            S.stt("dve", omg[:, b0:16, 32:64], t_o2[:, 0:n, :], flg[:, 2 + j:3 + j], omg[:, b0:16, 32:64], ALU.mult, ALU.add,
                  [t_o2, flg, omg], [omg])
        for j in range(4):
            S.act(t_o[:, 0, :], Ppre[:, 4 * j, 0:32], AF.Exp, [Ppre], [t_o])
            S.stt("dve", wini[:, 0:32], t_o[:, 0, :], flg[:, 2 + j:3 + j], wini[:, 0:32], ALU.mult, ALU.add, [t_o, flg, wini], [wini])
            S.tt("dve", t_o[:, 1, :], Ppre[:, 16, 32:64], Ppre[:, 4 * (j + 1), 32:64], ALU.subtract, [Ppre], [t_o])
            S.act(t_o[:, 2, :], t_o[:, 1, :], AF.Exp, [t_o], [t_o])
            S.stt("dve", wini[:, 32:64], t_o[:, 2, :], flg[:, 2 + j:3 + j], wini[:, 32:64], ALU.mult, ALU.add, [t_o, flg, wini], [wini])
        S.tt("dve", dtdecX[:], dtdecX[:], omg[:], ALU.mult, [dtdecX, omg], [dtdecX])

    if stop == 'dt':
        dtA(0); dtB(0); dtA(1); dtB(1); dtA(2); dtB(2); dtOmega()
        S.barrier(); S.emit(); return nc

    tc_sb = S.alloc("tc_sb", [16, 512], BF16); ts_sb = S.alloc("ts_sb", [16, 512], BF16)
    S.dma("sp", tc_sb[:], tcs[0].rearrange("(lt p) n -> p lt n", p=128), tc_sb, writes=[tc_sb])
    S.dma("sp", ts_sb[:], tcs[1].rearrange("(lt p) n -> p lt n", p=128), ts_sb, writes=[ts_sb])
    wuf = [wuf0, S.alloc("wuf1", [8, 256], BF16)]
    Ug = [S.alloc("Ug%d" % i, [20, 256], BF16) for i in range(1)]
    Vt = S.alloc("Vt", [2, 2, 1024], BF16)
    ev = [0]

    def evac(out, in_, reads, writes):
        ev[0] += 1
        S.cp("act" if ev[0] % 2 == 0 else "dve", out, in_, reads, writes)

    dtA(0)
    for g in range(4):
        w = wuf[g % 2]; U = Ug[0]
        if g > 0:
            S.dma("pool", w[:], wview(w_in, W_UF + g * 256, 256), w, writes=[w])
        for tp in range(10):
            pb = S.bank()
            for j in range(2):
                ti = tp * 2 + j
                ht, c0 = (hTX, ti * 128) if ti < 16 else (hTPM, (ti - 16) * 128)
                for kc in range(8):
                    S.mm(pb[:, j * 256:(j + 1) * 256], ht[:, kc, c0:c0 + 128], w[:, kc, :], kc == 0, kc == 7,
                         [ht, w], [pb])
            evac(U[:, tp * 2:tp * 2 + 2, :], pb[:, :].rearrange("p (a b) -> p a b", a=2), [pb], [U])
        if g == 0:
            dtB(0); dtA(1)
        if g == 1:
            dtB(2)
        for cs, tab in ((0, tc_sb), (1, ts_sb)):
            for wc in range(2):
                pb = S.bank()
                for lt in range(16):
                    S.mm(pb[:, :], U[:, lt, wc * 128:(wc + 1) * 128], tab[:, lt, :], lt == 0, lt == 15, [U, tab], [pb])
                evac(Vt[:, cs, wc, 512:1024], pb[:, :], [pb], [Vt])
        if g == 0:
            dtB(1); dtA(2)
        if g == 1:
            dtOmega()
        for cs in range(2):
            for wc in range(2):
                pb = S.bank()
                for sq in range(2):
                    for lt in range(2):
                        S.mm(pb[:, sq * 256:(sq + 1) * 256], U[:, 16 + sq * 2 + lt, wc * 128:(wc + 1) * 128],
                             t2[:, 0 if cs == 0 else 2, lt, :], lt == 0, lt == 1, [U, t2], [pb])
                evac(Vt[:, cs, wc, 0:512], pb[:, :], [pb], [Vt])
        for tb in range(2):
            for wpc in range(2):
                pb = S.bank()
                k = 0
                for cs in range(2):
                    for wc in range(2):
                        S.mm(pb[:, :], t2[:, 0 if cs == 0 else 1, wc, wpc * 128:(wpc + 1) * 128],
                             Vt[:, cs, wc, tb * 512:(tb + 1) * 512], k == 0, k == 3, [t2, Vt], [pb])
                        k += 1
                evac(fmT[:, g * 2 + wpc, tb * 512:(tb + 1) * 512], pb[:, :], [pb], [fmT])
    dump("fmT", fmT, [8, 1024], BF16)
    dump("dtO", dtO, [8, 64]); dump("scO", scO, [8, 64])
    dump("dtdec", dtdec, [8, 64]); dump("cdA", cdA, [8, 64])
    S.release(mark_dt)
    if stop == 'fourier':
        S.barrier(); S.emit(); return nc
    accs = [S.alloc("acc%d" % i, [512], F32) for i in range(4)]
    cvk = [0]

    def conv_unit(wt, wcol0, cidx, ht, hc0, n_in, o_rel0, n_out, left_from, right_to, dest_ap, dest_tile):
        st_ = {}

        def s0():
            pb = S.bank()
            st_["pb"] = pb
            for kc in range(8):
                S.mm(pb[:, 0:n_in], wt[:, kc, wcol0:wcol0 + 128], ht[:, kc, hc0:hc0 + n_in], kc == 0, kc == 7, [wt, ht], [pb])

        def s1():
            pb = st_["pb"]
            cvk[0] += 1
            acc = accs[cvk[0] % len(accs)]
            st_["acc"] = acc
            S.act(acc[:, 0:n_out], pb[:, o_rel0:o_rel0 + n_out], AF.Identity, [pb, cw, cb], [acc],
                  bias=cb[:, cidx:cidx + 1], scale=cw[:, cidx * 3 + 1:cidx * 3 + 2])
        def s1b():
            pb = st_["pb"]; acc = st_["acc"]
            a, b = max(o_rel0, left_from), o_rel0 + n_out
            S.stt("dve", acc[:, a - o_rel0:b - o_rel0], pb[:, a - 1:b - 1], cw[:, cidx * 3:cidx * 3 + 1],
                  acc[:, a - o_rel0:b - o_rel0], ALU.mult, ALU.add, [pb, cw, acc], [acc])
            a, b = o_rel0, min(o_rel0 + n_out, right_to)
            S.stt("dve", acc[:, a - o_rel0:b - o_rel0], pb[:, a + 1:b + 1], cw[:, cidx * 3 + 2:cidx * 3 + 3],
                  acc[:, a - o_rel0:b - o_rel0], ALU.mult, ALU.add, [pb, cw, acc], [acc])

        def s2():
            acc = st_["acc"]
            S.act(dest_ap, acc[:, 0:n_out], AF.Silu, [acc], [dest_tile])

        return [s0, s1, s1b, s2]

    def h4(ap):
        return ap.rearrange("p (h q) -> p h q", h=4)

    def bc4(ap):
        return ap.unsqueeze(2).to_broadcast([128, 4, 64])

    wgc = [S.alloc("wgc%d" % i, [8, 384], BF16) for i in range(2)]
    xsTX = S.alloc("xsTX", [2, 2048], BF16); BTX = S.alloc("BTX", [2048], BF16)
    xbs = [S.alloc("xbs%d" % i, [384], BF16) for i in range(3)]
    xdd = [S.alloc("xdd%d" % i, [256], BF16) for i in range(6)]
    Sst = [S.alloc("Sst%d" % i, [256], F32) for i in range(2)]
    Stmp = [S.alloc("Stmp%d" % i, [256], F32) for i in range(2)]
    segs = [(0, 410), (410, 820), (820, 1230), (1230, 1640), (1640, 2048)]
    def load_wgc(g):
        wt = wgc[g % 2]
        S.dma("pool", wt[:, :, 0:256], wview(w_in, W_XS + g * 256, 256), wt, writes=[wt])
        S.dma("pool", wt[:, :, 256:384], wview(w_in, W_B + g * 128, 128), wt, writes=[wt])

    load_wgc(0)
    for g in range(8):
        wt = wgc[g % 2]
        units = []
        for (o0, o1) in segs:
            i0, i1 = max(o0 - 1, 0), min(o1 + 1, 2048)
            for cc in range(3):
                cidx = (2 * g + cc) if cc < 2 else 16 + g
                dest = xsTX[:, cc, o0:o1] if cc < 2 else BTX[:, o0:o1]
                units.append(conv_unit(wt, cc * 128, cidx, hTX, i0, i1 - i0, o0 - i0, o1 - o0, 1 if i0 == 0 else 0,
                                       2047 - i0, dest, xsTX if cc < 2 else BTX))
        run_pipeline(units)
        if g + 1 < 8:
            load_wgc(g + 1)
        psF = S.reserve_bank(); psB = S.reserve_bank()
        def ctx_chunk(c):
            st_ = {}

            def s0():
                pb = S.bank(); pbb = pb[:, :].bitcast(BF16)
                st_["pb"] = pb
                for cc in range(2):
                    S.tr(pbb[:, cc * 128:(cc + 1) * 128], xsTX[:, cc, c * 128:(c + 1) * 128], identb[:], [xsTX, identb], [pb])
                S.tr(pbb[:, 256:384], BTX[:, c * 128:(c + 1) * 128], identb[:], [BTX, identb], [pb])

            def s1():
                pb = st_["pb"]; pbb = pb[:, :].bitcast(BF16)
                xb = xbs[c % len(xbs)]
                S.cp("act", xb[:], pbb[:, 0:384], [pb], [xb])
                for d in range(2):
                    if (d == 0 and c >= 12) or (d == 1 and c < 4):
                        continue
                    col0 = d * 32 + g * 4
                    xd = xdd[(2 * c + d) % len(xdd)]
                    S.tt("dve", h4(xd[:]), h4(xb[:, 0:256]), bc4(dtdecX[:, c, col0:col0 + 4]),
                         ALU.mult, [xb, dtdecX], [xd])

            def s2():
                xb = xbs[c % len(xbs)]
                for d in range(2):
                    if (d == 0 and c >= 12) or (d == 1 and c < 4):
                        continue
                    xd = xdd[(2 * c + d) % len(xdd)]
                    pacc = psF if d == 0 else psB
                    first = (c == 0) if d == 0 else (c == 4)
                    last = (c == 11) if d == 0 else (c == 15)
                    S.mm(pacc[:, 0:256], xb[:, 256:384], xd[:], first, last, [xb, xd], [pacc])

            return [s0, s1, s2]

        run_pipeline([ctx_chunk(c) for c in range(16)])
        for d in range(2):
            Sm = SmF if d == 0 else SmB
            pacc = psF if d == 0 else psB
            S.dma("sp", Sst[d][:], stT[d][:, g * 256:(g + 1) * 256], Sst[d], writes=[Sst[d]])
            S.tt("dve", h4(Stmp[d][:]), h4(Sst[d][:]), bc4(wini[:, d * 32 + g * 4:d * 32 + g * 4 + 4]), ALU.mult,
                 [Sst[d], wini], [Stmp[d]])
            S.tt("dve", Sm[:, g, :], Stmp[d][:], pacc[:, 0:256], ALU.add, [Stmp[d], pacc], [Sm])
        S.unreserve(psF); S.unreserve(psB)
    dump("SmF", SmF, [8, 256], BF16); dump("SmB", SmB, [8, 256], BF16)
    S.release(mark_pre_hTX)
    if stop == 'ctx':
        S.barrier(); S.emit(); return nc
    ynT = S.alloc("ynT", [16, 1024], BF16)
    mark_m = S.top
    accs = [S.alloc("accm%d" % i, [264], F32) for i in range(4)]
    wg2 = [S.alloc("wg2_%d" % i, [8, 768], BF16) for i in range(2)]
    xsT = S.alloc("xsT", [2, 1024], BF16); BT = S.alloc("BT", [1024], BF16); CT = S.alloc("CT", [1024], BF16)
    zs = S.alloc("zs", [8, 256], BF16)
    xb8 = S.alloc("xb8", [8, 384], BF16)
    xdt = S.alloc("xdt", [8, 2, 256], BF16)
    SinB = S.alloc("SinB", [8, 2, 256], BF16)
    xbc_ = [Tile("xb8_%d" % c, xb8[:, c, :]) for c in range(8)]
    xdc_ = [Tile("xdt_%d" % c, xdt[:, c, :, :]) for c in range(8)]
    zsc_ = [Tile("zs_%d" % c, zs[:, c, :]) for c in range(8)]
    sic_ = [[Tile("sin_%d_%d" % (c, d), SinB[:, c, d, :]) for d in range(2)] for c in range(8)]
    xdd2 = [S.alloc("xdd2_%d" % i, [256], BF16) for i in range(6)]
    Sa = [S.alloc("Sa%d" % i, [256], F32) for i in range(6)]
    Stm = [S.alloc("Stm%d" % i, [256], F32) for i in range(4)]
    stg = [S.alloc("stg%d" % i, [512], F32) for i in range(1)]
    DAt = [S.alloc("DAt%d" % i, [8, 128], BF16) for i in range(2)]
    Ee = [S.alloc("E%d" % i, [8, 128], BF16) for i in range(2)]
    Xb = [S.alloc("Xb%d" % i, [8, 128], F32) for i in range(2)]
    Gs = [S.alloc("Gs%d" % i, [128], BF16) for i in range(4)]
    Ww = [S.alloc("W%d" % i, [8, 128], BF16) for i in range(2)]
    yA = [S.alloc("yA%d" % i, [256], F32) for i in range(4)]
    yH = [[Tile("yA%d_%d" % (i, h), yA[i][:, h * 64:(h + 1) * 64]) for h in range(4)] for i in range(4)]
    Dm2 = [S.alloc("Dm%d" % i, [4, 128], BF16) for i in range(2)]
    ynk = [S.alloc("ynk%d" % i, [256], BF16) for i in range(2)]
    jk = S.alloc("jk", [256], F32)
    st2 = [S.alloc("st2_%d" % i, [4], F32) for i in range(2)]
    rr = {"sa": 0, "xd": 0, "tm": 0, "stg": 0, "ew": 0}

    def nxt(key, lst):
        rr[key] += 1
        return lst[rr[key] % len(lst)]

    def ew():
        return "dve"

    def load_wg2(g):
        wt = wg2[g % 2]
        S.dma("pool", wt[:, :, 0:256], wview(w_in, W_XS + g * 256, 256), wt, writes=[wt])
        S.dma("pool", wt[:, :, 256:384], wview(w_in, W_B + g * 128, 128), wt, writes=[wt])
        S.dma("pool", wt[:, :, 384:512], wview(w_in, W_C + g * 128, 128), wt, writes=[wt])
        S.dma("pool", wt[:, :, 512:768], wview(w_in, W_Z + g * 256, 256), wt, writes=[wt])

    def front_units(g):
        wt = wg2[g % 2]
        units = []
        for cc in range(4):
            cidx = (2 * g + cc) if cc < 2 else (16 + g if cc == 2 else 24 + g)
            dtile = xsT if cc < 2 else (BT if cc == 2 else CT)
            for u in range(4):
                dest = dtile[:, cc, u * 256:(u + 1) * 256] if cc < 2 else dtile[:, u * 256:(u + 1) * 256]
                if u < 2:
                    units.append(conv_unit(wt, cc * 128, cidx, hTPM, u * 256, 256, 0, 256, 1, 255, dest, dtile))
                else:
                    units.append(conv_unit(wt, cc * 128, cidx, hTPM, 512 + (u - 2) * 256, 258, 1, 256, 0, 10 ** 6,
                                           dest, dtile))

        def z_unit(cp_):
            st_ = {}

            def s0():
                pb = S.bank()
                st_["pb"] = pb
                for j in range(2):
                    ht, c0 = chunk_cols(cp_ * 2 + j)
                    for kc in range(8):
                        S.mm(pb[:, j * 256:(j + 1) * 256], ht[:, kc, c0:c0 + 128], wt[:, kc, 512:768], kc == 0, kc == 7,
                             [ht, wt], [pb])

            def s1():
                pb = st_["pb"]
                S.act(zs[:, cp_ * 2:cp_ * 2 + 2, :], pb[:, :].rearrange("p (a b) -> p a b", a=2), AF.Silu, [pb],
                      [zsc_[cp_ * 2], zsc_[cp_ * 2 + 1]])

            return [s0, s1]

        for cp_ in range(4):
            units.append(z_unit(cp_))
        return units

    load_wg2(0)
    run_pipeline(front_units(0))
    for g in range(8):
        if g + 1 < 8:
            load_wg2(g + 1)
        Dm = Dm2[g % 2]
        for h in range(4):
            S.ts("pool", Dm[:, h, :], identb[:], rows[:, 128 + g * 4 + h:129 + g * 4 + h], ALU.mult, [identb, rows], [Dm])

        def tr_chunk(c):
            st_ = {}

            def s0():
                pb = S.bank(); pbb = pb[:, :].bitcast(BF16)
                st_["pb"] = pb
                for cc in range(2):
                    S.tr(pbb[:, cc * 128:(cc + 1) * 128], xsT[:, cc, c * 128:(c + 1) * 128], identb[:], [xsT, identb], [pb])
                S.tr(pbb[:, 256:384], BT[:, c * 128:(c + 1) * 128], identb[:], [BT, identb], [pb])

            def s1():
                pb = st_["pb"]; pbb = pb[:, :].bitcast(BF16)
                S.cp("act", xbc_[c][:], pbb[:, 0:384], [pb], [xbc_[c]])
                for d in range(2):
                    S.tt("pool", h4(xdc_[c][:, d, :]), h4(xbc_[c][:, 0:256]), bc4(dtO[:, c, d * 32 + g * 4:d * 32 + g * 4 + 4]),
                         ALU.mult, [xbc_[c], dtO], [xdc_[c]])

            return [s0, s1]

        run_pipeline([tr_chunk(c) for c in range(8)])

        def mk_xd(c, d):
            xd = nxt("xd", xdd2)
            col0 = d * 32 + g * 4
            S.tt(ew(), h4(xd[:]), h4(xbc_[c][:, 0:256]), bc4(dtdec[:, c, col0:col0 + 4]), ALU.mult, [xbc_[c], dtdec], [xd])
            return xd

        def sc_mm(c, d):
            xd = mk_xd(c, d)
            pc = S.bank()
            S.mm(pc[:, 0:256], xbc_[c][:, 256:384], xd[:], True, True, [xbc_[c], xd], [pc])
            return pc

        def upd(Sold_ap, Sold_tile, c, d, pc):
            col0 = d * 32 + g * 4
            tm = nxt("tm", Stm); Snew = nxt("sa", Sa)
            S.tt(ew(), h4(tm[:]), h4(Sold_ap), bc4(cdA[:, c, col0:col0 + 4]), ALU.mult, [Sold_tile, cdA], [tm])
            S.tt("dve", Snew[:], tm[:], pc[:, 0:256], ALU.add, [tm, pc], [Snew])
            return Snew

        for sq in range(2):
            c0, c1 = 2 * sq, 2 * sq + 1
            pb = S.bank()
            for d, (ca, cb_) in enumerate(((c0, c1), (c1, c0))):
                col0 = d * 32 + g * 4
                xa = mk_xd(ca, d); xb_ = mk_xd(cb_, d); xa2 = nxt("xd", xdd2)
                S.tt(ew(), h4(xa2[:]), h4(xa[:]), bc4(cdA[:, cb_, col0:col0 + 4]), ALU.mult, [xa, cdA], [xa2])
                pc = S.bank()
                S.mm(pc[:, 0:256], xbc_[ca][:, 256:384], xa[:], True, True, [xbc_[ca], xa], [pc])
                S.cp("act", sic_[cb_][d][:], pc[:, 0:256], [pc], [sic_[cb_][d]])
                for hh in range(2):
                    osl = slice((d * 2 + hh) * 128, (d * 2 + hh + 1) * 128)
                    S.mm(pb[:, osl], xa2[:, hh * 128:(hh + 1) * 128], xbc_[ca][:, 256:384], True, False, [xa2, xbc_[ca]], [pb])
                    S.mm(pb[:, osl], xb_[:, hh * 128:(hh + 1) * 128], xbc_[cb_][:, 256:384], False, True, [xb_, xbc_[cb_]], [pb])
            sg = nxt("stg", stg)
            S.cp("act", sg[:], pb[:, :], [pb], [sg])
            for d, dst in enumerate((sF, sB)):
                for hh in range(2):
                    h0 = g * 4 + hh * 2
                    S.dma("sp", dst[sq, h0:h0 + 2].rearrange("h p n -> (h p) n"),
                          sg[:, (d * 2 + hh) * 128:(d * 2 + hh + 1) * 128], sg, reads=[sg])
        curs = [(SmF[:, g, :], SmF), (SmB[:, g, :], SmB)]
        orders = [[4, 5, 6, 7], [7, 6, 5, 4]]
        pcs = {}
        for k in range(3):
            pbk = S.bank()
            for d in range(2):
                c = orders[d][k]
                xd = mk_xd(c, d)
                S.mm(pbk[:, d * 256:(d + 1) * 256], xbc_[c][:, 256:384], xd[:], True, True, [xbc_[c], xd], [pbk])
                pcs[(k, d)] = pbk
        for k in range(4):
            for d in range(2):
                c = orders[d][k]
                cur_ap, cur_t = curs[d]
                S.cp("act", sic_[c][d][:], cur_ap, [cur_t], [sic_[c][d]])
                if k < 3:
                    col0 = d * 32 + g * 4
                    tm = nxt("tm", Stm); Snew = nxt("sa", Sa)
                    S.tt("dve", h4(tm[:]), h4(cur_ap), bc4(cdA[:, c, col0:col0 + 4]), ALU.mult, [cur_t, cdA], [tm])
                    S.tt("dve", Snew[:], tm[:], pcs[(k, d)][:, d * 256:(d + 1) * 256], ALU.add, [tm, pcs[(k, d)]], [Snew])
                    curs[d] = (Snew[:], Snew)

        def out_chunk(c):
            tc0 = c * 128
            has_f = (c % 2 == 1) if c < 4 else True
            has_b = (c % 2 == 0) if c < 4 else True
            i2 = c % 2
            st_ = {}

            bkA = [S.psum[0 + i2], S.psum[2 + i2]]; bkC = S.psum[4 + i2]; bkD = S.psum[6 + i2]

            def sA():
                pg = bkC
                st_["pg"] = pg
                S.mm(pg[:, 0:128], BT[:, tc0:tc0 + 128], CT[:, tc0:tc0 + 128], True, True, [BT, CT], [pg])
                S.cp("act", Gs[c % 4][:], pg[:, 0:128], [pg], [Gs[c % 4]])
                da_ = DAt[i2]
                S.cp("act", da_[:], dabf[:, c, g * 8:(g + 1) * 8].unsqueeze(2).to_broadcast([128, 8, 128]), [dabf], [da_])
                st_["ps"] = []
                for d in range(2):
                    U2 = LEb if d == 0 else NLTb
                    NG = NEGf if d == 0 else NEGb
                    ps = bkA[d]
                    st_["ps"].append(ps)
                    for h in range(4):
                        S.mm(ps[:, h * 128:(h + 1) * 128], da_[:, d * 4 + h, :], U2[:], True, False, [da_, U2], [ps])
                        S.mm(ps[:, h * 128:(h + 1) * 128], identb[:], NG[:], False, True, [identb, NG], [ps])

            def sA2():
                x_ = Xb[i2]
                for d in range(2):
                    ps = st_["ps"][d]
                    col = g * 8 + d * 4
                    S.tt("dve", x_[:, d * 4:(d + 1) * 4, :], ps[:, :].rearrange("p (h q) -> p h q", h=4),
                         ebias[:, c, col:col + 4].unsqueeze(2).to_broadcast([128, 4, 128]), ALU.add, [ps, ebias], [x_])

            def sB():
                S.act(Ee[i2][:], Xb[i2][:], AF.Exp, [Xb[i2]], [Ee[i2]])

            def sB2():
                S.tt("dve", Ww[i2][:, 0:4, :], Ee[i2][:, 0:4, :], Gs[c % 4][:].unsqueeze(1).to_broadcast([128, 4, 128]), ALU.mult,
                     [Ee[i2], Gs[c % 4]], [Ww[i2]])
                S.tt("pool", Ww[i2][:, 4:8, :], Ee[i2][:, 4:8, :], Gs[c % 4][:].unsqueeze(1).to_broadcast([128, 4, 128]), ALU.mult,
                     [Ee[i2], Gs[c % 4]], [Ww[i2]])

            def sC():
                w_ = Ww[i2]
                py = bkC
                pyo = 128
                for h in range(4):
                    S.mm(py[:, pyo + h * 64:pyo + (h + 1) * 64], w_[:, h, :], xdc_[c][:, 0, h * 64:(h + 1) * 64], True, False,
                         [w_, xdc_[c]], [py])
                    S.mm(py[:, pyo + h * 64:pyo + (h + 1) * 64], w_[:, 4 + h, :], xdc_[c][:, 1, h * 64:(h + 1) * 64], False, False,
                         [w_, xdc_[c]], [py])
                    S.mm(py[:, pyo + h * 64:pyo + (h + 1) * 64], Dm[:, h, :], xbc_[c][:, h * 64:(h + 1) * 64], False, True,
                         [Dm, xbc_[c]], [py])
                po = bkD
                if has_f:
                    S.mm(po[:, 0:256], CT[:, tc0:tc0 + 128], sic_[c][0][:], True, True, [CT, sic_[c][0]], [po])
                if has_b:
                    S.mm(po[:, 256:512], CT[:, tc0:tc0 + 128], sic_[c][1][:], True, True, [CT, sic_[c][1]], [po])
                y = yA[c % 4]; yh = yH[c % 4]
                S.cp("act", y[:], py[:, pyo:pyo + 256], [py], [y] + yh)

            def sC2():
                po = bkD
                y = yA[c % 4]; yh = yH[c % 4]
                for (has, off, sc0) in ((has_f, 0, g * 4), (has_b, 256, 32 + g * 4)):
                    if not has:
                        continue
                    for h in range(4):
                        hs = slice(h * 64, (h + 1) * 64)
                        S.stt("dve", y[:, hs], po[:, off + h * 64:off + (h + 1) * 64], scO[:, c, sc0 + h:sc0 + h + 1],
                              y[:, hs], ALU.mult, ALU.add, [po, scO, yh[h]], [yh[h]])

            def sD():
                y = yA[c % 4]
                S.tt("pool", y[:], y[:], zsc_[c][:], ALU.mult, yH[c % 4] + [zsc_[c]], [y] + yH[c % 4])

            def sD2():
                y = yA[c % 4]
                st = st2[i2]
                S.act(jk[:], y[:], AF.Square, [y] + yH[c % 4], [jk, st], accum=st[:, 0:1])
                S.act(st[:, 1:2], st[:, 0:1], AF.Ln, [st, epsb], [st], bias=epsb[:, 0:1], scale=1.0 / 256)
                S.act(st[:, 2:3], st[:, 1:2], AF.Exp, [st], [st], scale=-0.5)
                S.act(ynk[i2][:], y[:], AF.Copy, [y, st] + yH[c % 4], [ynk[i2]], scale=st[:, 2:3])

            def sE():
                pt = bkC; ptb = pt[:, :].bitcast(BF16)
                for j in range(2):
                    S.tr(ptb[:, 768 + j * 128:768 + (j + 1) * 128], ynk[i2][:, j * 128:(j + 1) * 128], identb[:],
                         [ynk[i2], identb], [pt])
                S.cp("act", ynT[:, g * 2:g * 2 + 2, tc0:tc0 + 128], ptb[:, 768:1024].rearrange("p (a b) -> p a b", a=2),
                     [pt], [ynT])

            return [sA, sA2, sB, sB2, sC, sC2, sD, sD2, sE]

        merged = [out_chunk(c) for c in range(8)]
        if g + 1 < 8:
            merged += [[] for _ in range(5)]
            merged += front_units(g + 1)
        S.reserved.extend([S.psum[4], S.psum[5]])
        run_pipeline(merged)
        S.reserved.remove(S.psum[4]); S.reserved.remove(S.psum[5])
    dump("ynT", ynT, [16, 1024], BF16)
    S.release(mark_m)
    if stop == 'ssd':
        S.barrier(); S.emit(); return nc
    f1T = S.alloc("f1T", [8, 1024], BF16); mT = S.alloc("mT", [8, 1024], BF16)
    mark_t = S.top
    wA = [S.alloc("wA%d" % i, [8, 512], BF16) for i in range(2)]
    wB = [S.alloc("wB%d" % i, [8, 512], BF16) for i in range(2)]
    sgt = [S.alloc("sgt%d" % i, [512], F32) for i in range(2)]
    for cbk in range(2):
        wf = wA[cbk % 2]; wg_ = wB[cbk % 2]
        S.dma("pool", wf[:], wview(fnet_w, cbk * 512, 512), wf, writes=[wf])
        S.dma("pool", wg_[:], wview(w_in, W_GF + cbk * 512, 512), wg_, writes=[wg_])
        for tb in range(2):
            hc0 = 0 if tb == 0 else 513
            for c4 in range(4):
                cp_ = cbk * 4 + c4
                p1 = S.bank(); p2 = S.bank()
                for kc in range(8):
                    S.mm(p1[:, :], wf[:, kc, c4 * 128:(c4 + 1) * 128], fmT[:, kc, tb * 512:(tb + 1) * 512], kc == 0, kc == 7,
                         [wf, fmT], [p1])
                for kc in range(8):
                    S.mm(p2[:, :], wg_[:, kc, c4 * 128:(c4 + 1) * 128], hTPM[:, kc, hc0:hc0 + 512], kc == 0, kc == 7,
                         [wg_, hTPM], [p2])
                sg = sgt[c4 % 2]
                S.act(sg[:], p2[:, :], AF.Silu, [p2], [sg])
                S.tt("dve", f1T[:, cp_, tb * 512:(tb + 1) * 512], p1[:, :], sg[:], ALU.mult, [p1, sg], [f1T])
    S.release(mark_t)
    wA2 = [S.alloc("wA2_%d" % i, [8, 256], BF16) for i in range(2)]; wS2 = [S.alloc("wS_%d" % i, [16, 256], BF16) for i in range(2)]
    wC2 = [S.alloc("wC_%d" % i, [8, 256], BF16) for i in range(2)]; wD2 = [S.alloc("wD_%d" % i, [8, 256], BF16) for i in range(2)]
    nwT = S.alloc("nwT", [16], F32)
    S.dma("sp", nwT[:], nwTd, nwT, writes=[nwT])
    sga = [S.alloc("sga%d" % i, [512], F32) for i in range(2)]
    sgb = [S.alloc("sgb%d" % i, [512], F32) for i in range(2)]

    def load_t2(cbk):
        i = cbk % 2
        S.dma("pool", wA2[i][:], wview(wbf, cbk * 256, 256), wA2[i], writes=[wA2[i]])
        S.dma("pool", wS2[i][:], wview(wbs, cbk * 256, 256), wS2[i], writes=[wS2[i]])
        S.dma("pool", wC2[i][:], wview(w_in, W_MGF + cbk * 256, 256), wC2[i], writes=[wC2[i]])
        S.dma("pool", wD2[i][:], wview(w_in, W_MGS + cbk * 256, 256), wD2[i], writes=[wD2[i]])
        S.tt("dve", wS2[i][:], wS2[i][:], nwT[:].unsqueeze(2).to_broadcast([128, 16, 256]), ALU.mult, [wS2[i], nwT], [wS2[i]])

    load_t2(0)
    for cbk in range(4):
        if cbk + 1 < 4:
            load_t2(cbk + 1)
        wA = wA2[cbk % 2]; wS = wS2[cbk % 2]; wC = wC2[cbk % 2]; wD = wD2[cbk % 2]
        for tb in range(2):
            hc0 = 0 if tb == 0 else 513
            tsl = slice(tb * 512, (tb + 1) * 512)
            for c4 in range(2):
                cp_ = cbk * 2 + c4
                csl = slice(c4 * 128, (c4 + 1) * 128)
                pA = S.bank(); pB = S.bank(); pC = S.bank(); pD = S.bank()
                for kc in range(8):
                    S.mm(pC[:, :], wC[:, kc, csl], hTPM[:, kc, hc0:hc0 + 512], kc == 0, kc == 7, [wC, hTPM], [pC])
                for kc in range(8):
                    S.mm(pD[:, :], wD[:, kc, csl], hTPM[:, kc, hc0:hc0 + 512], kc == 0, kc == 7, [wD, hTPM], [pD])
                for kc in range(8):
                    S.mm(pA[:, :], wA[:, kc, csl], f1T[:, kc, tsl], kc == 0, kc == 7, [wA, f1T], [pA])
                for kc in range(16):
                    S.mm(pB[:, :], wS[:, kc, csl], ynT[:, kc, tsl], kc == 0, kc == 15, [wS, ynT], [pB])
                a_ = sga[c4 % 2]; b_ = sgb[c4 % 2]
                S.act(a_[:], pC[:, :], AF.Sigmoid, [pC], [a_])
                S.act(b_[:], pD[:, :], AF.Sigmoid, [pD], [b_])
                S.tt("dve", a_[:], pA[:, :], a_[:], ALU.mult, [pA, a_], [a_])
                S.tt("dve", b_[:], pB[:, :], b_[:], ALU.mult, [pB, b_], [b_])
                S.tt("dve", mT[:, cp_, tsl], a_[:], b_[:], ALU.add, [a_, b_], [mT])
    S.release(mark_t)
    wo = [S.alloc("wo%d" % i, [8, 512], BF16) for i in range(2)]
    xt3 = [S.alloc("xt3_%d" % i, [D], F32) for i in range(3)]
    tmp3 = [S.alloc("tmp3_%d" % i, [D], F32) for i in range(2)]
    ot3 = [S.alloc("ot3_%d" % i, [D], F32) for i in range(2)]
    st3 = [S.alloc("st3_%d" % i, [8], F32) for i in range(2)]
    jk3 = S.alloc("jk3", [512], F32)
    for cbk in range(2):
        S.dma("pool", wo[cbk][:], wview(w_out, cbk * 512, 512), wo[cbk], writes=[wo[cbk]])
    def t3_tile(ti):
        src = xP[ti * 128:(ti + 1) * 128, :] if ti < 4 else xM[(ti - 4) * 128:(ti - 3) * 128, :]
        dst = yP[ti * 128:(ti + 1) * 128, :] if ti < 4 else yM[(ti - 4) * 128:(ti - 3) * 128, :]
        cond = 0 if ti < 4 else 1
        xt = xt3[ti % 3]; ot = ot3[ti % 2]; st = st3[ti % 2]; tmp = tmp3[ti % 2]
        st_ = {}

        def s0():
            S.dma("sp", xt[:], src, xt, writes=[xt])
            pO = [S.bank(), S.bank()]
            st_["pO"] = pO
            for cbk in range(2):
                for kc in range(8):
                    S.mm(pO[cbk][:, :], mT[:, kc, ti * 128:(ti + 1) * 128], wo[cbk][:, kc, :], kc == 0, kc == 7,
                         [mT, wo[cbk]], [pO[cbk]])

        def s1():
            pO = st_["pO"]
            for cbk in range(2):
                S.act(jk3[:], pO[cbk][:, :], AF.Square, [pO[cbk]], [jk3, st], accum=st[:, cbk:cbk + 1])
            S.tt("dve", st[:, 2:3], st[:, 0:1], st[:, 1:2], ALU.add, [st], [st])
            S.act(st[:, 3:4], st[:, 2:3], AF.Ln, [st, epsb], [st], bias=epsb[:, 0:1], scale=1.0 / D)
            S.act(st[:, 4:5], st[:, 3:4], AF.Exp, [st], [st], scale=-0.5)

        def s2():
            pO = st_["pO"]
            for cbk in range(2):
                sl = slice(cbk * 512, (cbk + 1) * 512)
                S.stt("dve", tmp[:, sl], pO[cbk][:, :], st[:, 4:5], g2[:, cond, sl], ALU.mult, ALU.mult,
                      [pO[cbk], st, g2], [tmp])
                S.tt("dve", ot[:, sl], tmp[:, sl], xt[:, sl], ALU.add, [tmp, xt], [ot])

        def s3():
            S.dma("sp", dst, ot[:], ot, reads=[ot])

        return [s0, s1, s2, s3]

    run_pipeline([t3_tile(ti) for ti in range(8)])
    S.barrier()
    S.emit()
    return nc


def _dft_tables():
    l = np.arange(2048)
    r, c = l // 64, l % 64
    ph = (np.outer(r, r) / 32.0 + np.outer(c, c) / 64.0)
    ang = 2.0 * np.pi * (ph % 1.0)
    sc = 1.0 / math.sqrt(2048.0)
    cpos = (np.cos(ang) * sc).astype(np.float32)
    spos = (np.sin(ang) * sc).astype(np.float32)
    k = np.arange(256)
    a2 = 2.0 * np.pi * ((np.outer(k, k) % 256) / 256.0)
    c256 = (np.cos(a2) / 16.0).astype(np.float32)
    s256 = (np.sin(a2) / 16.0).astype(np.float32)
    return cpos, spos, c256, s256


def prep_inputs(x_prompt, x_sample, c, state_ssd_fwd, state_ssd_bwd, c_ctx, ada_w, ada_b, pre_norm_w,
                post_norm_w, w_in, conv_w, conv_b, dt_bias_fwd, dt_bias_bwd, a_log_fwd, a_log_bwd, d_skip,
                ssd_norm_w, fnet_w, w_branch_f, w_branch_s, w_out):
    f = lambda a: np.ascontiguousarray(np.asarray(a, dtype=np.float32))
    bf = lambda a: np.ascontiguousarray(a.astype(ml_dtypes.bfloat16))
    x_prompt, x_sample, c, c_ctx = f(x_prompt), f(x_sample), f(c), f(c_ctx)
    sf, sb = f(state_ssd_fwd), f(state_ssd_bwd)
    cpos, spos, c256, s256 = _dft_tables()
    t256 = bf(np.stack([c256, -s256, s256]))
    cw = f(conv_w)[0].reshape(3, 32, 128).transpose(2, 1, 0).reshape(128, 96)
    cb = f(conv_b)[0].reshape(32, 128).T
    rowsm = np.concatenate([f(dt_bias_fwd)[0], f(dt_bias_bwd)[0], f(a_log_fwd)[0], f(a_log_bwd)[0],
                            f(d_skip)[0]])[None, :]
    rowbig = np.concatenate([f(ssd_norm_w)[0], f(pre_norm_w)[0], f(post_norm_w)[0]])[None, :]
    shared = {
        "t256": t256, "cw": f(cw), "cb": f(cb), "rowsm": f(rowsm), "rowbig": f(rowbig), "nwT": f(f(ssd_norm_w)[0].reshape(16, 128).T),
        "ada_w": f(ada_w)[0], "ada_b": f(ada_b), "w_in": f(w_in)[0], "fnet_w": f(fnet_w)[0],
        "wbf": f(w_branch_f)[0], "wbs": f(w_branch_s)[0], "w_out": f(w_out)[0],
    }
    maps = []
    zero_row = np.zeros((1, D), np.float32)
    for core in range(8):
        b, q = core // 4, core % 4
        xs = x_sample[b]
        hl = xs[q * 512 - 1:q * 512] if q > 0 else zero_row
        hr = xs[(q + 1) * 512:(q + 1) * 512 + 1] if q < 3 else zero_row
        flags = np.zeros((1, 8), np.float32)
        flags[0, 0] = 1.0 if q > 0 else 0.0
        flags[0, 1] = 1.0 if q < 3 else 0.0
        flags[0, 2 + q] = 1.0
        cc = np.stack([c_ctx, c[b]])
        cT = cc.reshape(2, 8, 128).transpose(2, 1, 0).reshape(128, 16)
        st = np.stack([sf[b, 0], sb[b, 0]])
        stT = st.reshape(2, 2048, 128).transpose(0, 2, 1)
        m = dict(shared)
        m.update({
            "xX": f(xs), "xP": f(x_prompt[2 * core:2 * core + 2].reshape(512, D)),
            "xM": f(xs[q * 512:(q + 1) * 512]), "xH": f(np.concatenate([hl, hr], 0)),
            "fl": flags, "cT": f(cT), "stT": f(stT),
            "tcs": bf(np.stack([cpos[:, q * 512:(q + 1) * 512], spos[:, q * 512:(q + 1) * 512]])),
        })
        maps.append(m)
    return maps


_NC_CACHE = {}


def kernel(**inputs):
    maps = prep_inputs(**inputs)
    if "nc" not in _NC_CACHE:
        _NC_CACHE["nc"] = build(False)
    res = run_bass_kernel_spmd(_NC_CACHE["nc"], maps, core_ids=list(range(8)))
    r = res.results
    y_prompt = np.concatenate([r[i]["yP"].reshape(2, 256, D) for i in range(8)], 0).astype(np.float32)
    y_sample = np.stack([np.concatenate([r[4 * b + q]["yM"] for q in range(4)], 0) for b in range(2)]).astype(np.float32)
    nf = np.concatenate([r[i]["sF"] for i in range(8)], 0)[:, None].astype(np.float32)
    nb = np.concatenate([r[i]["sB"] for i in range(8)], 0)[:, None].astype(np.float32)
    return (y_prompt, y_sample, nf, nb)
```

```python
import math
import numpy as np
import ml_dtypes
import concourse.bass as bass
import concourse.mybir as mybir
from concourse.bass_utils import run_bass_kernel_spmd

F32 = mybir.dt.float32
BF16 = mybir.dt.bfloat16
AF = mybir.ActivationFunctionType
ALU = mybir.AluOpType

SAME_ENGINE_SYNC = True
EPS = 1e-6
D = 1024
NX, NP_, NM = 2048, 512, 512
COL_X, COL_P, COL_ME = 0, 2048, 2560
NCOL = 2048 + 512 + 514
W_UF, W_GF, W_Z, W_XS, W_B, W_C, W_DTF, W_DTB, W_MGF, W_MGS = 0, 1024, 2048, 4096, 6144, 7168, 8192, 8224, 8256, 9280


class StopBuild(Exception):
    pass


class Tile:
    __slots__ = ("name", "t", "last_write", "reads", "dsem")

    def __init__(self, name, t):
        self.name = name
        self.t = t
        self.last_write = None
        self.reads = {}
        self.dsem = None

    def __getitem__(self, k):
        return self.t[k]


class Sched:
    ENG = ("pe", "act", "dve", "pool", "sp")

    def __init__(self, nc, arena_words):
        self.nc = nc
        self.streams = {e: [] for e in self.ENG}
        self.count = {}
        self.seen = {e: {} for e in self.ENG}
        self.sems = {}
        self.ndsem = 0
        for e in self.ENG:
            self._mksem(e)
        self.arena = nc.alloc_sbuf_tensor("arena", [128, arena_words], F32)
        self.arena_words = arena_words
        self.top = 0
        self.psum = [Tile("ps%d" % i, nc.alloc_psum_tensor("ps%d" % i, [128, 512], F32)) for i in range(8)]
        self.psi = 0
        self.reserved = []
        self.nops = 0

    def _mksem(self, key):
        self.sems[key] = self.nc.alloc_semaphore(name="s_" + key)
        self.count[key] = 0

    def alloc(self, name, free_shape, dt):
        n = int(np.prod(free_shape))
        words = (n * (2 if dt == BF16 else 4) + 3) // 4
        words = (words + 7) // 8 * 8
        assert self.top + words <= self.arena_words, (name, self.top, words)
        ap = self.arena[:, self.top:self.top + words]
        self.top += words
        if dt == BF16:
            ap = ap.bitcast(BF16)[:, 0:n]
        else:
            ap = ap[:, 0:n]
        if len(free_shape) == 2:
            ap = ap.rearrange("p (a b) -> p a b", a=free_shape[0])
        elif len(free_shape) == 3:
            ap = ap.rearrange("p (a b c) -> p a b c", a=free_shape[0], b=free_shape[1])
        return Tile(name, ap)

    def bank(self):
        while True:
            t = self.psum[self.psi]
            self.psi = (self.psi + 1) % 8
            if t not in self.reserved:
                return t

    def reserve_bank(self):
        t = self.bank()
        self.reserved.append(t)
        return t

    def unreserve(self, t):
        self.reserved.remove(t)

    def _deps(self, reads, writes):
        deps = []
        for t in reads:
            if t.last_write is not None:
                deps.append(t.last_write)
        for t in writes:
            if t.last_write is not None:
                deps.append(t.last_write)
            deps.extend(t.reads.items())
        return deps

    def _emit_waits(self, eng, deps, skip_own=False):
        need = {}
        for key, val in deps:
            if key == eng and (skip_own or not SAME_ENGINE_SYNC):
                continue
            if self.seen[eng].get(key, 0) >= val:
                continue
            if need.get(key, 0) < val:
                need[key] = val
        for key, val in need.items():
            self.seen[eng][key] = val
            sem = self.sems[key]
            self.streams[eng].append(lambda e, sem=sem, val=val: e.wait_ge(sem, val))

    def _post(self, tk, reads, writes):
        for t in reads:
            if t.reads.get(tk[0], 0) < tk[1]:
                t.reads[tk[0]] = tk[1]
        for t in writes:
            t.last_write = tk
            t.reads = {}

    def op(self, eng, fn, reads=(), writes=()):
        self._emit_waits(eng, self._deps(reads, writes), skip_own=(eng == "pe"))
        self.count[eng] += 1
        tk = (eng, self.count[eng])
        sem = self.sems[eng]
        self.streams[eng].append(lambda e, fn=fn, sem=sem: fn(e).then_inc(sem, 1))
        self._post(tk, reads, writes)
        self.nops += 1
        return tk

    def dma(self, q, out_ap, in_ap, sem_tile, reads=(), writes=()):
        self._emit_waits(q, self._deps(reads, writes))
        st = sem_tile
        if st.dsem is None:
            st.dsem = "d%d" % self.ndsem
            self.ndsem += 1
            self._mksem(st.dsem)
        key = st.dsem
        self.count[key] += 16
        tk = (key, self.count[key])
        sem = self.sems[key]
        self.streams[q].append(lambda e, o=out_ap, i=in_ap, sem=sem: e.dma_start(out=o, in_=i).then_inc(sem, 16))
        self._post(tk, reads, writes)
        return tk

    def barrier(self):
        tks = [(k, v) for k, v in self.count.items() if v > 0]
        for e in self.ENG:
            self._emit_waits(e, tks)

    def release(self, mark):
        self.barrier()
        self.top = mark

    def mm(self, out, lhsT, rhs, start, stop, reads, writes):
        return self.op("pe", lambda e: e.matmul(out, lhsT=lhsT, rhs=rhs, start=start, stop=stop), reads, writes)

    def tr(self, out, in_, ident, reads, writes):
        return self.op("pe", lambda e: e.transpose(out=out, in_=in_, identity=ident), reads, writes)

    def act(self, out, in_, func, reads, writes, bias=None, scale=None, accum=None):
        kw = {}
        if bias is not None:
            kw["bias"] = bias
        if scale is not None:
            kw["scale"] = scale
        if accum is not None:
            kw["accum_out"] = accum
        return self.op("act", lambda e: e.activation(out=out, in_=in_, func=func, **kw), reads, writes)

    def tt(self, eng, out, in0, in1, op, reads, writes):
        return self.op(eng, lambda e: e.tensor_tensor(out=out, in0=in0, in1=in1, op=op), reads, writes)

    def ts(self, eng, out, in0, s1, op0, reads, writes, s2=None, op1=None):
        if op1 is None:
            return self.op(eng, lambda e: e.tensor_scalar(out=out, in0=in0, scalar1=s1, scalar2=None, op0=op0),
                           reads, writes)
        return self.op(eng, lambda e: e.tensor_scalar(out=out, in0=in0, scalar1=s1, scalar2=s2, op0=op0, op1=op1),
                       reads, writes)

    def stt(self, eng, out, in0, scalar, in1, op0, op1, reads, writes):
        return self.op(eng, lambda e: e.scalar_tensor_tensor(out=out, in0=in0, scalar=scalar, in1=in1,
                                                              op0=op0, op1=op1), reads, writes)

    def cp(self, eng, out, in_, reads, writes):
        if eng == "act":
            return self.op("act", lambda e: e.copy(out=out, in_=in_), reads, writes)
        return self.op(eng, lambda e: e.tensor_copy(out=out, in_=in_), reads, writes)

    def memset(self, eng, ap, val, writes):
        return self.op(eng, lambda e: e.memset(ap, val), (), writes)

    def emit(self):
        nc = self.nc
        with nc.Block() as block:
            @block.tensor
            def _(e):
                for f in self.streams["pe"]:
                    f(e)

            @block.scalar
            def _(e):
                for f in self.streams["act"]:
                    f(e)

            @block.vector
            def _(e):
                for f in self.streams["dve"]:
                    f(e)

            @block.gpsimd
            def _(e):
                for f in self.streams["pool"]:
                    f(e)

            @block.sync
            def _(e):
                for f in self.streams["sp"]:
                    f(e)


ARENA_WORDS = 53000


def run_pipeline(units):
    if not units:
        return
    nst = max(len(u) for u in units)
    for step in range(len(units) + nst - 1):
        for si in range(nst):
            k = step - si
            if 0 <= k < len(units) and si < len(units[k]):
                units[k][si]()

NROW_SM = 160
PROMPT_TOK = [(COL_P + i * 128) for i in range(4)]


def build(dbg=False, stop=None):
    import os
    stop = stop or os.environ.get('KSTOP')
    nc = bass.Bass("TRN2", target_bir_lowering=False)
    S = Sched(nc, ARENA_WORDS)

    def din(name, shape, dt=F32):
        return nc.dram_tensor(name, list(shape), dt, kind="ExternalInput").ap()

    def dout(name, shape):
        return nc.dram_tensor(name, list(shape), F32, kind="ExternalOutput").ap()

    xX = din("xX", [2048, D]); xP = din("xP", [512, D]); xM = din("xM", [512, D]); xH = din("xH", [2, D])
    fl = din("fl", [1, 8]); cT = din("cT", [128, 16]); stT = din("stT", [2, 128, 2048])
    tcs = din("tcs", [2, 2048, 512], BF16); t256 = din("t256", [3, 256, 256], BF16)
    cwd = din("cw", [128, 96]); cbd = din("cb", [128, 32]); rowsm = din("rowsm", [1, NROW_SM])
    nwTd = din("nwT", [128, 16])
    rowbig = din("rowbig", [1, 4096])
    ada_w = din("ada_w", [D, 3072]); ada_b = din("ada_b", [1, 3072]); w_in = din("w_in", [D, 10304])
    fnet_w = din("fnet_w", [D, D]); wbf = din("wbf", [D, D]); wbs = din("wbs", [2048, D]); w_out = din("w_out", [D, D])
    yP = dout("yP", [512, D]); yM = dout("yM", [512, D])
    sF = dout("sF", [2, 32, 64, 128]); sB = dout("sB", [2, 32, 64, 128])
    def dump(name, tile, shape, dt=F32):
        if not dbg:
            return
        d = nc.dram_tensor("dbg_" + name, [128] + list(shape), dt, kind="ExternalOutput").ap()
        S.dma("sp", d, tile[:], tile, reads=[tile])

    def wview(w, c0, n):
        return w[:, c0:c0 + n].rearrange("(kc p) n -> p kc n", p=128)

    identb = S.alloc("identb", [128], BF16); identf = S.alloc("identf", [128], F32)
    LEf = S.alloc("LEf", [128], F32); onesf = S.alloc("onesf", [128], F32)
    LEb = S.alloc("LEb", [128], BF16); GEb = S.alloc("GEb", [128], BF16)
    GTb = S.alloc("GTb", [128], BF16); LTb = S.alloc("LTb", [128], BF16)
    epsb = S.alloc("epsb", [1], F32); oneb = S.alloc("oneb", [1], F32)
    sel2 = S.alloc("sel2", [256], F32)
    cw = S.alloc("cw", [96], F32); cb = S.alloc("cb", [32], F32)
    rows = S.alloc("rows", [NROW_SM], F32); flg = S.alloc("flg", [8], F32)
    Arow = S.alloc("Arow", [64], F32)
    g2 = S.alloc("g2", [2, D], F32)
    hTPM = S.alloc("hTPM", [8, 1026], BF16)
    fmT = S.alloc("fmT", [8, 1024], BF16)
    dtO = S.alloc("dtO", [8, 64], F32); scO = S.alloc("scO", [8, 64], F32)
    dtdec = S.alloc("dtdec", [8, 64], F32); cdA = S.alloc("cdA", [8, 64], F32)
    wini = S.alloc("wini", [64], F32)
    dabf = S.alloc("dabf", [8, 64], BF16); ebias = S.alloc("ebias", [8, 64], F32)
    NEGf = S.alloc("NEGf", [128], BF16); NEGb = S.alloc("NEGb", [128], BF16); NLTb = S.alloc("NLTb", [128], BF16)
    SmF = S.alloc("SmF", [8, 256], BF16); SmB = S.alloc("SmB", [8, 256], BF16)
    wgxs0 = S.alloc("wgxs0", [8, 256], BF16)
    mark_pre_hTX = S.top
    hTX = S.alloc("hTX", [8, 2048], BF16)
    dtdecX = S.alloc("dtdecX", [16, 64], F32)
    mark_base = S.top

    wgc0 = S.alloc("wgc0", [8, 384], BF16)
    mark_ctx = S.top
    wdt = S.alloc("wdt", [8, 64], BF16)
    wuf0 = S.alloc("wuf0", [8, 256], BF16)
    t2 = S.alloc("t2", [3, 2, 256], BF16)
    mark_base2 = S.top
    tmpf = S.alloc("tmpf", [128], F32)

    def tri(dst_bf, pattern_step, cm, cmp, also_f32=None):
        S.memset("pool", tmpf[:], 1.0, [tmpf])
        S.op("pool", lambda e: e.affine_select(out=tmpf[:], in_=tmpf[:], pattern=[[pattern_step, 128]],
                                               compare_op=cmp, fill=0.0, base=0, channel_multiplier=cm),
             [tmpf], [tmpf])
        if dst_bf is not None:
            S.cp("pool", dst_bf[:], tmpf[:], [tmpf], [dst_bf])
        if also_f32 is not None:
            S.cp("pool", also_f32[:], tmpf[:], [tmpf], [also_f32])

    tri(identb, -1, 1, ALU.is_equal, identf)
    tri(LEb, 1, -1, ALU.is_ge, LEf)
    tri(GEb, -1, 1, ALU.is_ge)
    tri(GTb, -1, 1, ALU.is_gt)
    tri(LTb, 1, -1, ALU.is_gt)
    S.ts("pool", NEGf[:], GTb[:], -30000.0, ALU.mult, [GTb], [NEGf])
    S.ts("pool", NEGb[:], LTb[:], -30000.0, ALU.mult, [LTb], [NEGb])
    S.ts("pool", NLTb[:], LTb[:], -1.0, ALU.mult, [LTb], [NLTb])
    S.memset("pool", onesf[:], 1.0, [onesf])
    S.memset("pool", epsb[:], EPS, [epsb])
    S.memset("pool", oneb[:], 1.0, [oneb])
    S.memset("pool", sel2[:], 1.0, [sel2])
    S.op("pool", lambda e: e.affine_select(out=sel2[:], in_=sel2[:], pattern=[[1, 256]], compare_op=ALU.is_ge,
                                           fill=0.0, base=0, channel_multiplier=-128), [sel2], [sel2])
    S.op("pool", lambda e: e.affine_select(out=sel2[:], in_=sel2[:], pattern=[[-1, 256]], compare_op=ALU.is_ge,
                                           fill=0.0, base=127, channel_multiplier=128), [sel2], [sel2])
    S.dma("sp", cw[:], cwd, cw, writes=[cw])
    S.dma("sp", cb[:], cbd, cb, writes=[cb])
    S.dma("sp", rows[:], rowsm.partition_broadcast(128), rows, writes=[rows])
    S.dma("sp", flg[:], fl.partition_broadcast(128), flg, writes=[flg])
    S.act(Arow[:], rows[:, 64:128], AF.Exp, [rows], [Arow])
    S.ts("dve", Arow[:], Arow[:], -1.0, ALU.mult, [Arow], [Arow])

    if stop == 'const':
        dump('rows', rows, [NROW_SM]); S.barrier(); S.emit(); return nc
    g1 = S.alloc("g1", [2, D], F32); shf = S.alloc("shf", [2, D], F32)
    mark1 = S.top
    scT = S.alloc("scT", [8, 2], F32)
    ada2 = S.alloc("ada2", [3072], F32)
    prew = S.alloc("prew", [D], F32); postw = S.alloc("postw", [D], F32)
    wst = [S.alloc("wst%d" % i, [8, 512], F32) for i in range(3)]
    S.dma("sp", scT[:].rearrange("p a b -> p (a b)"), cT, scT, writes=[scT])
    S.dma("sp", ada2[0:2, :], ada_b.partition_broadcast(2), ada2, writes=[ada2])
    S.dma("sp", prew[:], rowbig[:, 2048:3072].partition_broadcast(128), prew, writes=[prew])
    S.dma("sp", postw[:], rowbig[:, 3072:4096].partition_broadcast(128), postw, writes=[postw])
    S.act(scT[:], scT[:], AF.Silu, [scT], [scT])
    for cbk in range(6):
        wt = wst[cbk % 3]
        S.dma(("sp", "act")[cbk % 2], wt[:], wview(ada_w, cbk * 512, 512), wt, writes=[wt])
        pb = S.bank()
        for kc in range(8):
            S.mm(pb[0:2, :], scT[:, kc, :], wt[:, kc, :], kc == 0, kc == 7, [scT, wt], [pb])
        S.tt("dve", ada2[0:2, cbk * 512:(cbk + 1) * 512], pb[0:2, :], ada2[0:2, cbk * 512:(cbk + 1) * 512],
             ALU.add, [pb, ada2], [ada2])
    if stop == 'p0a':
        dump('ada2', ada2, [3072]); S.barrier(); S.emit(); return nc
    S.dma("pool", wdt[:], wview(w_in, W_DTF, 64), wdt, writes=[wdt])
    S.dma("pool", wuf0[:], wview(w_in, W_UF, 256), wuf0, writes=[wuf0])
    S.dma("sp", t2[:], t256.rearrange("w (c p) n -> p w c n", p=128), t2, writes=[t2])
    for c in range(2):
        for cbk in range(6):
            pb = S.bank()
            S.mm(pb[:, :], sel2[0:2, c * 128:(c + 1) * 128], ada2[0:2, cbk * 512:(cbk + 1) * 512], True, True,
                 [sel2, ada2], [pb])
            sl = slice((cbk % 2) * 512, (cbk % 2) * 512 + 512)
            if cbk < 2:
                S.cp("act", shf[:, c, sl], pb[:, :], [pb], [shf])
            elif cbk < 4:
                S.stt("dve", g1[:, c, sl], pb[:, :], 1.0, prew[:, sl], ALU.add, ALU.mult, [pb, prew], [g1])
            else:
                S.tt("dve", g2[:, c, sl], pb[:, :], postw[:, sl], ALU.mult, [pb, postw], [g2])
    S.release(mark1)

    if stop == 'p0':
        dump('g2', g2, [2, D]); S.barrier(); S.emit(); return nc
    xbuf = [S.alloc("xbuf%d" % i, [D], F32) for i in range(5)]
    xhalo = S.alloc("xhalo", [D], F32)
    tmpb = [S.alloc("tmpb%d" % i, [D], F32) for i in range(2)]
    hbb = [S.alloc("hbb%d" % i, [D], BF16) for i in range(2)]
    stat = [S.alloc("stat%d" % i, [4], F32) for i in range(4)]
    junk = S.alloc("junk", [D], F32)
    S.memset("pool", xhalo[:], 0.0, [xhalo])
    tiles = []
    for i in range(16):
        tiles.append((xX[i * 128:(i + 1) * 128, :], 128, 1, ("X", i)))
    for i in range(4):
        tiles.append((xP[i * 128:(i + 1) * 128, :], 128, 0, ("P", i)))
    for i in range(4):
        tiles.append((xM[i * 128:(i + 1) * 128, :], 128, 1, ("M", i)))
    tiles.append((xH, 2, 1, ("H", 0)))
    def p1_tile(ti, src, nrow, cond, kind, idx):
        xt = xhalo if kind == "H" else xbuf[ti % 5]
        st = stat[ti % 4]; tb = tmpb[ti % 2]; hb = hbb[ti % 2]
        st_ = {}

        def s0():
            S.dma("sp", xt[0:nrow, :], src, xt, writes=[xt])

        def s0b():
            S.act(junk[:], xt[:], AF.Square, [xt], [junk, st], accum=st[:, 0:1])

        def s0c():
            S.act(st[:, 1:2], st[:, 0:1], AF.Ln, [st, epsb], [st], bias=epsb[:, 0:1], scale=1.0 / D)

        def s0d():
            S.act(st[:, 2:3], st[:, 1:2], AF.Exp, [st], [st], scale=-0.5)

        def s1():
            S.stt("dve", tb[:], xt[:], st[:, 2:3], g1[:, cond, :], ALU.mult, ALU.mult, [xt, st, g1], [tb])

        def s1b():
            S.tt("dve", hb[:], tb[:], shf[:, cond, :], ALU.add, [tb, shf], [hb])

        def s2():
            pb = S.bank()
            st_["pb"] = pb
            pbb = pb[:, :].bitcast(BF16)
            for kc in range(8):
                S.tr(pbb[:, kc * 128:(kc + 1) * 128], hb[:, kc * 128:(kc + 1) * 128], identb[:], [hb, identb], [pb])

        def s3():
            pb = st_["pb"]
            pv = pb[:, :].bitcast(BF16).rearrange("p (a b) -> p a b", a=8)
            if kind == "X":
                S.cp("act", hTX[:, :, idx * 128:(idx + 1) * 128], pv, [pb], [hTX])
            elif kind == "P":
                S.cp("act", hTPM[:, :, idx * 128:(idx + 1) * 128], pv, [pb], [hTPM])
            elif kind == "M":
                S.cp("act", hTPM[:, :, 513 + idx * 128:513 + (idx + 1) * 128], pv, [pb], [hTPM])
            else:
                S.ts("dve", hTPM[:, :, 512:513], pv[:, :, 0:1], flg[:, 0:1], ALU.mult, [pb, flg], [hTPM])
                S.ts("dve", hTPM[:, :, 1025:1026], pv[:, :, 1:2], flg[:, 1:2], ALU.mult, [pb, flg], [hTPM])

        return [s0, s0b, s0c, s0d, s1, s1b, s2, s3]

    run_pipeline([p1_tile(ti, src, nrow, cond, kind, idx) for ti, (src, nrow, cond, (kind, idx)) in enumerate(tiles)])
    S.release(mark_base2)
    dump("hTX", hTX, [8, 2048], BF16)
    dump("hTPM", hTPM, [8, 1026], BF16)
    dump("g2", g2, [2, D])
    if stop == 'p1':
        S.barrier(); S.emit(); return nc
    mark_dt = mark_base
    t_dtr = S.alloc("t_dtr", [8, 64], F32); t_e = S.alloc("t_e", [8, 64], F32); t_dt = S.alloc("t_dt", [8, 64], F32)
    t_da = S.alloc("t_da", [8, 64], F32); t_tot = S.alloc("t_tot", [8, 64], F32); t_x = S.alloc("t_x", [8, 64], F32)
    t_dec = S.alloc("t_dec", [8, 64], F32); t_y = S.alloc("t_y", [8, 64], F32)
    totX = S.alloc("totX", [16, 64], F32); Ppre = S.alloc("Ppre", [17, 64], F32)
    omg = S.alloc("omg", [16, 64], F32); t_o = S.alloc("t_o", [16, 32], F32); t_o2 = S.alloc("t_o2", [16, 32], F32)

    def chunk_cols(ci):
        if ci < 4:
            return hTPM, ci * 128
        if ci < 8:
            return hTPM, 513 + (ci - 4) * 128
        return hTX, (ci - 8) * 128

    def v8(ap):
        return ap.rearrange("p (a b) -> p a b", a=8)

    def dtA(bi):
        pb = S.bank()
        for j in range(8):
            ht, c0 = chunk_cols(bi * 8 + j)
            for kc in range(8):
                S.mm(pb[:, j * 64:(j + 1) * 64], ht[:, kc, c0:c0 + 128], wdt[:, kc, :], kc == 0, kc == 7, [ht, wdt], [pb])
        S.tt("dve", t_dtr[:], v8(pb[:, :]), rows[:, 0:64].unsqueeze(1).to_broadcast([128, 8, 64]), ALU.add,
             [pb, rows], [t_dtr])
        S.act(t_e[:], t_dtr[:], AF.Exp, [t_dtr], [t_e])
        S.act(t_dt[:], t_e[:], AF.Ln, [t_e, oneb], [t_dt], bias=oneb[:, 0:1])
        S.tt("pool", t_da[:], t_dt[:], Arow[:].unsqueeze(1).to_broadcast([128, 8, 64]), ALU.mult, [t_dt, Arow], [t_da])

    def dtB(bi):
        pi = S.bank(); po = S.bank()
        da_flat = t_da[:].rearrange("p a b -> p (a b)")
        S.mm(pi[:, :], LEf[:], da_flat, True, True, [LEf, t_da], [pi])
        S.mm(po[:, :], onesf[:], da_flat, True, True, [onesf, t_da], [po])
        piv = v8(pi[:, :]); pov = v8(po[:, :])
        S.cp("act", t_tot[:], pov, [po], [t_tot])
        if bi == 0:
            S.act(cdA[:], t_tot[:], AF.Exp, [t_tot], [cdA])
        S.tt("dve", t_x[:, :, 0:32], t_tot[:, :, 0:32], piv[:, :, 0:32], ALU.subtract, [t_tot, pi], [t_x])
        S.tt("dve", t_x[:, :, 32:64], piv[:, :, 32:64], t_da[:, :, 32:64], ALU.subtract, [pi, t_da], [t_x])
        S.act(t_dec[:], t_x[:], AF.Exp, [t_x], [t_dec])
        if bi == 0:
            S.tt("pool", dtdec[:], t_dt[:], t_dec[:], ALU.mult, [t_dt, t_dec], [dtdec])
        else:
            S.tt("pool", dtdecX[:, (bi - 1) * 8:bi * 8, :], t_dt[:], t_dec[:], ALU.mult, [t_dt, t_dec], [dtdecX])
        if bi > 0:
            S.cp("pool", totX[:, (bi - 1) * 8:bi * 8, :], t_tot[:], [t_tot], [totX])
        if bi == 0:
            S.cp("pool", dtO[:], t_dt[:], [t_dt], [dtO])
            S.act(scO[:, :, 0:32], piv[:, :, 0:32], AF.Exp, [pi], [scO])
            dav = dabf[:].rearrange("p c (g d h) -> p c g d h", g=8, d=2, h=4)
            ebv = ebias[:].rearrange("p c (g d h) -> p c g d h", g=8, d=2, h=4)
            for d_ in range(2):
                S.cp("pool", dav[:, :, :, d_, :], t_da[:, :, d_ * 32:(d_ + 1) * 32].rearrange("p c (g h) -> p c g h", g=8),
                     [t_da], [dabf])
            for d_ in range(2):
                S.cp("pool", t_y[:, :, d_ * 32:(d_ + 1) * 32].rearrange("p c (g h) -> p c g h", g=8), dav[:, :, :, d_, :],
                     [dabf], [t_y])
            pr = S.bank()
            S.mm(pr[:, :], LEf[:], t_y[:].rearrange("p a b -> p (a b)"), True, True, [LEf, t_y], [pr])
            prv = v8(pr[:, :])
            S.ts("dve", ebv[:, :, :, 0, :], prv[:, :, 0:32].rearrange("p c (g h) -> p c g h", g=8), -1.0, ALU.mult, [pr], [ebias])
            S.tt("dve", ebv[:, :, :, 1, :], prv[:, :, 32:64].rearrange("p c (g h) -> p c g h", g=8),
                 t_y[:, :, 32:64].rearrange("p c (g h) -> p c g h", g=8), ALU.subtract, [pr, t_y], [ebias])
            S.tt("dve", t_y[:, :, 32:64], t_tot[:, :, 32:64], t_x[:, :, 32:64], ALU.subtract, [t_tot, t_x], [t_y])
            S.act(scO[:, :, 32:64], t_y[:, :, 32:64], AF.Exp, [t_y], [scO])

    def dtOmega():
        S.memset("pool", Ppre[:, 0:1, :], 0.0, [Ppre])
        for k in range(16):
            S.tt("dve", Ppre[:, k + 1, :], Ppre[:, k, :], totX[:, k, :], ALU.add, [Ppre, totX], [Ppre])
        S.memset("pool", omg[:], 0.0, [omg])
        S.memset("pool", wini[:], 0.0, [wini])
        for j in range(1, 4):
            n = 4 * j
            S.tt("dve", t_o[:, 0:n, :], Ppre[:, n:n + 1, 0:32].to_broadcast([128, n, 32]), Ppre[:, 1:n + 1, 0:32],
                 ALU.subtract, [Ppre], [t_o])
            S.act(t_o2[:, 0:n, :], t_o[:, 0:n, :], AF.Exp, [t_o], [t_o2])
            S.stt("dve", omg[:, 0:n, 0:32], t_o2[:, 0:n, :], flg[:, 2 + j:3 + j], omg[:, 0:n, 0:32], ALU.mult, ALU.add,
                  [t_o2, flg, omg], [omg])
        for j in range(0, 3):
            b0 = 4 * (j + 1); n = 16 - b0
            S.tt("dve", t_o[:, 0:n, :], Ppre[:, b0:16, 32:64], Ppre[:, b0:b0 + 1, 32:64].to_broadcast([128, n, 32]),
                 ALU.subtract, [Ppre], [t_o])
            S.act(t_o2[:, 0:n, :], t_o[:, 0:n, :], AF.Exp, [t_o], [t_o2])
            S.stt("dve", omg[:, b0:16, 32:64], t_o2[:, 0:n, :], flg[:, 2 + j:3 + j], omg[:, b0:16, 32:64], ALU.mult, ALU.add,
                  [t_o2, flg, omg], [omg])
        for j in range(4):
            S.act(t_o[:, 0, :], Ppre[:, 4 * j, 0:32], AF.Exp, [Ppre], [t_o])
            S.stt("dve", wini[:, 0:32], t_o[:, 0, :], flg[:, 2 + j:3 + j], wini[:, 0:32], ALU.mult, ALU.add, [t_o, flg, wini], [wini])
            S.tt("dve", t_o[:, 1, :], Ppre[:, 16, 32:64], Ppre[:, 4 * (j + 1), 32:64], ALU.subtract, [Ppre], [t_o])
            S.act(t_o[:, 2, :], t_o[:, 1, :], AF.Exp, [t_o], [t_o])
            S.stt("dve", wini[:, 32:64], t_o[:, 2, :], flg[:, 2 + j:3 + j], wini[:, 32:64], ALU.mult, ALU.add, [t_o, flg, wini], [wini])
        S.tt("dve", dtdecX[:], dtdecX[:], omg[:], ALU.mult, [dtdecX, omg], [dtdecX])

    if stop == 'dt':
        dtA(0); dtB(0); dtA(1); dtB(1); dtA(2); dtB(2); dtOmega()
        S.barrier(); S.emit(); return nc

    tc_sb = S.alloc("tc_sb", [16, 512], BF16); ts_sb = S.alloc("ts_sb", [16, 512], BF16)
    S.dma("sp", tc_sb[:], tcs[0].rearrange("(lt p) n -> p lt n", p=128), tc_sb, writes=[tc_sb])
    S.dma("sp", ts_sb[:], tcs[1].rearrange("(lt p) n -> p lt n", p=128), ts_sb, writes=[ts_sb])
    wuf = [wuf0, S.alloc("wuf1", [8, 256], BF16)]
    Ug = [S.alloc("Ug%d" % i, [20, 256], BF16) for i in range(1)]
    Vt = S.alloc("Vt", [2, 2, 1024], BF16)
    ev = [0]

    def evac(out, in_, reads, writes):
        ev[0] += 1
        S.cp("act" if ev[0] % 2 == 0 else "dve", out, in_, reads, writes)

    S.dma("pool", wgc0[:, :, 0:256], wview(w_in, W_XS, 256), wgc0, writes=[wgc0])
    S.dma("pool", wgc0[:, :, 256:384], wview(w_in, W_B, 128), wgc0, writes=[wgc0])
    dtA(0)
    for g in range(4):
        w = wuf[g % 2]; U = Ug[0]
        if g > 0:
            S.dma("pool", w[:], wview(w_in, W_UF + g * 256, 256), w, writes=[w])
        for tp in range(10):
            pb = S.bank()
            for j in range(2):
                ti = tp * 2 + j
                ht, c0 = (hTX, ti * 128) if ti < 16 else (hTPM, (ti - 16) * 128)
                for kc in range(8):
                    S.mm(pb[:, j * 256:(j + 1) * 256], ht[:, kc, c0:c0 + 128], w[:, kc, :], kc == 0, kc == 7,
                         [ht, w], [pb])
            evac(U[:, tp * 2:tp * 2 + 2, :], pb[:, :].rearrange("p (a b) -> p a b", a=2), [pb], [U])
        if g == 0:
            dtB(0); dtA(1)
        if g == 1:
            dtB(2)
        for cs, tab in ((0, tc_sb), (1, ts_sb)):
            for wc in range(2):
                pb = S.bank()
                for lt in range(16):
                    S.mm(pb[:, :], U[:, lt, wc * 128:(wc + 1) * 128], tab[:, lt, :], lt == 0, lt == 15, [U, tab], [pb])
                evac(Vt[:, cs, wc, 512:1024], pb[:, :], [pb], [Vt])
        if g == 0:
            dtB(1); dtA(2)
        if g == 1:
            dtOmega()
        for cs in range(2):
            for wc in range(2):
                pb = S.bank()
                for sq in range(2):
                    for lt in range(2):
                        S.mm(pb[:, sq * 256:(sq + 1) * 256], U[:, 16 + sq * 2 + lt, wc * 128:(wc + 1) * 128],
                             t2[:, 0 if cs == 0 else 2, lt, :], lt == 0, lt == 1, [U, t2], [pb])
                evac(Vt[:, cs, wc, 0:512], pb[:, :], [pb], [Vt])
        for tb in range(2):
            for wpc in range(2):
                pb = S.bank()
                k = 0
                for cs in range(2):
                    for wc in range(2):
                        S.mm(pb[:, :], t2[:, 0 if cs == 0 else 1, wc, wpc * 128:(wpc + 1) * 128],
                             Vt[:, cs, wc, tb * 512:(tb + 1) * 512], k == 0, k == 3, [t2, Vt], [pb])
                        k += 1
                evac(fmT[:, g * 2 + wpc, tb * 512:(tb + 1) * 512], pb[:, :], [pb], [fmT])
    dump("fmT", fmT, [8, 1024], BF16)
    dump("dtO", dtO, [8, 64]); dump("scO", scO, [8, 64])
    dump("dtdec", dtdec, [8, 64]); dump("cdA", cdA, [8, 64])
    S.release(mark_ctx)
    if stop == 'fourier':
        S.barrier(); S.emit(); return nc
    accs = [S.alloc("acc%d" % i, [512], F32) for i in range(4)]
    cvk = [0]

    def conv_unit(wt, wcol0, cidx, ht, hc0, n_in, o_rel0, n_out, left_from, right_to, dest_ap, dest_tile):
        st_ = {}

        def s0():
            pb = S.bank()
            st_["pb"] = pb
            for kc in range(8):
                S.mm(pb[:, 0:n_in], wt[:, kc, wcol0:wcol0 + 128], ht[:, kc, hc0:hc0 + n_in], kc == 0, kc == 7, [wt, ht], [pb])

        def s1():
            pb = st_["pb"]
            cvk[0] += 1
            acc = accs[cvk[0] % len(accs)]
            st_["acc"] = acc
            S.act(acc[:, 0:n_out], pb[:, o_rel0:o_rel0 + n_out], AF.Identity, [pb, cw, cb], [acc],
                  bias=cb[:, cidx:cidx + 1], scale=cw[:, cidx * 3 + 1:cidx * 3 + 2])
        def s1b():
            pb = st_["pb"]; acc = st_["acc"]
            a, b = max(o_rel0, left_from), o_rel0 + n_out
            S.stt("dve", acc[:, a - o_rel0:b - o_rel0], pb[:, a - 1:b - 1], cw[:, cidx * 3:cidx * 3 + 1],
                  acc[:, a - o_rel0:b - o_rel0], ALU.mult, ALU.add, [pb, cw, acc], [acc])
            a, b = o_rel0, min(o_rel0 + n_out, right_to)
            S.stt("dve", acc[:, a - o_rel0:b - o_rel0], pb[:, a + 1:b + 1], cw[:, cidx * 3 + 2:cidx * 3 + 3],
                  acc[:, a - o_rel0:b - o_rel0], ALU.mult, ALU.add, [pb, cw, acc], [acc])

        def s2():
            acc = st_["acc"]
            S.act(dest_ap, acc[:, 0:n_out], AF.Silu, [acc], [dest_tile])

        return [s0, s1, s1b, s2]

    def h4(ap):
        return ap.rearrange("p (h q) -> p h q", h=4)

    def bc4(ap):
        return ap.unsqueeze(2).to_broadcast([128, 4, 64])

    wgc = [wgc0, S.alloc("wgc1", [8, 384], BF16)]
    xsTX = S.alloc("xsTX", [2, 2048], BF16); BTX = S.alloc("BTX", [2048], BF16)
    xbs = [S.alloc("xbs%d" % i, [384], BF16) for i in range(3)]
    xdd = [S.alloc("xdd%d" % i, [256], BF16) for i in range(6)]
    Sst = [S.alloc("Sst%d" % i, [256], F32) for i in range(2)]
    Stmp = [S.alloc("Stmp%d" % i, [256], F32) for i in range(2)]
    segs = [(0, 410), (410, 820), (820, 1230), (1230, 1640), (1640, 2048)]
    def load_wgc(g):
        wt = wgc[g % 2]
        S.dma("pool", wt[:, :, 0:256], wview(w_in, W_XS + g * 256, 256), wt, writes=[wt])
        S.dma("pool", wt[:, :, 256:384], wview(w_in, W_B + g * 128, 128), wt, writes=[wt])

    for g in range(8):
        wt = wgc[g % 2]
        units = []
        for (o0, o1) in segs:
            i0, i1 = max(o0 - 1, 0), min(o1 + 1, 2048)
            for cc in range(3):
                cidx = (2 * g + cc) if cc < 2 else 16 + g
                dest = xsTX[:, cc, o0:o1] if cc < 2 else BTX[:, o0:o1]
                units.append(conv_unit(wt, cc * 128, cidx, hTX, i0, i1 - i0, o0 - i0, o1 - o0, 1 if i0 == 0 else 0,
                                       2047 - i0, dest, xsTX if cc < 2 else BTX))
        run_pipeline(units)
        if g + 1 < 8:
            load_wgc(g + 1)
        if g == 6:
            S.dma("pool", wgxs0[:], wview(w_in, W_XS, 256), wgxs0, writes=[wgxs0])
        psF = S.reserve_bank(); psB = S.reserve_bank()
        def ctx_chunk(c):
            st_ = {}

            def s0():
                pb = S.bank(); pbb = pb[:, :].bitcast(BF16)
                st_["pb"] = pb
                for cc in range(2):
                    S.tr(pbb[:, cc * 128:(cc + 1) * 128], xsTX[:, cc, c * 128:(c + 1) * 128], identb[:], [xsTX, identb], [pb])
                S.tr(pbb[:, 256:384], BTX[:, c * 128:(c + 1) * 128], identb[:], [BTX, identb], [pb])

            def s1():
                pb = st_["pb"]; pbb = pb[:, :].bitcast(BF16)
                xb = xbs[c % len(xbs)]
                S.cp("act", xb[:], pbb[:, 0:384], [pb], [xb])
                for d in range(2):
                    if (d == 0 and c >= 12) or (d == 1 and c < 4):
                        continue
                    col0 = d * 32 + g * 4
                    xd = xdd[(2 * c + d) % len(xdd)]
                    S.tt("dve", h4(xd[:]), h4(xb[:, 0:256]), bc4(dtdecX[:, c, col0:col0 + 4]),
                         ALU.mult, [xb, dtdecX], [xd])

            def s2():
                xb = xbs[c % len(xbs)]
                for d in range(2):
                    if (d == 0 and c >= 12) or (d == 1 and c < 4):
                        continue
                    xd = xdd[(2 * c + d) % len(xdd)]
                    pacc = psF if d == 0 else psB
                    first = (c == 0) if d == 0 else (c == 4)
                    last = (c == 11) if d == 0 else (c == 15)
                    S.mm(pacc[:, 0:256], xb[:, 256:384], xd[:], first, last, [xb, xd], [pacc])

            return [s0, s1, s2]

        run_pipeline([ctx_chunk(c) for c in range(16)])
        for d in range(2):
            Sm = SmF if d == 0 else SmB
            pacc = psF if d == 0 else psB
            S.dma("sp", Sst[d][:], stT[d][:, g * 256:(g + 1) * 256], Sst[d], writes=[Sst[d]])
            S.tt("dve", h4(Stmp[d][:]), h4(Sst[d][:]), bc4(wini[:, d * 32 + g * 4:d * 32 + g * 4 + 4]), ALU.mult,
                 [Sst[d], wini], [Stmp[d]])
            S.tt("dve", Sm[:, g, :], Stmp[d][:], pacc[:, 0:256], ALU.add, [Stmp[d], pacc], [Sm])
        S.unreserve(psF); S.unreserve(psB)
    dump("SmF", SmF, [8, 256], BF16); dump("SmB", SmB, [8, 256], BF16)
    S.release(mark_pre_hTX)
    if stop == 'ctx':
        S.barrier(); S.emit(); return nc
    ynT = S.alloc("ynT", [16, 1024], BF16)
    mark_m = S.top
    accs = [S.alloc("accm%d" % i, [264], F32) for i in range(4)]
    wg2 = [S.alloc("wg2_%d" % i, [8, 768], BF16) for i in range(2)]
    xsT = S.alloc("xsT", [2, 1024], BF16); BT = S.alloc("BT", [1024], BF16); CT = S.alloc("CT", [1024], BF16)
    zs = S.alloc("zs", [8, 256], BF16)
    xb8 = S.alloc("xb8", [8, 384], BF16)
    xdt = S.alloc("xdt", [8, 2, 256], BF16)
    SinB = S.alloc("SinB", [8, 2, 256], BF16)
    xbc_ = [Tile("xb8_%d" % c, xb8[:, c, :]) for c in range(8)]
    xdc_ = [Tile("xdt_%d" % c, xdt[:, c, :, :]) for c in range(8)]
    zsc_ = [Tile("zs_%d" % c, zs[:, c, :]) for c in range(8)]
    sic_ = [[Tile("sin_%d_%d" % (c, d), SinB[:, c, d, :]) for d in range(2)] for c in range(8)]
    xdd2 = [S.alloc("xdd2_%d" % i, [256], BF16) for i in range(6)]
    Sa = [S.alloc("Sa%d" % i, [256], F32) for i in range(6)]
    Stm = [S.alloc("Stm%d" % i, [256], F32) for i in range(4)]
    stg = [S.alloc("stg%d" % i, [512], F32) for i in range(1)]
    DAt = [S.alloc("DAt%d" % i, [8, 128], BF16) for i in range(1)]
    Ee = [S.alloc("E%d" % i, [8, 128], BF16) for i in range(2)]
    Xb = [S.alloc("Xb%d" % i, [8, 128], F32) for i in range(2)]
    Gs = [S.alloc("Gs%d" % i, [128], BF16) for i in range(4)]
    Ww = [S.alloc("W%d" % i, [8, 128], BF16) for i in range(2)]
    yA = [S.alloc("yA%d" % i, [256], F32) for i in range(4)]
    yH = [[Tile("yA%d_%d" % (i, h), yA[i][:, h * 64:(h + 1) * 64]) for h in range(4)] for i in range(4)]
    Dm2 = [S.alloc("Dm%d" % i, [4, 128], BF16) for i in range(2)]
    ynk = [S.alloc("ynk%d" % i, [256], BF16) for i in range(2)]
    jk = S.alloc("jk", [256], F32)
    st2 = [S.alloc("st2_%d" % i, [4], F32) for i in range(2)]
    rr = {"sa": 0, "xd": 0, "tm": 0, "stg": 0, "ew": 0}

    def nxt(key, lst):
        rr[key] += 1
        return lst[rr[key] % len(lst)]

    def ew():
        return "dve"

    def load_wg2(g):
        wt = wg2[g % 2]
        if g > 0:
            S.dma("pool", wt[:, :, 0:256], wview(w_in, W_XS + g * 256, 256), wt, writes=[wt])
        S.dma("pool", wt[:, :, 256:384], wview(w_in, W_B + g * 128, 128), wt, writes=[wt])
        S.dma("pool", wt[:, :, 384:512], wview(w_in, W_C + g * 128, 128), wt, writes=[wt])
        S.dma("pool", wt[:, :, 512:768], wview(w_in, W_Z + g * 256, 256), wt, writes=[wt])

    def front_units(g):
        wt = wg2[g % 2]
        units = []
        for cc in range(4):
            cidx = (2 * g + cc) if cc < 2 else (16 + g if cc == 2 else 24 + g)
            dtile = xsT if cc < 2 else (BT if cc == 2 else CT)
            wt_c = wgxs0 if (g == 0 and cc < 2) else wt
            for u in range(4):
                dest = dtile[:, cc, u * 256:(u + 1) * 256] if cc < 2 else dtile[:, u * 256:(u + 1) * 256]
                if u < 2:
                    units.append(conv_unit(wt_c, cc * 128, cidx, hTPM, u * 256, 256, 0, 256, 1, 255, dest, dtile))
                else:
                    units.append(conv_unit(wt_c, cc * 128, cidx, hTPM, 512 + (u - 2) * 256, 258, 1, 256, 0, 10 ** 6,
                                           dest, dtile))

        def z_unit(cp_):
            st_ = {}

            def s0():
                pb = S.bank()
                st_["pb"] = pb
                for j in range(2):
                    ht, c0 = chunk_cols(cp_ * 2 + j)
                    for kc in range(8):
                        S.mm(pb[:, j * 256:(j + 1) * 256], ht[:, kc, c0:c0 + 128], wt[:, kc, 512:768], kc == 0, kc == 7,
                             [ht, wt], [pb])

            def s1():
                pb = st_["pb"]
                S.act(zs[:, cp_ * 2:cp_ * 2 + 2, :], pb[:, :].rearrange("p (a b) -> p a b", a=2), AF.Silu, [pb],
                      [zsc_[cp_ * 2], zsc_[cp_ * 2 + 1]])

            return [s0, s1]

        for cp_ in range(4):
            units.append(z_unit(cp_))
        return units

    load_wg2(0)
    run_pipeline(front_units(0))
    for g in range(8):
        if g + 1 < 8:
            load_wg2(g + 1)
        Dm = Dm2[g % 2]
        for h in range(4):
            S.ts("pool", Dm[:, h, :], identb[:], rows[:, 128 + g * 4 + h:129 + g * 4 + h], ALU.mult, [identb, rows], [Dm])

        def tr_chunk(c):
            st_ = {}

            def s0():
                pb = S.bank(); pbb = pb[:, :].bitcast(BF16)
                st_["pb"] = pb
                for cc in range(2):
                    S.tr(pbb[:, cc * 128:(cc + 1) * 128], xsT[:, cc, c * 128:(c + 1) * 128], identb[:], [xsT, identb], [pb])
                S.tr(pbb[:, 256:384], BT[:, c * 128:(c + 1) * 128], identb[:], [BT, identb], [pb])

            def s1():
                pb = st_["pb"]; pbb = pb[:, :].bitcast(BF16)
                S.cp("act", xbc_[c][:], pbb[:, 0:384], [pb], [xbc_[c]])
                for d in range(2):
                    S.tt("pool", h4(xdc_[c][:, d, :]), h4(xbc_[c][:, 0:256]), bc4(dtO[:, c, d * 32 + g * 4:d * 32 + g * 4 + 4]),
                         ALU.mult, [xbc_[c], dtO], [xdc_[c]])

            return [s0, s1]

        run_pipeline([tr_chunk(c) for c in range(8)])

        def mk_xd(c, d):
            xd = nxt("xd", xdd2)
            col0 = d * 32 + g * 4
            S.tt(ew(), h4(xd[:]), h4(xbc_[c][:, 0:256]), bc4(dtdec[:, c, col0:col0 + 4]), ALU.mult, [xbc_[c], dtdec], [xd])
            return xd

        def sc_mm(c, d):
            xd = mk_xd(c, d)
            pc = S.bank()
            S.mm(pc[:, 0:256], xbc_[c][:, 256:384], xd[:], True, True, [xbc_[c], xd], [pc])
            return pc

        def upd(Sold_ap, Sold_tile, c, d, pc):
            col0 = d * 32 + g * 4
            tm = nxt("tm", Stm); Snew = nxt("sa", Sa)
            S.tt(ew(), h4(tm[:]), h4(Sold_ap), bc4(cdA[:, c, col0:col0 + 4]), ALU.mult, [Sold_tile, cdA], [tm])
            S.tt("dve", Snew[:], tm[:], pc[:, 0:256], ALU.add, [tm, pc], [Snew])
            return Snew

        for sq in range(2):
            c0, c1 = 2 * sq, 2 * sq + 1
            pb = S.bank()
            for d, (ca, cb_) in enumerate(((c0, c1), (c1, c0))):
                col0 = d * 32 + g * 4
                xa = mk_xd(ca, d); xb_ = mk_xd(cb_, d); xa2 = nxt("xd", xdd2)
                S.tt(ew(), h4(xa2[:]), h4(xa[:]), bc4(cdA[:, cb_, col0:col0 + 4]), ALU.mult, [xa, cdA], [xa2])
                pc = S.bank()
                S.mm(pc[:, 0:256], xbc_[ca][:, 256:384], xa[:], True, True, [xbc_[ca], xa], [pc])
                S.cp("act", sic_[cb_][d][:], pc[:, 0:256], [pc], [sic_[cb_][d]])
                for hh in range(2):
                    osl = slice((d * 2 + hh) * 128, (d * 2 + hh + 1) * 128)
                    S.mm(pb[:, osl], xa2[:, hh * 128:(hh + 1) * 128], xbc_[ca][:, 256:384], True, False, [xa2, xbc_[ca]], [pb])
                    S.mm(pb[:, osl], xb_[:, hh * 128:(hh + 1) * 128], xbc_[cb_][:, 256:384], False, True, [xb_, xbc_[cb_]], [pb])
            sg = nxt("stg", stg)
            S.cp("act", sg[:], pb[:, :], [pb], [sg])
            for d, dst in enumerate((sF, sB)):
                for hh in range(2):
                    h0 = g * 4 + hh * 2
                    S.dma("sp", dst[sq, h0:h0 + 2].rearrange("h p n -> (h p) n"),
                          sg[:, (d * 2 + hh) * 128:(d * 2 + hh + 1) * 128], sg, reads=[sg])
        curs = [(SmF[:, g, :], SmF), (SmB[:, g, :], SmB)]
        orders = [[4, 5, 6, 7], [7, 6, 5, 4]]
        pcs = {}
        for k in range(3):
            pbk = S.bank()
            for d in range(2):
                c = orders[d][k]
                xd = mk_xd(c, d)
                S.mm(pbk[:, d * 256:(d + 1) * 256], xbc_[c][:, 256:384], xd[:], True, True, [xbc_[c], xd], [pbk])
                pcs[(k, d)] = pbk
        for k in range(4):
            for d in range(2):
                c = orders[d][k]
                cur_ap, cur_t = curs[d]
                S.cp("act", sic_[c][d][:], cur_ap, [cur_t], [sic_[c][d]])
                if k < 3:
                    col0 = d * 32 + g * 4
                    tm = nxt("tm", Stm); Snew = nxt("sa", Sa)
                    S.tt("dve", h4(tm[:]), h4(cur_ap), bc4(cdA[:, c, col0:col0 + 4]), ALU.mult, [cur_t, cdA], [tm])
                    S.tt("dve", Snew[:], tm[:], pcs[(k, d)][:, d * 256:(d + 1) * 256], ALU.add, [tm, pcs[(k, d)]], [Snew])
                    curs[d] = (Snew[:], Snew)

        def out_chunk(c):
            tc0 = c * 128
            has_f = (c % 2 == 1) if c < 4 else True
            has_b = (c % 2 == 0) if c < 4 else True
            i2 = c % 2
            st_ = {}

            bkA = [S.psum[0 + i2], S.psum[2 + i2]]; bkC = S.psum[4 + i2]; bkD = S.psum[6 + i2]

            def sA():
                pg = bkC
                st_["pg"] = pg
                S.mm(pg[:, 0:128], BT[:, tc0:tc0 + 128], CT[:, tc0:tc0 + 128], True, True, [BT, CT], [pg])
                S.cp("act", Gs[c % 4][:], pg[:, 0:128], [pg], [Gs[c % 4]])
                da_ = DAt[0]
                S.cp("act", da_[:], dabf[:, c, g * 8:(g + 1) * 8].unsqueeze(2).to_broadcast([128, 8, 128]), [dabf], [da_])
                st_["ps"] = []
                for d in range(2):
                    U2 = LEb if d == 0 else NLTb
                    NG = NEGf if d == 0 else NEGb
                    ps = bkA[d]
                    st_["ps"].append(ps)
                    for h in range(4):
                        S.mm(ps[:, h * 128:(h + 1) * 128], da_[:, d * 4 + h, :], U2[:], True, False, [da_, U2], [ps])
                        S.mm(ps[:, h * 128:(h + 1) * 128], identb[:], NG[:], False, True, [identb, NG], [ps])

            def sA2():
                x_ = Xb[i2]
                for d in range(2):
                    ps = st_["ps"][d]
                    col = g * 8 + d * 4
                    S.tt("dve", x_[:, d * 4:(d + 1) * 4, :], ps[:, :].rearrange("p (h q) -> p h q", h=4),
                         ebias[:, c, col:col + 4].unsqueeze(2).to_broadcast([128, 4, 128]), ALU.add, [ps, ebias], [x_])

            def sB():
                S.act(Ee[i2][:], Xb[i2][:], AF.Exp, [Xb[i2]], [Ee[i2]])

            def sB2():
                S.tt("dve", Ww[i2][:, 0:4, :], Ee[i2][:, 0:4, :], Gs[c % 4][:].unsqueeze(1).to_broadcast([128, 4, 128]), ALU.mult,
                     [Ee[i2], Gs[c % 4]], [Ww[i2]])
                S.tt("pool", Ww[i2][:, 4:8, :], Ee[i2][:, 4:8, :], Gs[c % 4][:].unsqueeze(1).to_broadcast([128, 4, 128]), ALU.mult,
                     [Ee[i2], Gs[c % 4]], [Ww[i2]])

            def sC():
                w_ = Ww[i2]
                py = bkC
                pyo = 128
                for h in range(4):
                    S.mm(py[:, pyo + h * 64:pyo + (h + 1) * 64], w_[:, h, :], xdc_[c][:, 0, h * 64:(h + 1) * 64], True, False,
                         [w_, xdc_[c]], [py])
                    S.mm(py[:, pyo + h * 64:pyo + (h + 1) * 64], w_[:, 4 + h, :], xdc_[c][:, 1, h * 64:(h + 1) * 64], False, False,
                         [w_, xdc_[c]], [py])
                    S.mm(py[:, pyo + h * 64:pyo + (h + 1) * 64], Dm[:, h, :], xbc_[c][:, h * 64:(h + 1) * 64], False, True,
                         [Dm, xbc_[c]], [py])
                po = bkD
                if has_f:
                    S.mm(po[:, 0:256], CT[:, tc0:tc0 + 128], sic_[c][0][:], True, True, [CT, sic_[c][0]], [po])
                if has_b:
                    S.mm(po[:, 256:512], CT[:, tc0:tc0 + 128], sic_[c][1][:], True, True, [CT, sic_[c][1]], [po])
                y = yA[c % 4]; yh = yH[c % 4]
                S.cp("act", y[:], py[:, pyo:pyo + 256], [py], [y] + yh)

            def sC2():
                po = bkD
                y = yA[c % 4]; yh = yH[c % 4]
                for (has, off, sc0) in ((has_f, 0, g * 4), (has_b, 256, 32 + g * 4)):
                    if not has:
                        continue
                    for h in range(4):
                        hs = slice(h * 64, (h + 1) * 64)
                        S.stt("dve", y[:, hs], po[:, off + h * 64:off + (h + 1) * 64], scO[:, c, sc0 + h:sc0 + h + 1],
                              y[:, hs], ALU.mult, ALU.add, [po, scO, yh[h]], [yh[h]])

            def sD():
                y = yA[c % 4]
                S.tt("pool", y[:], y[:], zsc_[c][:], ALU.mult, yH[c % 4] + [zsc_[c]], [y] + yH[c % 4])

            def sD2():
                y = yA[c % 4]
                st = st2[i2]
                S.act(jk[:], y[:], AF.Square, [y] + yH[c % 4], [jk, st], accum=st[:, 0:1])
                S.act(st[:, 1:2], st[:, 0:1], AF.Ln, [st, epsb], [st], bias=epsb[:, 0:1], scale=1.0 / 256)
                S.act(st[:, 2:3], st[:, 1:2], AF.Exp, [st], [st], scale=-0.5)
                S.act(ynk[i2][:], y[:], AF.Copy, [y, st] + yH[c % 4], [ynk[i2]], scale=st[:, 2:3])

            def sE():
                pt = bkC; ptb = pt[:, :].bitcast(BF16)
                for j in range(2):
                    S.tr(ptb[:, 768 + j * 128:768 + (j + 1) * 128], ynk[i2][:, j * 128:(j + 1) * 128], identb[:],
                         [ynk[i2], identb], [pt])
                S.cp("act", ynT[:, g * 2:g * 2 + 2, tc0:tc0 + 128], ptb[:, 768:1024].rearrange("p (a b) -> p a b", a=2),
                     [pt], [ynT])

            return [sA, sA2, sB, sB2, sC, sC2, sD, sD2, sE]

        merged = [out_chunk(c) for c in range(8)]
        if g + 1 < 8:
            merged += [[] for _ in range(5)]
            merged += front_units(g + 1)
        S.reserved.extend([S.psum[4], S.psum[5]])
        run_pipeline(merged)
        S.reserved.remove(S.psum[4]); S.reserved.remove(S.psum[5])
    dump("ynT", ynT, [16, 1024], BF16)
    S.release(mark_m)
    if stop == 'ssd':
        S.barrier(); S.emit(); return nc
    f1T = S.alloc("f1T", [8, 1024], BF16); mT = S.alloc("mT", [8, 1024], BF16)
    wC0 = S.alloc("wC_p", [8, 256], BF16); wD0 = S.alloc("wD_p", [8, 256], BF16)
    mark_t = S.top
    wA = [S.alloc("wA%d" % i, [8, 512], BF16) for i in range(2)]
    wB = [S.alloc("wB%d" % i, [8, 512], BF16) for i in range(2)]
    sgt = [S.alloc("sgt%d" % i, [512], F32) for i in range(2)]
    for cbk in range(2):
        wf = wA[cbk % 2]; wg_ = wB[cbk % 2]
        S.dma("pool", wf[:], wview(fnet_w, cbk * 512, 512), wf, writes=[wf])
        S.dma("pool", wg_[:], wview(w_in, W_GF + cbk * 512, 512), wg_, writes=[wg_])
        if cbk == 1:
            S.dma("pool", wC0[:], wview(w_in, W_MGF, 256), wC0, writes=[wC0])
            S.dma("pool", wD0[:], wview(w_in, W_MGS, 256), wD0, writes=[wD0])
        for tb in range(2):
            hc0 = 0 if tb == 0 else 513
            for c4 in range(4):
                cp_ = cbk * 4 + c4
                p1 = S.bank(); p2 = S.bank()
                for kc in range(8):
                    S.mm(p1[:, :], wf[:, kc, c4 * 128:(c4 + 1) * 128], fmT[:, kc, tb * 512:(tb + 1) * 512], kc == 0, kc == 7,
                         [wf, fmT], [p1])
                for kc in range(8):
                    S.mm(p2[:, :], wg_[:, kc, c4 * 128:(c4 + 1) * 128], hTPM[:, kc, hc0:hc0 + 512], kc == 0, kc == 7,
                         [wg_, hTPM], [p2])
                sg = sgt[c4 % 2]
                S.act(sg[:], p2[:, :], AF.Silu, [p2], [sg])
                S.tt("dve", f1T[:, cp_, tb * 512:(tb + 1) * 512], p1[:, :], sg[:], ALU.mult, [p1, sg], [f1T])
    S.release(mark_t)
    wo = [S.alloc("wo%d" % i, [8, 512], BF16) for i in range(2)]
    mark_t2 = S.top
    wA2 = [S.alloc("wA2_%d" % i, [8, 256], BF16) for i in range(2)]; wS2 = [S.alloc("wS_%d" % i, [16, 256], BF16) for i in range(2)]
    wC2 = [wC0, S.alloc("wC_1", [8, 256], BF16)]; wD2 = [wD0, S.alloc("wD_1", [8, 256], BF16)]
    nwT = S.alloc("nwT", [16], F32)
    S.dma("sp", nwT[:], nwTd, nwT, writes=[nwT])
    sga = [S.alloc("sga%d" % i, [512], F32) for i in range(2)]
    sgb = [S.alloc("sgb%d" % i, [512], F32) for i in range(2)]

    def load_t2(cbk):
        i = cbk % 2
        S.dma("pool", wA2[i][:], wview(wbf, cbk * 256, 256), wA2[i], writes=[wA2[i]])
        S.dma("pool", wS2[i][:], wview(wbs, cbk * 256, 256), wS2[i], writes=[wS2[i]])
        S.dma("pool", wC2[i][:], wview(w_in, W_MGF + cbk * 256, 256), wC2[i], writes=[wC2[i]])
        S.dma("pool", wD2[i][:], wview(w_in, W_MGS + cbk * 256, 256), wD2[i], writes=[wD2[i]])
        S.tt("dve", wS2[i][:], wS2[i][:], nwT[:].unsqueeze(2).to_broadcast([128, 16, 256]), ALU.mult, [wS2[i], nwT], [wS2[i]])

    load_t2(0)
    for cbk in range(4):
        if cbk + 1 < 4:
            load_t2(cbk + 1)
        if cbk == 1:
            for cbk2 in range(2):
                S.dma("pool", wo[cbk2][:], wview(w_out, cbk2 * 512, 512), wo[cbk2], writes=[wo[cbk2]])
        wA = wA2[cbk % 2]; wS = wS2[cbk % 2]; wC = wC2[cbk % 2]; wD = wD2[cbk % 2]
        for tb in range(2):
            hc0 = 0 if tb == 0 else 513
            tsl = slice(tb * 512, (tb + 1) * 512)
            for c4 in range(2):
                cp_ = cbk * 2 + c4
                csl = slice(c4 * 128, (c4 + 1) * 128)
                pA = S.bank(); pB = S.bank(); pC = S.bank(); pD = S.bank()
                for kc in range(8):
                    S.mm(pC[:, :], wC[:, kc, csl], hTPM[:, kc, hc0:hc0 + 512], kc == 0, kc == 7, [wC, hTPM], [pC])
                for kc in range(8):
                    S.mm(pD[:, :], wD[:, kc, csl], hTPM[:, kc, hc0:hc0 + 512], kc == 0, kc == 7, [wD, hTPM], [pD])
                for kc in range(8):
                    S.mm(pA[:, :], wA[:, kc, csl], f1T[:, kc, tsl], kc == 0, kc == 7, [wA, f1T], [pA])
                for kc in range(16):
                    S.mm(pB[:, :], wS[:, kc, csl], ynT[:, kc, tsl], kc == 0, kc == 15, [wS, ynT], [pB])
                a_ = sga[c4 % 2]; b_ = sgb[c4 % 2]
                S.act(a_[:], pC[:, :], AF.Sigmoid, [pC], [a_])
                S.act(b_[:], pD[:, :], AF.Sigmoid, [pD], [b_])
                S.tt("dve", a_[:], pA[:, :], a_[:], ALU.mult, [pA, a_], [a_])
                S.tt("dve", b_[:], pB[:, :], b_[:], ALU.mult, [pB, b_], [b_])
                S.tt("dve", mT[:, cp_, tsl], a_[:], b_[:], ALU.add, [a_, b_], [mT])
    S.release(mark_t2)
    xt3 = [S.alloc("xt3_%d" % i, [D], F32) for i in range(3)]
    tmp3 = [S.alloc("tmp3_%d" % i, [D], F32) for i in range(2)]
    ot3 = [S.alloc("ot3_%d" % i, [D], F32) for i in range(2)]
    st3 = [S.alloc("st3_%d" % i, [8], F32) for i in range(2)]
    jk3 = S.alloc("jk3", [512], F32)
    def t3_tile(ti):
        src = xP[ti * 128:(ti + 1) * 128, :] if ti < 4 else xM[(ti - 4) * 128:(ti - 3) * 128, :]
        dst = yP[ti * 128:(ti + 1) * 128, :] if ti < 4 else yM[(ti - 4) * 128:(ti - 3) * 128, :]
        cond = 0 if ti < 4 else 1
        xt = xt3[ti % 3]; ot = ot3[ti % 2]; st = st3[ti % 2]; tmp = tmp3[ti % 2]
        st_ = {}

        def s0():
            S.dma("sp", xt[:], src, xt, writes=[xt])
            pO = [S.bank(), S.bank()]
            st_["pO"] = pO
            for cbk in range(2):
                for kc in range(8):
                    S.mm(pO[cbk][:, :], mT[:, kc, ti * 128:(ti + 1) * 128], wo[cbk][:, kc, :], kc == 0, kc == 7,
                         [mT, wo[cbk]], [pO[cbk]])

        def s1():
            pO = st_["pO"]
            for cbk in range(2):
                S.act(jk3[:], pO[cbk][:, :], AF.Square, [pO[cbk]], [jk3, st], accum=st[:, cbk:cbk + 1])
            S.tt("dve", st[:, 2:3], st[:, 0:1], st[:, 1:2], ALU.add, [st], [st])
            S.act(st[:, 3:4], st[:, 2:3], AF.Ln, [st, epsb], [st], bias=epsb[:, 0:1], scale=1.0 / D)
            S.act(st[:, 4:5], st[:, 3:4], AF.Exp, [st], [st], scale=-0.5)

        def s2():
            pO = st_["pO"]
            for cbk in range(2):
                sl = slice(cbk * 512, (cbk + 1) * 512)
                S.stt("dve", tmp[:, sl], pO[cbk][:, :], st[:, 4:5], g2[:, cond, sl], ALU.mult, ALU.mult,
                      [pO[cbk], st, g2], [tmp])
                S.tt("dve", ot[:, sl], tmp[:, sl], xt[:, sl], ALU.add, [tmp, xt], [ot])

        def s3():
            S.dma("sp", dst, ot[:], ot, reads=[ot])

        return [s0, s1, s2, s3]

    run_pipeline([t3_tile(ti) for ti in range(8)])
    S.barrier()
    S.emit()
    return nc


def _dft_tables():
    l = np.arange(2048)
    r, c = l // 64, l % 64
    ph = (np.outer(r, r) / 32.0 + np.outer(c, c) / 64.0)
    ang = 2.0 * np.pi * (ph % 1.0)
    sc = 1.0 / math.sqrt(2048.0)
    cpos = (np.cos(ang) * sc).astype(np.float32)
    spos = (np.sin(ang) * sc).astype(np.float32)
    k = np.arange(256)
    a2 = 2.0 * np.pi * ((np.outer(k, k) % 256) / 256.0)
    c256 = (np.cos(a2) / 16.0).astype(np.float32)
    s256 = (np.sin(a2) / 16.0).astype(np.float32)
    return cpos, spos, c256, s256


def prep_inputs(x_prompt, x_sample, c, state_ssd_fwd, state_ssd_bwd, c_ctx, ada_w, ada_b, pre_norm_w,
                post_norm_w, w_in, conv_w, conv_b, dt_bias_fwd, dt_bias_bwd, a_log_fwd, a_log_bwd, d_skip,
                ssd_norm_w, fnet_w, w_branch_f, w_branch_s, w_out):
    f = lambda a: np.ascontiguousarray(np.asarray(a, dtype=np.float32))
    bf = lambda a: np.ascontiguousarray(a.astype(ml_dtypes.bfloat16))
    x_prompt, x_sample, c, c_ctx = f(x_prompt), f(x_sample), f(c), f(c_ctx)
    sf, sb = f(state_ssd_fwd), f(state_ssd_bwd)
    cpos, spos, c256, s256 = _dft_tables()
    t256 = bf(np.stack([c256, -s256, s256]))
    cw = f(conv_w)[0].reshape(3, 32, 128).transpose(2, 1, 0).reshape(128, 96)
    cb = f(conv_b)[0].reshape(32, 128).T
    rowsm = np.concatenate([f(dt_bias_fwd)[0], f(dt_bias_bwd)[0], f(a_log_fwd)[0], f(a_log_bwd)[0],
                            f(d_skip)[0]])[None, :]
    rowbig = np.concatenate([f(ssd_norm_w)[0], f(pre_norm_w)[0], f(post_norm_w)[0]])[None, :]
    shared = {
        "t256": t256, "cw": f(cw), "cb": f(cb), "rowsm": f(rowsm), "rowbig": f(rowbig), "nwT": f(f(ssd_norm_w)[0].reshape(16, 128).T),
        "ada_w": f(ada_w)[0], "ada_b": f(ada_b), "w_in": f(w_in)[0], "fnet_w": f(fnet_w)[0],
        "wbf": f(w_branch_f)[0], "wbs": f(w_branch_s)[0], "w_out": f(w_out)[0],
    }
    maps = []
    zero_row = np.zeros((1, D), np.float32)
    for core in range(8):
        b, q = core // 4, core % 4
        xs = x_sample[b]
        hl = xs[q * 512 - 1:q * 512] if q > 0 else zero_row
        hr = xs[(q + 1) * 512:(q + 1) * 512 + 1] if q < 3 else zero_row
        flags = np.zeros((1, 8), np.float32)
        flags[0, 0] = 1.0 if q > 0 else 0.0
        flags[0, 1] = 1.0 if q < 3 else 0.0
        flags[0, 2 + q] = 1.0
        cc = np.stack([c_ctx, c[b]])
        cT = cc.reshape(2, 8, 128).transpose(2, 1, 0).reshape(128, 16)
        st = np.stack([sf[b, 0], sb[b, 0]])
        stT = st.reshape(2, 2048, 128).transpose(0, 2, 1)
        m = dict(shared)
        m.update({
            "xX": f(xs), "xP": f(x_prompt[2 * core:2 * core + 2].reshape(512, D)),
            "xM": f(xs[q * 512:(q + 1) * 512]), "xH": f(np.concatenate([hl, hr], 0)),
            "fl": flags, "cT": f(cT), "stT": f(stT),
            "tcs": bf(np.stack([cpos[:, q * 512:(q + 1) * 512], spos[:, q * 512:(q + 1) * 512]])),
        })
        maps.append(m)
    return maps


_NC_CACHE = {}


def kernel(**inputs):
    maps = prep_inputs(**inputs)
    if "nc" not in _NC_CACHE:
        _NC_CACHE["nc"] = build(False)
    res = run_bass_kernel_spmd(_NC_CACHE["nc"], maps, core_ids=list(range(8)))
    r = res.results
    y_prompt = np.concatenate([r[i]["yP"].reshape(2, 256, D) for i in range(8)], 0).astype(np.float32)
    y_sample = np.stack([np.concatenate([r[4 * b + q]["yM"] for q in range(4)], 0) for b in range(2)]).astype(np.float32)
    nf = np.concatenate([r[i]["sF"] for i in range(8)], 0)[:, None].astype(np.float32)
    nb = np.concatenate([r[i]["sB"] for i in range(8)], 0)[:, None].astype(np.float32)
    return (y_prompt, y_sample, nf, nb)
```

```python
import math
import numpy as np
import ml_dtypes
import concourse.bass as bass
import concourse.mybir as mybir
from concourse.bass_utils import run_bass_kernel_spmd

F32 = mybir.dt.float32
BF16 = mybir.dt.bfloat16
AF = mybir.ActivationFunctionType
ALU = mybir.AluOpType

SAME_ENGINE_SYNC = True
EPS = 1e-6
D = 1024
NX, NP_, NM = 2048, 512, 512
COL_X, COL_P, COL_ME = 0, 2048, 2560
NCOL = 2048 + 512 + 514
W_UF, W_GF, W_Z, W_XS, W_B, W_C, W_DTF, W_DTB, W_MGF, W_MGS = 0, 1024, 2048, 4096, 6144, 7168, 8192, 8224, 8256, 9280


class StopBuild(Exception):
    pass


class Tile:
    __slots__ = ("name", "t", "last_write", "reads", "dsem")

    def __init__(self, name, t):
        self.name = name
        self.t = t
        self.last_write = None
        self.reads = {}
        self.dsem = None

    def __getitem__(self, k):
        return self.t[k]


class Sched:
    ENG = ("pe", "act", "dve", "pool", "sp")

    def __init__(self, nc, arena_words):
        self.nc = nc
        self.streams = {e: [] for e in self.ENG}
        self.count = {}
        self.seen = {e: {} for e in self.ENG}
        self.sems = {}
        self.ndsem = 0
        for e in self.ENG:
            self._mksem(e)
        self.arena = nc.alloc_sbuf_tensor("arena", [128, arena_words], F32)
        self.arena_words = arena_words
        self.top = 0
        self.psum = [Tile("ps%d" % i, nc.alloc_psum_tensor("ps%d" % i, [128, 512], F32)) for i in range(8)]
        self.psi = 0
        self.reserved = []
        self.nops = 0

    def _mksem(self, key):
        self.sems[key] = self.nc.alloc_semaphore(name="s_" + key)
        self.count[key] = 0

    def alloc(self, name, free_shape, dt):
        n = int(np.prod(free_shape))
        words = (n * (2 if dt == BF16 else 4) + 3) // 4
        words = (words + 7) // 8 * 8
        assert self.top + words <= self.arena_words, (name, self.top, words)
        ap = self.arena[:, self.top:self.top + words]
        self.top += words
        if dt == BF16:
            ap = ap.bitcast(BF16)[:, 0:n]
        else:
            ap = ap[:, 0:n]
        if len(free_shape) == 2:
            ap = ap.rearrange("p (a b) -> p a b", a=free_shape[0])
        elif len(free_shape) == 3:
            ap = ap.rearrange("p (a b c) -> p a b c", a=free_shape[0], b=free_shape[1])
        return Tile(name, ap)

    def bank(self):
        while True:
            t = self.psum[self.psi]
            self.psi = (self.psi + 1) % 8
            if t not in self.reserved:
                return t

    def reserve_bank(self):
        t = self.bank()
        self.reserved.append(t)
        return t

    def unreserve(self, t):
        self.reserved.remove(t)

    def _deps(self, reads, writes):
        deps = []
        for t in reads:
            if t.last_write is not None:
                deps.append(t.last_write)
        for t in writes:
            if t.last_write is not None:
                deps.append(t.last_write)
            deps.extend(t.reads.items())
        return deps

    def _emit_waits(self, eng, deps, skip_own=False):
        need = {}
        for key, val in deps:
            if key == eng and (skip_own or not SAME_ENGINE_SYNC):
                continue
            if self.seen[eng].get(key, 0) >= val:
                continue
            if need.get(key, 0) < val:
                need[key] = val
        for key, val in need.items():
            self.seen[eng][key] = val
            sem = self.sems[key]
            self.streams[eng].append(lambda e, sem=sem, val=val: e.wait_ge(sem, val))

    def _post(self, tk, reads, writes):
        for t in reads:
            if t.reads.get(tk[0], 0) < tk[1]:
                t.reads[tk[0]] = tk[1]
        for t in writes:
            t.last_write = tk
            t.reads = {}

    def op(self, eng, fn, reads=(), writes=()):
        self._emit_waits(eng, self._deps(reads, writes), skip_own=(eng == "pe"))
        self.count[eng] += 1
        tk = (eng, self.count[eng])
        sem = self.sems[eng]
        self.streams[eng].append(lambda e, fn=fn, sem=sem: fn(e).then_inc(sem, 1))
        self._post(tk, reads, writes)
        self.nops += 1
        return tk

    def dma(self, q, out_ap, in_ap, sem_tile, reads=(), writes=()):
        self._emit_waits(q, self._deps(reads, writes))
        st = sem_tile
        if st.dsem is None:
            st.dsem = "d%d" % self.ndsem
            self.ndsem += 1
            self._mksem(st.dsem)
        key = st.dsem
        self.count[key] += 16
        tk = (key, self.count[key])
        sem = self.sems[key]
        self.streams[q].append(lambda e, o=out_ap, i=in_ap, sem=sem: e.dma_start(out=o, in_=i).then_inc(sem, 16))
        self._post(tk, reads, writes)
        return tk

    def barrier(self):
        tks = [(k, v) for k, v in self.count.items() if v > 0]
        for e in self.ENG:
            self._emit_waits(e, tks)

    def release(self, mark):
        self.barrier()
        self.top = mark

    def mm(self, out, lhsT, rhs, start, stop, reads, writes):
        return self.op("pe", lambda e: e.matmul(out, lhsT=lhsT, rhs=rhs, start=start, stop=stop), reads, writes)

    def tr(self, out, in_, ident, reads, writes):
        return self.op("pe", lambda e: e.transpose(out=out, in_=in_, identity=ident), reads, writes)

    def act(self, out, in_, func, reads, writes, bias=None, scale=None, accum=None):
        kw = {}
        if bias is not None:
            kw["bias"] = bias
        if scale is not None:
            kw["scale"] = scale
        if accum is not None:
            kw["accum_out"] = accum
        return self.op("act", lambda e: e.activation(out=out, in_=in_, func=func, **kw), reads, writes)

    def tt(self, eng, out, in0, in1, op, reads, writes):
        return self.op(eng, lambda e: e.tensor_tensor(out=out, in0=in0, in1=in1, op=op), reads, writes)

    def ts(self, eng, out, in0, s1, op0, reads, writes, s2=None, op1=None):
        if op1 is None:
            return self.op(eng, lambda e: e.tensor_scalar(out=out, in0=in0, scalar1=s1, scalar2=None, op0=op0),
                           reads, writes)
        return self.op(eng, lambda e: e.tensor_scalar(out=out, in0=in0, scalar1=s1, scalar2=s2, op0=op0, op1=op1),
                       reads, writes)

    def stt(self, eng, out, in0, scalar, in1, op0, op1, reads, writes):
        return self.op(eng, lambda e: e.scalar_tensor_tensor(out=out, in0=in0, scalar=scalar, in1=in1,
                                                              op0=op0, op1=op1), reads, writes)

    def cp(self, eng, out, in_, reads, writes):
        if eng == "act":
            return self.op("act", lambda e: e.copy(out=out, in_=in_), reads, writes)
        return self.op(eng, lambda e: e.tensor_copy(out=out, in_=in_), reads, writes)

    def memset(self, eng, ap, val, writes):
        return self.op(eng, lambda e: e.memset(ap, val), (), writes)

    def emit(self):
        nc = self.nc
        with nc.Block() as block:
            @block.tensor
            def _(e):
                for f in self.streams["pe"]:
                    f(e)

            @block.scalar
            def _(e):
                for f in self.streams["act"]:
                    f(e)

            @block.vector
            def _(e):
                for f in self.streams["dve"]:
                    f(e)

            @block.gpsimd
            def _(e):
                for f in self.streams["pool"]:
                    f(e)

            @block.sync
            def _(e):
                for f in self.streams["sp"]:
                    f(e)


ARENA_WORDS = 53000


def run_pipeline(units):
    if not units:
        return
    nst = max(len(u) for u in units)
    for step in range(len(units) + nst - 1):
        for si in range(nst):
            k = step - si
            if 0 <= k < len(units) and si < len(units[k]):
                units[k][si]()

NROW_SM = 160
PROMPT_TOK = [(COL_P + i * 128) for i in range(4)]


def build(dbg=False, stop=None):
    import os
    stop = stop or os.environ.get('KSTOP')
    nc = bass.Bass("TRN2", target_bir_lowering=False)
    S = Sched(nc, ARENA_WORDS)

    def din(name, shape, dt=F32):
        return nc.dram_tensor(name, list(shape), dt, kind="ExternalInput").ap()

    def dout(name, shape):
        return nc.dram_tensor(name, list(shape), F32, kind="ExternalOutput").ap()

    xX = din("xX", [2048, D]); xP = din("xP", [512, D]); xM = din("xM", [512, D]); xH = din("xH", [2, D])
    fl = din("fl", [1, 8]); cT = din("cT", [128, 16]); stT = din("stT", [2, 128, 2048])
    tcs = din("tcs", [2, 2048, 512], BF16); t256 = din("t256", [3, 256, 256], BF16)
    cwd = din("cw", [128, 96]); cbd = din("cb", [128, 32]); rowsm = din("rowsm", [1, NROW_SM])
    nwTd = din("nwT", [128, 16])
    rowbig = din("rowbig", [1, 4096])
    ada_w = din("ada_w", [D, 3072]); ada_b = din("ada_b", [1, 3072]); w_in = din("w_in", [D, 10304])
    fnet_w = din("fnet_w", [D, D]); wbf = din("wbf", [D, D]); wbs = din("wbs", [2048, D]); w_out = din("w_out", [D, D])
    yP = dout("yP", [512, D]); yM = dout("yM", [512, D])
    sF = dout("sF", [2, 32, 64, 128]); sB = dout("sB", [2, 32, 64, 128])
    def dump(name, tile, shape, dt=F32):
        if not dbg:
            return
        d = nc.dram_tensor("dbg_" + name, [128] + list(shape), dt, kind="ExternalOutput").ap()
        S.dma("sp", d, tile[:], tile, reads=[tile])

    def wview(w, c0, n):
        return w[:, c0:c0 + n].rearrange("(kc p) n -> p kc n", p=128)

    identb = S.alloc("identb", [128], BF16); identf = S.alloc("identf", [128], F32)
    LEf = S.alloc("LEf", [128], F32); onesf = S.alloc("onesf", [128], F32)
    LEb = S.alloc("LEb", [128], BF16); GEb = S.alloc("GEb", [128], BF16)
    GTb = S.alloc("GTb", [128], BF16); LTb = S.alloc("LTb", [128], BF16)
    epsb = S.alloc("epsb", [1], F32); oneb = S.alloc("oneb", [1], F32)
    sel2 = S.alloc("sel2", [256], F32)
    cw = S.alloc("cw", [96], F32); cb = S.alloc("cb", [32], F32)
    rows = S.alloc("rows", [NROW_SM], F32); flg = S.alloc("flg", [8], F32)
    Arow = S.alloc("Arow", [64], F32)
    g2 = S.alloc("g2", [2, D], F32)
    hTPM = S.alloc("hTPM", [8, 1026], BF16)
    fmT = S.alloc("fmT", [8, 1024], BF16)
    dtO = S.alloc("dtO", [8, 64], F32); scO = S.alloc("scO", [8, 64], F32)
    dtdec = S.alloc("dtdec", [8, 64], F32); cdA = S.alloc("cdA", [8, 64], F32)
    wini = S.alloc("wini", [64], F32)
    dabf = S.alloc("dabf", [8, 64], BF16); ebias = S.alloc("ebias", [8, 64], F32)
    NEGf = S.alloc("NEGf", [128], BF16); NEGb = S.alloc("NEGb", [128], BF16); NLTb = S.alloc("NLTb", [128], BF16)
    SmF = S.alloc("SmF", [8, 256], BF16); SmB = S.alloc("SmB", [8, 256], BF16)
    wgxs0 = S.alloc("wgxs0", [8, 256], BF16)
    mark_pre_hTX = S.top
    hTX = S.alloc("hTX", [8, 2048], BF16)
    dtdecX = S.alloc("dtdecX", [16, 64], F32)
    mark_base = S.top

    off_wgc0 = S.top
    wgc0 = S.alloc("wgc0", [8, 384], BF16)
    xpre = Tile("xpre", S.arena[:, off_wgc0:off_wgc0 + D])
    mark_ctx = S.top
    wdt = S.alloc("wdt", [8, 64], BF16)
    wuf0 = S.alloc("wuf0", [8, 256], BF16)
    t2 = S.alloc("t2", [3, 2, 256], BF16)
    mark_base2 = S.top
    tmpf = S.alloc("tmpf", [128], F32)

    def tri(dst_bf, pattern_step, cm, cmp, also_f32=None):
        S.memset("pool", tmpf[:], 1.0, [tmpf])
        S.op("pool", lambda e: e.affine_select(out=tmpf[:], in_=tmpf[:], pattern=[[pattern_step, 128]],
                                               compare_op=cmp, fill=0.0, base=0, channel_multiplier=cm),
             [tmpf], [tmpf])
        if dst_bf is not None:
            S.cp("pool", dst_bf[:], tmpf[:], [tmpf], [dst_bf])
        if also_f32 is not None:
            S.cp("pool", also_f32[:], tmpf[:], [tmpf], [also_f32])

    tri(identb, -1, 1, ALU.is_equal, identf)
    tri(LEb, 1, -1, ALU.is_ge, LEf)
    tri(GEb, -1, 1, ALU.is_ge)
    tri(GTb, -1, 1, ALU.is_gt)
    tri(LTb, 1, -1, ALU.is_gt)
    S.ts("pool", NEGf[:], GTb[:], -30000.0, ALU.mult, [GTb], [NEGf])
    S.ts("pool", NEGb[:], LTb[:], -30000.0, ALU.mult, [LTb], [NEGb])
    S.ts("pool", NLTb[:], LTb[:], -1.0, ALU.mult, [LTb], [NLTb])
    S.memset("pool", onesf[:], 1.0, [onesf])
    S.memset("pool", epsb[:], EPS, [epsb])
    S.memset("pool", oneb[:], 1.0, [oneb])
    S.memset("pool", sel2[:], 1.0, [sel2])
    S.op("pool", lambda e: e.affine_select(out=sel2[:], in_=sel2[:], pattern=[[1, 256]], compare_op=ALU.is_ge,
                                           fill=0.0, base=0, channel_multiplier=-128), [sel2], [sel2])
    S.op("pool", lambda e: e.affine_select(out=sel2[:], in_=sel2[:], pattern=[[-1, 256]], compare_op=ALU.is_ge,
                                           fill=0.0, base=127, channel_multiplier=128), [sel2], [sel2])
    S.dma("sp", cw[:], cwd, cw, writes=[cw])
    S.dma("sp", cb[:], cbd, cb, writes=[cb])
    S.dma("sp", rows[:], rowsm.partition_broadcast(128), rows, writes=[rows])
    S.dma("sp", flg[:], fl.partition_broadcast(128), flg, writes=[flg])
    S.act(Arow[:], rows[:, 64:128], AF.Exp, [rows], [Arow])
    S.ts("dve", Arow[:], Arow[:], -1.0, ALU.mult, [Arow], [Arow])

    if stop == 'const':
        dump('rows', rows, [NROW_SM]); S.barrier(); S.emit(); return nc
    g1 = S.alloc("g1", [2, D], F32); shf = S.alloc("shf", [2, D], F32)
    mark1 = S.top
    scT = S.alloc("scT", [8, 2], F32)
    ada2 = S.alloc("ada2", [3072], F32)
    prew = S.alloc("prew", [D], F32); postw = S.alloc("postw", [D], F32)
    wst = [S.alloc("wst%d" % i, [8, 512], F32) for i in range(3)]
    S.dma("sp", scT[:].rearrange("p a b -> p (a b)"), cT, scT, writes=[scT])
    S.dma("sp", ada2[0:2, :], ada_b.partition_broadcast(2), ada2, writes=[ada2])
    S.dma("sp", prew[:], rowbig[:, 2048:3072].partition_broadcast(128), prew, writes=[prew])
    S.dma("sp", postw[:], rowbig[:, 3072:4096].partition_broadcast(128), postw, writes=[postw])
    S.act(scT[:], scT[:], AF.Silu, [scT], [scT])
    for cbk in range(6):
        wt = wst[cbk % 3]
        S.dma(("sp", "act")[cbk % 2], wt[:], wview(ada_w, cbk * 512, 512), wt, writes=[wt])
        pb = S.bank()
        for kc in range(8):
            S.mm(pb[0:2, :], scT[:, kc, :], wt[:, kc, :], kc == 0, kc == 7, [scT, wt], [pb])
        S.tt("dve", ada2[0:2, cbk * 512:(cbk + 1) * 512], pb[0:2, :], ada2[0:2, cbk * 512:(cbk + 1) * 512],
             ALU.add, [pb, ada2], [ada2])
    S.dma("sp", xpre[:], xX[0:128, :], xpre, writes=[xpre])
    if stop == 'p0a':
        dump('ada2', ada2, [3072]); S.barrier(); S.emit(); return nc
    S.dma("pool", wdt[:], wview(w_in, W_DTF, 64), wdt, writes=[wdt])
    S.dma("pool", wuf0[:], wview(w_in, W_UF, 256), wuf0, writes=[wuf0])
    S.dma("sp", t2[:], t256.rearrange("w (c p) n -> p w c n", p=128), t2, writes=[t2])
    for c in range(2):
        for cbk in range(6):
            pb = S.bank()
            S.mm(pb[:, :], sel2[0:2, c * 128:(c + 1) * 128], ada2[0:2, cbk * 512:(cbk + 1) * 512], True, True,
                 [sel2, ada2], [pb])
            sl = slice((cbk % 2) * 512, (cbk % 2) * 512 + 512)
            if cbk < 2:
                S.cp("act", shf[:, c, sl], pb[:, :], [pb], [shf])
            elif cbk < 4:
                S.stt("dve", g1[:, c, sl], pb[:, :], 1.0, prew[:, sl], ALU.add, ALU.mult, [pb, prew], [g1])
            else:
                S.tt("dve", g2[:, c, sl], pb[:, :], postw[:, sl], ALU.mult, [pb, postw], [g2])
    S.release(mark1)

    if stop == 'p0':
        dump('g2', g2, [2, D]); S.barrier(); S.emit(); return nc
    xbuf = [xpre] + [S.alloc("xbuf%d" % i, [D], F32) for i in range(1, 5)]
    xhalo = S.alloc("xhalo", [D], F32)
    tmpb = [S.alloc("tmpb%d" % i, [D], F32) for i in range(2)]
    hbb = [S.alloc("hbb%d" % i, [D], BF16) for i in range(2)]
    stat = [S.alloc("stat%d" % i, [4], F32) for i in range(4)]
    junk = S.alloc("junk", [D], F32)
    S.memset("pool", xhalo[:], 0.0, [xhalo])
    tiles = []
    for i in range(16):
        tiles.append((xX[i * 128:(i + 1) * 128, :], 128, 1, ("X", i)))
    for i in range(4):
        tiles.append((xP[i * 128:(i + 1) * 128, :], 128, 0, ("P", i)))
    for i in range(4):
        tiles.append((xM[i * 128:(i + 1) * 128, :], 128, 1, ("M", i)))
    tiles.append((xH, 2, 1, ("H", 0)))
    def p1_tile(ti, src, nrow, cond, kind, idx):
        xt = xhalo if kind == "H" else xbuf[ti % 5]
        st = stat[ti % 4]; tb = tmpb[ti % 2]; hb = hbb[ti % 2]
        st_ = {}

        def s0():
            if ti > 0:
                S.dma("sp", xt[0:nrow, :], src, xt, writes=[xt])

        def s0b():
            S.act(junk[:], xt[:], AF.Square, [xt], [junk, st], accum=st[:, 0:1])

        def s0c():
            S.act(st[:, 1:2], st[:, 0:1], AF.Ln, [st, epsb], [st], bias=epsb[:, 0:1], scale=1.0 / D)

        def s0d():
            S.act(st[:, 2:3], st[:, 1:2], AF.Exp, [st], [st], scale=-0.5)

        def s1():
            S.stt("dve", tb[:], xt[:], st[:, 2:3], g1[:, cond, :], ALU.mult, ALU.mult, [xt, st, g1], [tb])

        def s1b():
            S.tt("dve", hb[:], tb[:], shf[:, cond, :], ALU.add, [tb, shf], [hb])

        def s2():
            pb = S.bank()
            st_["pb"] = pb
            pbb = pb[:, :].bitcast(BF16)
            for kc in range(8):
                S.tr(pbb[:, kc * 128:(kc + 1) * 128], hb[:, kc * 128:(kc + 1) * 128], identb[:], [hb, identb], [pb])

        def s3():
            pb = st_["pb"]
            pv = pb[:, :].bitcast(BF16).rearrange("p (a b) -> p a b", a=8)
            if kind == "X":
                S.cp("act", hTX[:, :, idx * 128:(idx + 1) * 128], pv, [pb], [hTX])
            elif kind == "P":
                S.cp("act", hTPM[:, :, idx * 128:(idx + 1) * 128], pv, [pb], [hTPM])
            elif kind == "M":
                S.cp("act", hTPM[:, :, 513 + idx * 128:513 + (idx + 1) * 128], pv, [pb], [hTPM])
            else:
                S.ts("dve", hTPM[:, :, 512:513], pv[:, :, 0:1], flg[:, 0:1], ALU.mult, [pb, flg], [hTPM])
                S.ts("dve", hTPM[:, :, 1025:1026], pv[:, :, 1:2], flg[:, 1:2], ALU.mult, [pb, flg], [hTPM])

        return [s0, s0b, s0c, s0d, s1, s1b, s2, s3]

    run_pipeline([p1_tile(ti, src, nrow, cond, kind, idx) for ti, (src, nrow, cond, (kind, idx)) in enumerate(tiles)])
    S.release(mark_base2)
    dump("hTX", hTX, [8, 2048], BF16)
    dump("hTPM", hTPM, [8, 1026], BF16)
    dump("g2", g2, [2, D])
    if stop == 'p1':
        S.barrier(); S.emit(); return nc
    mark_dt = mark_base
    t_dtr = S.alloc("t_dtr", [8, 64], F32); t_e = S.alloc("t_e", [8, 64], F32); t_dt = S.alloc("t_dt", [8, 64], F32)
    t_da = S.alloc("t_da", [8, 64], F32); t_tot = S.alloc("t_tot", [8, 64], F32); t_x = S.alloc("t_x", [8, 64], F32)
    t_dec = S.alloc("t_dec", [8, 64], F32); t_y = S.alloc("t_y", [8, 64], F32)
    totX = S.alloc("totX", [16, 64], F32); Ppre = S.alloc("Ppre", [17, 64], F32)
    omg = S.alloc("omg", [16, 64], F32); t_o = S.alloc("t_o", [16, 32], F32); t_o2 = S.alloc("t_o2", [16, 32], F32)

    def chunk_cols(ci):
        if ci < 4:
            return hTPM, ci * 128
        if ci < 8:
            return hTPM, 513 + (ci - 4) * 128
        return hTX, (ci - 8) * 128

    def v8(ap):
        return ap.rearrange("p (a b) -> p a b", a=8)

    def dtA(bi):
        pb = S.bank()
        for j in range(8):
            ht, c0 = chunk_cols(bi * 8 + j)
            for kc in range(8):
                S.mm(pb[:, j * 64:(j + 1) * 64], ht[:, kc, c0:c0 + 128], wdt[:, kc, :], kc == 0, kc == 7, [ht, wdt], [pb])
        S.tt("dve", t_dtr[:], v8(pb[:, :]), rows[:, 0:64].unsqueeze(1).to_broadcast([128, 8, 64]), ALU.add,
             [pb, rows], [t_dtr])
        S.act(t_e[:], t_dtr[:], AF.Exp, [t_dtr], [t_e])
        S.act(t_dt[:], t_e[:], AF.Ln, [t_e, oneb], [t_dt], bias=oneb[:, 0:1])
        S.tt("pool", t_da[:], t_dt[:], Arow[:].unsqueeze(1).to_broadcast([128, 8, 64]), ALU.mult, [t_dt, Arow], [t_da])

    def dtB(bi):
        pi = S.bank(); po = S.bank()
        da_flat = t_da[:].rearrange("p a b -> p (a b)")
        S.mm(pi[:, :], LEf[:], da_flat, True, True, [LEf, t_da], [pi])
        S.mm(po[:, :], onesf[:], da_flat, True, True, [onesf, t_da], [po])
        piv = v8(pi[:, :]); pov = v8(po[:, :])
        S.cp("act", t_tot[:], pov, [po], [t_tot])
        if bi == 0:
            S.act(cdA[:], t_tot[:], AF.Exp, [t_tot], [cdA])
        S.tt("dve", t_x[:, :, 0:32], t_tot[:, :, 0:32], piv[:, :, 0:32], ALU.subtract, [t_tot, pi], [t_x])
        S.tt("dve", t_x[:, :, 32:64], piv[:, :, 32:64], t_da[:, :, 32:64], ALU.subtract, [pi, t_da], [t_x])
        S.act(t_dec[:], t_x[:], AF.Exp, [t_x], [t_dec])
        if bi == 0:
            S.tt("pool", dtdec[:], t_dt[:], t_dec[:], ALU.mult, [t_dt, t_dec], [dtdec])
        else:
            S.tt("pool", dtdecX[:, (bi - 1) * 8:bi * 8, :], t_dt[:], t_dec[:], ALU.mult, [t_dt, t_dec], [dtdecX])
        if bi > 0:
            S.cp("pool", totX[:, (bi - 1) * 8:bi * 8, :], t_tot[:], [t_tot], [totX])
        if bi == 0:
            S.cp("pool", dtO[:], t_dt[:], [t_dt], [dtO])
            S.act(scO[:, :, 0:32], piv[:, :, 0:32], AF.Exp, [pi], [scO])
            dav = dabf[:].rearrange("p c (g d h) -> p c g d h", g=8, d=2, h=4)
            ebv = ebias[:].rearrange("p c (g d h) -> p c g d h", g=8, d=2, h=4)
            for d_ in range(2):
                S.cp("pool", dav[:, :, :, d_, :], t_da[:, :, d_ * 32:(d_ + 1) * 32].rearrange("p c (g h) -> p c g h", g=8),
                     [t_da], [dabf])
            for d_ in range(2):
                S.cp("pool", t_y[:, :, d_ * 32:(d_ + 1) * 32].rearrange("p c (g h) -> p c g h", g=8), dav[:, :, :, d_, :],
                     [dabf], [t_y])
            pr = S.bank()
            S.mm(pr[:, :], LEf[:], t_y[:].rearrange("p a b -> p (a b)"), True, True, [LEf, t_y], [pr])
            prv = v8(pr[:, :])
            S.ts("dve", ebv[:, :, :, 0, :], prv[:, :, 0:32].rearrange("p c (g h) -> p c g h", g=8), -1.0, ALU.mult, [pr], [ebias])
            S.tt("dve", ebv[:, :, :, 1, :], prv[:, :, 32:64].rearrange("p c (g h) -> p c g h", g=8),
                 t_y[:, :, 32:64].rearrange("p c (g h) -> p c g h", g=8), ALU.subtract, [pr, t_y], [ebias])
            S.tt("dve", t_y[:, :, 32:64], t_tot[:, :, 32:64], t_x[:, :, 32:64], ALU.subtract, [t_tot, t_x], [t_y])
            S.act(scO[:, :, 32:64], t_y[:, :, 32:64], AF.Exp, [t_y], [scO])

    def dtOmega():
        S.memset("pool", Ppre[:, 0:1, :], 0.0, [Ppre])
        for k in range(16):
            S.tt("dve", Ppre[:, k + 1, :], Ppre[:, k, :], totX[:, k, :], ALU.add, [Ppre, totX], [Ppre])
        S.memset("pool", omg[:], 0.0, [omg])
        S.memset("pool", wini[:], 0.0, [wini])
        for j in range(1, 4):
            n = 4 * j
            S.tt("dve", t_o[:, 0:n, :], Ppre[:, n:n + 1, 0:32].to_broadcast([128, n, 32]), Ppre[:, 1:n + 1, 0:32],
                 ALU.subtract, [Ppre], [t_o])
            S.act(t_o2[:, 0:n, :], t_o[:, 0:n, :], AF.Exp, [t_o], [t_o2])
            S.stt("dve", omg[:, 0:n, 0:32], t_o2[:, 0:n, :], flg[:, 2 + j:3 + j], omg[:, 0:n, 0:32], ALU.mult, ALU.add,
                  [t_o2, flg, omg], [omg])
        for j in range(0, 3):
            b0 = 4 * (j + 1); n = 16 - b0
            S.tt("dve", t_o[:, 0:n, :], Ppre[:, b0:16, 32:64], Ppre[:, b0:b0 + 1, 32:64].to_broadcast([128, n, 32]),
                 ALU.subtract, [Ppre], [t_o])
            S.act(t_o2[:, 0:n, :], t_o[:, 0:n, :], AF.Exp, [t_o], [t_o2])
            S.stt("dve", omg[:, b0:16, 32:64], t_o2[:, 0:n, :], flg[:, 2 + j:3 + j], omg[:, b0:16, 32:64], ALU.mult, ALU.add,
                  [t_o2, flg, omg], [omg])
        for j in range(4):
            S.act(t_o[:, 0, :], Ppre[:, 4 * j, 0:32], AF.Exp, [Ppre], [t_o])
            S.stt("dve", wini[:, 0:32], t_o[:, 0, :], flg[:, 2 + j:3 + j], wini[:, 0:32], ALU.mult, ALU.add, [t_o, flg, wini], [wini])
            S.tt("dve", t_o[:, 1, :], Ppre[:, 16, 32:64], Ppre[:, 4 * (j + 1), 32:64], ALU.subtract, [Ppre], [t_o])
            S.act(t_o[:, 2, :], t_o[:, 1, :], AF.Exp, [t_o], [t_o])
            S.stt("dve", wini[:, 32:64], t_o[:, 2, :], flg[:, 2 + j:3 + j], wini[:, 32:64], ALU.mult, ALU.add, [t_o, flg, wini], [wini])
        S.tt("dve", dtdecX[:], dtdecX[:], omg[:], ALU.mult, [dtdecX, omg], [dtdecX])

    if stop == 'dt':
        dtA(0); dtB(0); dtA(1); dtB(1); dtA(2); dtB(2); dtOmega()
        S.barrier(); S.emit(); return nc

    tc_sb = S.alloc("tc_sb", [16, 512], BF16); ts_sb = S.alloc("ts_sb", [16, 512], BF16)
    S.dma("sp", tc_sb[:], tcs[0].rearrange("(lt p) n -> p lt n", p=128), tc_sb, writes=[tc_sb])
    S.dma("sp", ts_sb[:], tcs[1].rearrange("(lt p) n -> p lt n", p=128), ts_sb, writes=[ts_sb])
    wuf = [wuf0, S.alloc("wuf1", [8, 256], BF16)]
    Ug = [S.alloc("Ug%d" % i, [20, 256], BF16) for i in range(1)]
    Vt = S.alloc("Vt", [2, 2, 1024], BF16)
    ev = [0]

    def evac(out, in_, reads, writes):
        ev[0] += 1
        S.cp("act" if ev[0] % 2 == 0 else "dve", out, in_, reads, writes)

    S.dma("pool", wgc0[:, :, 0:256], wview(w_in, W_XS, 256), wgc0, writes=[wgc0])
    S.dma("pool", wgc0[:, :, 256:384], wview(w_in, W_B, 128), wgc0, writes=[wgc0])
    dtA(0)
    for g in range(4):
        w = wuf[g % 2]; U = Ug[0]
        if g > 0:
            S.dma("pool", w[:], wview(w_in, W_UF + g * 256, 256), w, writes=[w])
        for tp in range(10):
            pb = S.bank()
            for j in range(2):
                ti = tp * 2 + j
                ht, c0 = (hTX, ti * 128) if ti < 16 else (hTPM, (ti - 16) * 128)
                for kc in range(8):
                    S.mm(pb[:, j * 256:(j + 1) * 256], ht[:, kc, c0:c0 + 128], w[:, kc, :], kc == 0, kc == 7,
                         [ht, w], [pb])
            evac(U[:, tp * 2:tp * 2 + 2, :], pb[:, :].rearrange("p (a b) -> p a b", a=2), [pb], [U])
        if g == 0:
            dtB(0); dtA(1)
        if g == 1:
            dtB(2)
        for cs, tab in ((0, tc_sb), (1, ts_sb)):
            for wc in range(2):
                pb = S.bank()
                for lt in range(16):
                    S.mm(pb[:, :], U[:, lt, wc * 128:(wc + 1) * 128], tab[:, lt, :], lt == 0, lt == 15, [U, tab], [pb])
                evac(Vt[:, cs, wc, 512:1024], pb[:, :], [pb], [Vt])
        if g == 0:
            dtB(1); dtA(2)
        if g == 1:
            dtOmega()
        for cs in range(2):
            for wc in range(2):
                pb = S.bank()
                for sq in range(2):
                    for lt in range(2):
                        S.mm(pb[:, sq * 256:(sq + 1) * 256], U[:, 16 + sq * 2 + lt, wc * 128:(wc + 1) * 128],
                             t2[:, 0 if cs == 0 else 2, lt, :], lt == 0, lt == 1, [U, t2], [pb])
                evac(Vt[:, cs, wc, 0:512], pb[:, :], [pb], [Vt])
        for tb in range(2):
            for wpc in range(2):
                pb = S.bank()
                k = 0
                for cs in range(2):
                    for wc in range(2):
                        S.mm(pb[:, :], t2[:, 0 if cs == 0 else 1, wc, wpc * 128:(wpc + 1) * 128],
                             Vt[:, cs, wc, tb * 512:(tb + 1) * 512], k == 0, k == 3, [t2, Vt], [pb])
                        k += 1
                evac(fmT[:, g * 2 + wpc, tb * 512:(tb + 1) * 512], pb[:, :], [pb], [fmT])
    dump("fmT", fmT, [8, 1024], BF16)
    dump("dtO", dtO, [8, 64]); dump("scO", scO, [8, 64])
    dump("dtdec", dtdec, [8, 64]); dump("cdA", cdA, [8, 64])
    S.release(mark_ctx)
    if stop == 'fourier':
        S.barrier(); S.emit(); return nc
    accs = [S.alloc("acc%d" % i, [512], F32) for i in range(4)]
    cvk = [0]

    def conv_unit(wt, wcol0, cidx, ht, hc0, n_in, o_rel0, n_out, left_from, right_to, dest_ap, dest_tile):
        st_ = {}

        def s0():
            pb = S.bank()
            st_["pb"] = pb
            for kc in range(8):
                S.mm(pb[:, 0:n_in], wt[:, kc, wcol0:wcol0 + 128], ht[:, kc, hc0:hc0 + n_in], kc == 0, kc == 7, [wt, ht], [pb])

        def s1():
            pb = st_["pb"]
            cvk[0] += 1
            acc = accs[cvk[0] % len(accs)]
            st_["acc"] = acc
            S.act(acc[:, 0:n_out], pb[:, o_rel0:o_rel0 + n_out], AF.Identity, [pb, cw, cb], [acc],
                  bias=cb[:, cidx:cidx + 1], scale=cw[:, cidx * 3 + 1:cidx * 3 + 2])
        def s1b():
            pb = st_["pb"]; acc = st_["acc"]
            a, b = max(o_rel0, left_from), o_rel0 + n_out
            S.stt("dve", acc[:, a - o_rel0:b - o_rel0], pb[:, a - 1:b - 1], cw[:, cidx * 3:cidx * 3 + 1],
                  acc[:, a - o_rel0:b - o_rel0], ALU.mult, ALU.add, [pb, cw, acc], [acc])
            a, b = o_rel0, min(o_rel0 + n_out, right_to)
            S.stt("dve", acc[:, a - o_rel0:b - o_rel0], pb[:, a + 1:b + 1], cw[:, cidx * 3 + 2:cidx * 3 + 3],
                  acc[:, a - o_rel0:b - o_rel0], ALU.mult, ALU.add, [pb, cw, acc], [acc])

        def s2():
            acc = st_["acc"]
            S.act(dest_ap, acc[:, 0:n_out], AF.Silu, [acc], [dest_tile])

        return [s0, s1, s1b, s2]

    def h4(ap):
        return ap.rearrange("p (h q) -> p h q", h=4)

    def bc4(ap):
        return ap.unsqueeze(2).to_broadcast([128, 4, 64])

    wgc = [wgc0, S.alloc("wgc1", [8, 384], BF16)]
    xsTX = S.alloc("xsTX", [2, 2048], BF16); BTX = S.alloc("BTX", [2048], BF16)
    xbs = [S.alloc("xbs%d" % i, [384], BF16) for i in range(3)]
    xdd = [S.alloc("xdd%d" % i, [256], BF16) for i in range(6)]
    Sst = [S.alloc("Sst%d" % i, [256], F32) for i in range(2)]
    Stmp = [S.alloc("Stmp%d" % i, [256], F32) for i in range(2)]
    segs = [(0, 410), (410, 820), (820, 1230), (1230, 1640), (1640, 2048)]
    def load_wgc(g):
        wt = wgc[g % 2]
        S.dma("pool", wt[:, :, 0:256], wview(w_in, W_XS + g * 256, 256), wt, writes=[wt])
        S.dma("pool", wt[:, :, 256:384], wview(w_in, W_B + g * 128, 128), wt, writes=[wt])

    for g in range(8):
        wt = wgc[g % 2]
        units = []
        for (o0, o1) in segs:
            i0, i1 = max(o0 - 1, 0), min(o1 + 1, 2048)
            for cc in range(3):
                cidx = (2 * g + cc) if cc < 2 else 16 + g
                dest = xsTX[:, cc, o0:o1] if cc < 2 else BTX[:, o0:o1]
                units.append(conv_unit(wt, cc * 128, cidx, hTX, i0, i1 - i0, o0 - i0, o1 - o0, 1 if i0 == 0 else 0,
                                       2047 - i0, dest, xsTX if cc < 2 else BTX))
        run_pipeline(units)
        if g + 1 < 8:
            load_wgc(g + 1)
        if g == 6:
            S.dma("pool", wgxs0[:], wview(w_in, W_XS, 256), wgxs0, writes=[wgxs0])
        psF = S.reserve_bank(); psB = S.reserve_bank()
        def ctx_chunk(c):
            st_ = {}

            def s0():
                pb = S.bank(); pbb = pb[:, :].bitcast(BF16)
                st_["pb"] = pb
                for cc in range(2):
                    S.tr(pbb[:, cc * 128:(cc + 1) * 128], xsTX[:, cc, c * 128:(c + 1) * 128], identb[:], [xsTX, identb], [pb])
                S.tr(pbb[:, 256:384], BTX[:, c * 128:(c + 1) * 128], identb[:], [BTX, identb], [pb])

            def s1():
                pb = st_["pb"]; pbb = pb[:, :].bitcast(BF16)
                xb = xbs[c % len(xbs)]
                S.cp("act", xb[:], pbb[:, 0:384], [pb], [xb])
                for d in range(2):
                    if (d == 0 and c >= 12) or (d == 1 and c < 4):
                        continue
                    col0 = d * 32 + g * 4
                    xd = xdd[(2 * c + d) % len(xdd)]
                    S.tt("dve", h4(xd[:]), h4(xb[:, 0:256]), bc4(dtdecX[:, c, col0:col0 + 4]),
                         ALU.mult, [xb, dtdecX], [xd])

            def s2():
                xb = xbs[c % len(xbs)]
                for d in range(2):
                    if (d == 0 and c >= 12) or (d == 1 and c < 4):
                        continue
                    xd = xdd[(2 * c + d) % len(xdd)]
                    pacc = psF if d == 0 else psB
                    first = (c == 0) if d == 0 else (c == 4)
                    last = (c == 11) if d == 0 else (c == 15)
                    S.mm(pacc[:, 0:256], xb[:, 256:384], xd[:], first, last, [xb, xd], [pacc])

            return [s0, s1, s2]

        run_pipeline([ctx_chunk(c) for c in range(16)])
        for d in range(2):
            Sm = SmF if d == 0 else SmB
            pacc = psF if d == 0 else psB
            S.dma("sp", Sst[d][:], stT[d][:, g * 256:(g + 1) * 256], Sst[d], writes=[Sst[d]])
            S.tt("dve", h4(Stmp[d][:]), h4(Sst[d][:]), bc4(wini[:, d * 32 + g * 4:d * 32 + g * 4 + 4]), ALU.mult,
                 [Sst[d], wini], [Stmp[d]])
            S.tt("dve", Sm[:, g, :], Stmp[d][:], pacc[:, 0:256], ALU.add, [Stmp[d], pacc], [Sm])
        S.unreserve(psF); S.unreserve(psB)
    dump("SmF", SmF, [8, 256], BF16); dump("SmB", SmB, [8, 256], BF16)
    S.release(mark_pre_hTX)
    if stop == 'ctx':
        S.barrier(); S.emit(); return nc
    ynT = S.alloc("ynT", [16, 1024], BF16)
    mark_m = S.top
    accs = [S.alloc("accm%d" % i, [264], F32) for i in range(4)]
    wg2 = [S.alloc("wg2_%d" % i, [8, 768], BF16) for i in range(2)]
    xsT = S.alloc("xsT", [2, 1024], BF16); BT = S.alloc("BT", [1024], BF16); CT = S.alloc("CT", [1024], BF16)
    zs = S.alloc("zs", [8, 256], BF16)
    xb8 = S.alloc("xb8", [8, 384], BF16)
    xdt = S.alloc("xdt", [8, 2, 256], BF16)
    SinB = S.alloc("SinB", [8, 2, 256], BF16)
    xbc_ = [Tile("xb8_%d" % c, xb8[:, c, :]) for c in range(8)]
    xdc_ = [Tile("xdt_%d" % c, xdt[:, c, :, :]) for c in range(8)]
    zsc_ = [Tile("zs_%d" % c, zs[:, c, :]) for c in range(8)]
    sic_ = [[Tile("sin_%d_%d" % (c, d), SinB[:, c, d, :]) for d in range(2)] for c in range(8)]
    xdd2 = [S.alloc("xdd2_%d" % i, [256], BF16) for i in range(6)]
    Sa = [S.alloc("Sa%d" % i, [256], F32) for i in range(6)]
    Stm = [S.alloc("Stm%d" % i, [256], F32) for i in range(4)]
    stg = [S.alloc("stg%d" % i, [512], F32) for i in range(1)]
    DAt = [S.alloc("DAt%d" % i, [8, 128], BF16) for i in range(1)]
    Ee = [S.alloc("E%d" % i, [8, 128], BF16) for i in range(2)]
    Xb = [S.alloc("Xb%d" % i, [8, 128], F32) for i in range(2)]
    Gs = [S.alloc("Gs%d" % i, [128], BF16) for i in range(4)]
    Ww = [S.alloc("W%d" % i, [8, 128], BF16) for i in range(2)]
    yA = [S.alloc("yA%d" % i, [256], F32) for i in range(4)]
    yH = [[Tile("yA%d_%d" % (i, h), yA[i][:, h * 64:(h + 1) * 64]) for h in range(4)] for i in range(4)]
    Dm2 = [S.alloc("Dm%d" % i, [4, 128], BF16) for i in range(2)]
    ynk = [S.alloc("ynk%d" % i, [256], BF16) for i in range(2)]
    jk = S.alloc("jk", [256], F32)
    st2 = [S.alloc("st2_%d" % i, [4], F32) for i in range(2)]
    rr = {"sa": 0, "xd": 0, "tm": 0, "stg": 0, "ew": 0}

    def nxt(key, lst):
        rr[key] += 1
        return lst[rr[key] % len(lst)]

    def ew():
        return "dve"

    def load_wg2(g):
        wt = wg2[g % 2]
        if g > 0:
            S.dma("pool", wt[:, :, 0:256], wview(w_in, W_XS + g * 256, 256), wt, writes=[wt])
        S.dma("pool", wt[:, :, 256:384], wview(w_in, W_B + g * 128, 128), wt, writes=[wt])
        S.dma("pool", wt[:, :, 384:512], wview(w_in, W_C + g * 128, 128), wt, writes=[wt])
        S.dma("pool", wt[:, :, 512:768], wview(w_in, W_Z + g * 256, 256), wt, writes=[wt])

    def front_units(g):
        wt = wg2[g % 2]
        units = []
        for cc in range(4):
            cidx = (2 * g + cc) if cc < 2 else (16 + g if cc == 2 else 24 + g)
            dtile = xsT if cc < 2 else (BT if cc == 2 else CT)
            wt_c = wgxs0 if (g == 0 and cc < 2) else wt
            for u in range(4):
                dest = dtile[:, cc, u * 256:(u + 1) * 256] if cc < 2 else dtile[:, u * 256:(u + 1) * 256]
                if u < 2:
                    units.append(conv_unit(wt_c, cc * 128, cidx, hTPM, u * 256, 256, 0, 256, 1, 255, dest, dtile))
                else:
                    units.append(conv_unit(wt_c, cc * 128, cidx, hTPM, 512 + (u - 2) * 256, 258, 1, 256, 0, 10 ** 6,
                                           dest, dtile))

        def z_unit(cp_):
            st_ = {}

            def s0():
                pb = S.bank()
                st_["pb"] = pb
                for j in range(2):
                    ht, c0 = chunk_cols(cp_ * 2 + j)
                    for kc in range(8):
                        S.mm(pb[:, j * 256:(j + 1) * 256], ht[:, kc, c0:c0 + 128], wt[:, kc, 512:768], kc == 0, kc == 7,
                             [ht, wt], [pb])

            def s1():
                pb = st_["pb"]
                S.act(zs[:, cp_ * 2:cp_ * 2 + 2, :], pb[:, :].rearrange("p (a b) -> p a b", a=2), AF.Silu, [pb],
                      [zsc_[cp_ * 2], zsc_[cp_ * 2 + 1]])

            return [s0, s1]

        for cp_ in range(4):
            units.append(z_unit(cp_))
        return units

    load_wg2(0)
    run_pipeline(front_units(0))
    for g in range(8):
        if g + 1 < 8:
            load_wg2(g + 1)
        Dm = Dm2[g % 2]
        for h in range(4):
            S.ts("pool", Dm[:, h, :], identb[:], rows[:, 128 + g * 4 + h:129 + g * 4 + h], ALU.mult, [identb, rows], [Dm])

        def tr_chunk(c):
            st_ = {}

            def s0():
                pb = S.bank(); pbb = pb[:, :].bitcast(BF16)
                st_["pb"] = pb
                for cc in range(2):
                    S.tr(pbb[:, cc * 128:(cc + 1) * 128], xsT[:, cc, c * 128:(c + 1) * 128], identb[:], [xsT, identb], [pb])
                S.tr(pbb[:, 256:384], BT[:, c * 128:(c + 1) * 128], identb[:], [BT, identb], [pb])

            def s1():
                pb = st_["pb"]; pbb = pb[:, :].bitcast(BF16)
                S.cp("act", xbc_[c][:], pbb[:, 0:384], [pb], [xbc_[c]])
                for d in range(2):
                    S.tt("pool", h4(xdc_[c][:, d, :]), h4(xbc_[c][:, 0:256]), bc4(dtO[:, c, d * 32 + g * 4:d * 32 + g * 4 + 4]),
                         ALU.mult, [xbc_[c], dtO], [xdc_[c]])

            return [s0, s1]

        run_pipeline([tr_chunk(c) for c in range(8)])

        def mk_xd(c, d):
            xd = nxt("xd", xdd2)
            col0 = d * 32 + g * 4
            S.tt(ew(), h4(xd[:]), h4(xbc_[c][:, 0:256]), bc4(dtdec[:, c, col0:col0 + 4]), ALU.mult, [xbc_[c], dtdec], [xd])
            return xd

        def sc_mm(c, d):
            xd = mk_xd(c, d)
            pc = S.bank()
            S.mm(pc[:, 0:256], xbc_[c][:, 256:384], xd[:], True, True, [xbc_[c], xd], [pc])
            return pc

        def upd(Sold_ap, Sold_tile, c, d, pc):
            col0 = d * 32 + g * 4
            tm = nxt("tm", Stm); Snew = nxt("sa", Sa)
            S.tt(ew(), h4(tm[:]), h4(Sold_ap), bc4(cdA[:, c, col0:col0 + 4]), ALU.mult, [Sold_tile, cdA], [tm])
            S.tt("dve", Snew[:], tm[:], pc[:, 0:256], ALU.add, [tm, pc], [Snew])
            return Snew

        for sq in range(2):
            c0, c1 = 2 * sq, 2 * sq + 1
            pb = S.bank()
            for d, (ca, cb_) in enumerate(((c0, c1), (c1, c0))):
                col0 = d * 32 + g * 4
                xa = mk_xd(ca, d); xb_ = mk_xd(cb_, d); xa2 = nxt("xd", xdd2)
                S.tt(ew(), h4(xa2[:]), h4(xa[:]), bc4(cdA[:, cb_, col0:col0 + 4]), ALU.mult, [xa, cdA], [xa2])
                pc = S.bank()
                S.mm(pc[:, 0:256], xbc_[ca][:, 256:384], xa[:], True, True, [xbc_[ca], xa], [pc])
                S.cp("act", sic_[cb_][d][:], pc[:, 0:256], [pc], [sic_[cb_][d]])
                for hh in range(2):
                    osl = slice((d * 2 + hh) * 128, (d * 2 + hh + 1) * 128)
                    S.mm(pb[:, osl], xa2[:, hh * 128:(hh + 1) * 128], xbc_[ca][:, 256:384], True, False, [xa2, xbc_[ca]], [pb])
                    S.mm(pb[:, osl], xb_[:, hh * 128:(hh + 1) * 128], xbc_[cb_][:, 256:384], False, True, [xb_, xbc_[cb_]], [pb])
            sg = nxt("stg", stg)
            S.cp("act", sg[:], pb[:, :], [pb], [sg])
            for d, dst in enumerate((sF, sB)):
                for hh in range(2):
                    h0 = g * 4 + hh * 2
                    S.dma("sp", dst[sq, h0:h0 + 2].rearrange("h p n -> (h p) n"),
                          sg[:, (d * 2 + hh) * 128:(d * 2 + hh + 1) * 128], sg, reads=[sg])
        curs = [(SmF[:, g, :], SmF), (SmB[:, g, :], SmB)]
        orders = [[4, 5, 6, 7], [7, 6, 5, 4]]
        pcs = {}
        for k in range(3):
            pbk = S.bank()
            for d in range(2):
                c = orders[d][k]
                xd = mk_xd(c, d)
                S.mm(pbk[:, d * 256:(d + 1) * 256], xbc_[c][:, 256:384], xd[:], True, True, [xbc_[c], xd], [pbk])
                pcs[(k, d)] = pbk
        for k in range(4):
            for d in range(2):
                c = orders[d][k]
                cur_ap, cur_t = curs[d]
                S.cp("act", sic_[c][d][:], cur_ap, [cur_t], [sic_[c][d]])
                if k < 3:
                    col0 = d * 32 + g * 4
                    tm = nxt("tm", Stm); Snew = nxt("sa", Sa)
                    S.tt("dve", h4(tm[:]), h4(cur_ap), bc4(cdA[:, c, col0:col0 + 4]), ALU.mult, [cur_t, cdA], [tm])
                    S.tt("dve", Snew[:], tm[:], pcs[(k, d)][:, d * 256:(d + 1) * 256], ALU.add, [tm, pcs[(k, d)]], [Snew])
                    curs[d] = (Snew[:], Snew)

        def out_chunk(c):
            tc0 = c * 128
            has_f = (c % 2 == 1) if c < 4 else True
            has_b = (c % 2 == 0) if c < 4 else True
            i2 = c % 2
            st_ = {}

            bkA = [S.psum[0 + i2], S.psum[2 + i2]]; bkC = S.psum[4 + i2]; bkD = S.psum[6 + i2]

            def sA():
                pg = bkC
                st_["pg"] = pg
                S.mm(pg[:, 0:128], BT[:, tc0:tc0 + 128], CT[:, tc0:tc0 + 128], True, True, [BT, CT], [pg])
                S.cp("act", Gs[c % 4][:], pg[:, 0:128], [pg], [Gs[c % 4]])
                da_ = DAt[0]
                S.cp("act", da_[:], dabf[:, c, g * 8:(g + 1) * 8].unsqueeze(2).to_broadcast([128, 8, 128]), [dabf], [da_])
                st_["ps"] = []
                for d in range(2):
                    U2 = LEb if d == 0 else NLTb
                    NG = NEGf if d == 0 else NEGb
                    ps = bkA[d]
                    st_["ps"].append(ps)
                    for h in range(4):
                        S.mm(ps[:, h * 128:(h + 1) * 128], da_[:, d * 4 + h, :], U2[:], True, False, [da_, U2], [ps])
                        S.mm(ps[:, h * 128:(h + 1) * 128], identb[:], NG[:], False, True, [identb, NG], [ps])

            def sA2():
                x_ = Xb[i2]
                for d in range(2):
                    ps = st_["ps"][d]
                    col = g * 8 + d * 4
                    S.tt("dve", x_[:, d * 4:(d + 1) * 4, :], ps[:, :].rearrange("p (h q) -> p h q", h=4),
                         ebias[:, c, col:col + 4].unsqueeze(2).to_broadcast([128, 4, 128]), ALU.add, [ps, ebias], [x_])

            def sB():
                S.act(Ee[i2][:], Xb[i2][:], AF.Exp, [Xb[i2]], [Ee[i2]])

            def sB2():
                S.tt("dve", Ww[i2][:, 0:4, :], Ee[i2][:, 0:4, :], Gs[c % 4][:].unsqueeze(1).to_broadcast([128, 4, 128]), ALU.mult,
                     [Ee[i2], Gs[c % 4]], [Ww[i2]])
                S.tt("pool", Ww[i2][:, 4:8, :], Ee[i2][:, 4:8, :], Gs[c % 4][:].unsqueeze(1).to_broadcast([128, 4, 128]), ALU.mult,
                     [Ee[i2], Gs[c % 4]], [Ww[i2]])

            def sC():
                w_ = Ww[i2]
                py = bkC
                pyo = 128
                for h in range(4):
                    S.mm(py[:, pyo + h * 64:pyo + (h + 1) * 64], w_[:, h, :], xdc_[c][:, 0, h * 64:(h + 1) * 64], True, False,
                         [w_, xdc_[c]], [py])
                    S.mm(py[:, pyo + h * 64:pyo + (h + 1) * 64], w_[:, 4 + h, :], xdc_[c][:, 1, h * 64:(h + 1) * 64], False, False,
                         [w_, xdc_[c]], [py])
                    S.mm(py[:, pyo + h * 64:pyo + (h + 1) * 64], Dm[:, h, :], xbc_[c][:, h * 64:(h + 1) * 64], False, True,
                         [Dm, xbc_[c]], [py])
                po = bkD
                if has_f:
                    S.mm(po[:, 0:256], CT[:, tc0:tc0 + 128], sic_[c][0][:], True, True, [CT, sic_[c][0]], [po])
                if has_b:
                    S.mm(po[:, 256:512], CT[:, tc0:tc0 + 128], sic_[c][1][:], True, True, [CT, sic_[c][1]], [po])
                y = yA[c % 4]; yh = yH[c % 4]
                S.cp("act", y[:], py[:, pyo:pyo + 256], [py], [y] + yh)

            def sC2():
                po = bkD
                y = yA[c % 4]; yh = yH[c % 4]
                for (has, off, sc0) in ((has_f, 0, g * 4), (has_b, 256, 32 + g * 4)):
                    if not has:
                        continue
                    for h in range(4):
                        hs = slice(h * 64, (h + 1) * 64)
                        S.stt("dve", y[:, hs], po[:, off + h * 64:off + (h + 1) * 64], scO[:, c, sc0 + h:sc0 + h + 1],
                              y[:, hs], ALU.mult, ALU.add, [po, scO, yh[h]], [yh[h]])

            def sD():
                y = yA[c % 4]
                S.tt("pool", y[:], y[:], zsc_[c][:], ALU.mult, yH[c % 4] + [zsc_[c]], [y] + yH[c % 4])

            def sD2():
                y = yA[c % 4]
                st = st2[i2]
                S.act(jk[:], y[:], AF.Square, [y] + yH[c % 4], [jk, st], accum=st[:, 0:1])
                S.act(st[:, 1:2], st[:, 0:1], AF.Ln, [st, epsb], [st], bias=epsb[:, 0:1], scale=1.0 / 256)
                S.act(st[:, 2:3], st[:, 1:2], AF.Exp, [st], [st], scale=-0.5)
                S.act(ynk[i2][:], y[:], AF.Copy, [y, st] + yH[c % 4], [ynk[i2]], scale=st[:, 2:3])

            def sE():
                pt = bkC; ptb = pt[:, :].bitcast(BF16)
                for j in range(2):
                    S.tr(ptb[:, 768 + j * 128:768 + (j + 1) * 128], ynk[i2][:, j * 128:(j + 1) * 128], identb[:],
                         [ynk[i2], identb], [pt])
                S.cp("act", ynT[:, g * 2:g * 2 + 2, tc0:tc0 + 128], ptb[:, 768:1024].rearrange("p (a b) -> p a b", a=2),
                     [pt], [ynT])

            return [sA, sA2, sB, sB2, sC, sC2, sD, sD2, sE]

        merged = [out_chunk(c) for c in range(8)]
        if g + 1 < 8:
            merged += [[] for _ in range(5)]
            merged += front_units(g + 1)
        S.reserved.extend([S.psum[4], S.psum[5]])
        run_pipeline(merged)
        S.reserved.remove(S.psum[4]); S.reserved.remove(S.psum[5])
    dump("ynT", ynT, [16, 1024], BF16)
    S.release(mark_m)
    if stop == 'ssd':
        S.barrier(); S.emit(); return nc
    f1T = S.alloc("f1T", [8, 1024], BF16); mT = S.alloc("mT", [8, 1024], BF16)
    wC0 = S.alloc("wC_p", [8, 256], BF16); wD0 = S.alloc("wD_p", [8, 256], BF16)
    mark_t = S.top
    wA = [S.alloc("wA%d" % i, [8, 512], BF16) for i in range(2)]
    wB = [S.alloc("wB%d" % i, [8, 512], BF16) for i in range(2)]
    sgt = [S.alloc("sgt%d" % i, [512], F32) for i in range(2)]
    for cbk in range(2):
        wf = wA[cbk % 2]; wg_ = wB[cbk % 2]
        S.dma("pool", wf[:], wview(fnet_w, cbk * 512, 512), wf, writes=[wf])
        S.dma("pool", wg_[:], wview(w_in, W_GF + cbk * 512, 512), wg_, writes=[wg_])
        if cbk == 1:
            S.dma("pool", wC0[:], wview(w_in, W_MGF, 256), wC0, writes=[wC0])
            S.dma("pool", wD0[:], wview(w_in, W_MGS, 256), wD0, writes=[wD0])
        for tb in range(2):
            hc0 = 0 if tb == 0 else 513
            for c4 in range(4):
                cp_ = cbk * 4 + c4
                p1 = S.bank(); p2 = S.bank()
                for kc in range(8):
                    S.mm(p1[:, :], wf[:, kc, c4 * 128:(c4 + 1) * 128], fmT[:, kc, tb * 512:(tb + 1) * 512], kc == 0, kc == 7,
                         [wf, fmT], [p1])
                for kc in range(8):
                    S.mm(p2[:, :], wg_[:, kc, c4 * 128:(c4 + 1) * 128], hTPM[:, kc, hc0:hc0 + 512], kc == 0, kc == 7,
                         [wg_, hTPM], [p2])
                sg = sgt[c4 % 2]
                S.act(sg[:], p2[:, :], AF.Silu, [p2], [sg])
                S.tt("dve", f1T[:, cp_, tb * 512:(tb + 1) * 512], p1[:, :], sg[:], ALU.mult, [p1, sg], [f1T])
    S.release(mark_t)
    wo = [S.alloc("wo%d" % i, [8, 512], BF16) for i in range(2)]
    mark_t2 = S.top
    wA2 = [S.alloc("wA2_%d" % i, [8, 256], BF16) for i in range(2)]; wS2 = [S.alloc("wS_%d" % i, [16, 256], BF16) for i in range(2)]
    wC2 = [wC0, S.alloc("wC_1", [8, 256], BF16)]; wD2 = [wD0, S.alloc("wD_1", [8, 256], BF16)]
    nwT = S.alloc("nwT", [16], F32)
    S.dma("sp", nwT[:], nwTd, nwT, writes=[nwT])
    sga = [S.alloc("sga%d" % i, [512], F32) for i in range(2)]
    sgb = [S.alloc("sgb%d" % i, [512], F32) for i in range(2)]

    def load_t2(cbk):
        i = cbk % 2
        S.dma("pool", wA2[i][:], wview(wbf, cbk * 256, 256), wA2[i], writes=[wA2[i]])
        S.dma("pool", wS2[i][:], wview(wbs, cbk * 256, 256), wS2[i], writes=[wS2[i]])
        S.dma("pool", wC2[i][:], wview(w_in, W_MGF + cbk * 256, 256), wC2[i], writes=[wC2[i]])
        S.dma("pool", wD2[i][:], wview(w_in, W_MGS + cbk * 256, 256), wD2[i], writes=[wD2[i]])
        S.tt("dve", wS2[i][:], wS2[i][:], nwT[:].unsqueeze(2).to_broadcast([128, 16, 256]), ALU.mult, [wS2[i], nwT], [wS2[i]])

    load_t2(0)
    for cbk in range(4):
        if cbk + 1 < 4:
            load_t2(cbk + 1)
        if cbk == 1:
            for cbk2 in range(2):
                S.dma("pool", wo[cbk2][:], wview(w_out, cbk2 * 512, 512), wo[cbk2], writes=[wo[cbk2]])
        wA = wA2[cbk % 2]; wS = wS2[cbk % 2]; wC = wC2[cbk % 2]; wD = wD2[cbk % 2]
        for tb in range(2):
            hc0 = 0 if tb == 0 else 513
            tsl = slice(tb * 512, (tb + 1) * 512)
            for c4 in range(2):
                cp_ = cbk * 2 + c4
                csl = slice(c4 * 128, (c4 + 1) * 128)
                pA = S.bank(); pB = S.bank(); pC = S.bank(); pD = S.bank()
                for kc in range(8):
                    S.mm(pC[:, :], wC[:, kc, csl], hTPM[:, kc, hc0:hc0 + 512], kc == 0, kc == 7, [wC, hTPM], [pC])
                for kc in range(8):
                    S.mm(pD[:, :], wD[:, kc, csl], hTPM[:, kc, hc0:hc0 + 512], kc == 0, kc == 7, [wD, hTPM], [pD])
                for kc in range(8):
                    S.mm(pA[:, :], wA[:, kc, csl], f1T[:, kc, tsl], kc == 0, kc == 7, [wA, f1T], [pA])
                for kc in range(16):
                    S.mm(pB[:, :], wS[:, kc, csl], ynT[:, kc, tsl], kc == 0, kc == 15, [wS, ynT], [pB])
                a_ = sga[c4 % 2]; b_ = sgb[c4 % 2]
                S.act(a_[:], pC[:, :], AF.Sigmoid, [pC], [a_])
                S.act(b_[:], pD[:, :], AF.Sigmoid, [pD], [b_])
                S.tt("dve", a_[:], pA[:, :], a_[:], ALU.mult, [pA, a_], [a_])
                S.tt("dve", b_[:], pB[:, :], b_[:], ALU.mult, [pB, b_], [b_])
                S.tt("dve", mT[:, cp_, tsl], a_[:], b_[:], ALU.add, [a_, b_], [mT])
    S.release(mark_t2)
    xt3 = [S.alloc("xt3_%d" % i, [D], F32) for i in range(3)]
    tmp3 = [S.alloc("tmp3_%d" % i, [D], F32) for i in range(2)]
    ot3 = [S.alloc("ot3_%d" % i, [D], F32) for i in range(2)]
    st3 = [S.alloc("st3_%d" % i, [8], F32) for i in range(2)]
    jk3 = S.alloc("jk3", [512], F32)
    def t3_tile(ti):
        src = xP[ti * 128:(ti + 1) * 128, :] if ti < 4 else xM[(ti - 4) * 128:(ti - 3) * 128, :]
        dst = yP[ti * 128:(ti + 1) * 128, :] if ti < 4 else yM[(ti - 4) * 128:(ti - 3) * 128, :]
        cond = 0 if ti < 4 else 1
        xt = xt3[ti % 3]; ot = ot3[ti % 2]; st = st3[ti % 2]; tmp = tmp3[ti % 2]
        st_ = {}

        def s0():
            S.dma("sp", xt[:], src, xt, writes=[xt])
            pO = [S.bank(), S.bank()]
            st_["pO"] = pO
            for cbk in range(2):
                for kc in range(8):
                    S.mm(pO[cbk][:, :], mT[:, kc, ti * 128:(ti + 1) * 128], wo[cbk][:, kc, :], kc == 0, kc == 7,
                         [mT, wo[cbk]], [pO[cbk]])

        def s1():
            pO = st_["pO"]
            for cbk in range(2):
                S.act(jk3[:], pO[cbk][:, :], AF.Square, [pO[cbk]], [jk3, st], accum=st[:, cbk:cbk + 1])
            S.tt("dve", st[:, 2:3], st[:, 0:1], st[:, 1:2], ALU.add, [st], [st])
            S.act(st[:, 3:4], st[:, 2:3], AF.Ln, [st, epsb], [st], bias=epsb[:, 0:1], scale=1.0 / D)
            S.act(st[:, 4:5], st[:, 3:4], AF.Exp, [st], [st], scale=-0.5)

        def s2():
            pO = st_["pO"]
            for cbk in range(2):
                sl = slice(cbk * 512, (cbk + 1) * 512)
                S.stt("dve", tmp[:, sl], pO[cbk][:, :], st[:, 4:5], g2[:, cond, sl], ALU.mult, ALU.mult,
                      [pO[cbk], st, g2], [tmp])
                S.tt("dve", ot[:, sl], tmp[:, sl], xt[:, sl], ALU.add, [tmp, xt], [ot])

        def s3():
            S.dma("sp", dst, ot[:], ot, reads=[ot])

        return [s0, s1, s2, s3]

    run_pipeline([t3_tile(ti) for ti in range(8)])
    S.barrier()
    S.emit()
    return nc


def _dft_tables():
    l = np.arange(2048)
    r, c = l // 64, l % 64
    ph = (np.outer(r, r) / 32.0 + np.outer(c, c) / 64.0)
    ang = 2.0 * np.pi * (ph % 1.0)
    sc = 1.0 / math.sqrt(2048.0)
    cpos = (np.cos(ang) * sc).astype(np.float32)
    spos = (np.sin(ang) * sc).astype(np.float32)
    k = np.arange(256)
    a2 = 2.0 * np.pi * ((np.outer(k, k) % 256) / 256.0)
    c256 = (np.cos(a2) / 16.0).astype(np.float32)
    s256 = (np.sin(a2) / 16.0).astype(np.float32)
    return cpos, spos, c256, s256


def prep_inputs(x_prompt, x_sample, c, state_ssd_fwd, state_ssd_bwd, c_ctx, ada_w, ada_b, pre_norm_w,
                post_norm_w, w_in, conv_w, conv_b, dt_bias_fwd, dt_bias_bwd, a_log_fwd, a_log_bwd, d_skip,
                ssd_norm_w, fnet_w, w_branch_f, w_branch_s, w_out):
    f = lambda a: np.ascontiguousarray(np.asarray(a, dtype=np.float32))
    bf = lambda a: np.ascontiguousarray(a.astype(ml_dtypes.bfloat16))
    x_prompt, x_sample, c, c_ctx = f(x_prompt), f(x_sample), f(c), f(c_ctx)
    sf, sb = f(state_ssd_fwd), f(state_ssd_bwd)
    cpos, spos, c256, s256 = _dft_tables()
    t256 = bf(np.stack([c256, -s256, s256]))
    cw = f(conv_w)[0].reshape(3, 32, 128).transpose(2, 1, 0).reshape(128, 96)
    cb = f(conv_b)[0].reshape(32, 128).T
    rowsm = np.concatenate([f(dt_bias_fwd)[0], f(dt_bias_bwd)[0], f(a_log_fwd)[0], f(a_log_bwd)[0],
                            f(d_skip)[0]])[None, :]
    rowbig = np.concatenate([f(ssd_norm_w)[0], f(pre_norm_w)[0], f(post_norm_w)[0]])[None, :]
    shared = {
        "t256": t256, "cw": f(cw), "cb": f(cb), "rowsm": f(rowsm), "rowbig": f(rowbig), "nwT": f(f(ssd_norm_w)[0].reshape(16, 128).T),
        "ada_w": f(ada_w)[0], "ada_b": f(ada_b), "w_in": f(w_in)[0], "fnet_w": f(fnet_w)[0],
        "wbf": f(w_branch_f)[0], "wbs": f(w_branch_s)[0], "w_out": f(w_out)[0],
    }
    maps = []
    zero_row = np.zeros((1, D), np.float32)
    for core in range(8):
        b, q = core // 4, core % 4
        xs = x_sample[b]
        hl = xs[q * 512 - 1:q * 512] if q > 0 else zero_row
        hr = xs[(q + 1) * 512:(q + 1) * 512 + 1] if q < 3 else zero_row
        flags = np.zeros((1, 8), np.float32)
        flags[0, 0] = 1.0 if q > 0 else 0.0
        flags[0, 1] = 1.0 if q < 3 else 0.0
        flags[0, 2 + q] = 1.0
        cc = np.stack([c_ctx, c[b]])
        cT = cc.reshape(2, 8, 128).transpose(2, 1, 0).reshape(128, 16)
        st = np.stack([sf[b, 0], sb[b, 0]])
        stT = st.reshape(2, 2048, 128).transpose(0, 2, 1)
        m = dict(shared)
        m.update({
            "xX": f(xs), "xP": f(x_prompt[2 * core:2 * core + 2].reshape(512, D)),
            "xM": f(xs[q * 512:(q + 1) * 512]), "xH": f(np.concatenate([hl, hr], 0)),
            "fl": flags, "cT": f(cT), "stT": f(stT),
            "tcs": bf(np.stack([cpos[:, q * 512:(q + 1) * 512], spos[:, q * 512:(q + 1) * 512]])),
        })
        maps.append(m)
    return maps


_NC_CACHE = {}


def kernel(**inputs):
    maps = prep_inputs(**inputs)
    if "nc" not in _NC_CACHE:
        _NC_CACHE["nc"] = build(False)
    res = run_bass_kernel_spmd(_NC_CACHE["nc"], maps, core_ids=list(range(8)))
    r = res.results
    y_prompt = np.concatenate([r[i]["yP"].reshape(2, 256, D) for i in range(8)], 0).astype(np.float32)
    y_sample = np.stack([np.concatenate([r[4 * b + q]["yM"] for q in range(4)], 0) for b in range(2)]).astype(np.float32)
    nf = np.concatenate([r[i]["sF"] for i in range(8)], 0)[:, None].astype(np.float32)
    nb = np.concatenate([r[i]["sB"] for i in range(8)], 0)[:, None].astype(np.float32)
    return (y_prompt, y_sample, nf, nb)
```
